# Optimizing a Trainium2 kernel written in Bass

```python
import jax, jax.numpy as jnp
from jax import lax
import numpy as np

D_MODEL = 1024
BATCH = 2
SEQ = 8192
DEPTH = 1

HEAD_DIM = 64
ATTN_GROUPS = ((128, 1), (512, 4), (2048, 16))
ATTN_HEADS_PER_GROUP = 4
ATTN_HEADS = ATTN_HEADS_PER_GROUP * len(ATTN_GROUPS)
ATTN_WIDTH = ATTN_HEADS * HEAD_DIM
ATTN_OUT_WIDTH = ATTN_HEADS_PER_GROUP * HEAD_DIM
ATTN_BLOCK = 128
ROPE_THETA = 500000.0
ROPE_DIM = HEAD_DIM // 4
RWKV_HEADS = D_MODEL // HEAD_DIM
RWKV_WIDTH = RWKV_HEADS * HEAD_DIM
DECAY_LORA = 64
ICLR_LORA = 64
GATE_LORA = 128
D_FF = 4 * D_MODEL
NORM_EPS = 1e-6
GN_EPS = 64e-5
NEG_INF = -1e30
SPLITS = (ATTN_WIDTH, ATTN_WIDTH, ATTN_WIDTH,
          RWKV_WIDTH, RWKV_WIDTH, RWKV_WIDTH,
          DECAY_LORA, ICLR_LORA, GATE_LORA,
          D_MODEL, D_MODEL)
ATTN_QKV_WIDTH = 3 * ATTN_WIDTH
SHIFT_WIDTH = 3 * RWKV_WIDTH + DECAY_LORA + ICLR_LORA + GATE_LORA
IN_WIDTH = sum(SPLITS)

kernel_name = "hybrid_dilated_attn_rwkv7_block"


def rms_norm(t, gain):
    t32 = t.astype(jnp.float32)
    y = t32 * lax.rsqrt(jnp.mean(t32 * t32, axis=-1, keepdims=True) + NORM_EPS)
    return (y * gain.astype(jnp.float32)).astype(t.dtype)


def partial_rope(t, positions):
    half = ROPE_DIM // 2
    inv = ROPE_THETA ** (-jnp.arange(half, dtype=jnp.float32) * 2.0 / ROPE_DIM)
    ang = positions.astype(jnp.float32)[..., None] * inv
    cos = jnp.cos(ang)[:, :, None, :]
    sin = jnp.sin(ang)[:, :, None, :]
    t32 = t.astype(jnp.float32)
    x1 = t32[..., :half]
    x2 = t32[..., half:ROPE_DIM]
    out = jnp.concatenate([x1 * cos - x2 * sin, x1 * sin + x2 * cos, t32[..., ROPE_DIM:]], axis=-1)
    return out.astype(t.dtype)


def dilated_band_attention(q, k, v, dilation, span):
    B, S, H, Dh = q.shape
    unit = dilation * ATTN_BLOCK
    Sp = -(-S // unit) * unit
    L = Sp // dilation
    nb = L // ATTN_BLOCK

    def to_blocks(t):
        t = jnp.pad(t.astype(jnp.float32), ((0, 0), (0, Sp - S), (0, 0), (0, 0)))
        t = t.reshape(B, L, dilation, H, Dh).transpose(0, 2, 1, 3, 4)
        return t.reshape(B, dilation, nb, ATTN_BLOCK, H, Dh)

    qb, kb, vb = to_blocks(q), to_blocks(k), to_blocks(v)

    def with_prev(t):
        prev = jnp.pad(t[:, :, :-1], ((0, 0), (0, 0), (1, 0), (0, 0), (0, 0), (0, 0)))
        return jnp.concatenate([prev, t], axis=3)

    kw, vw = with_prev(kb), with_prev(vb)
    s = jnp.einsum('bdnqhe,bdnkhe->bdnhqk', qb, kw) * (HEAD_DIM ** -0.5)
    qi = jnp.arange(ATTN_BLOCK)[:, None]
    kj = jnp.arange(2 * ATTN_BLOCK)[None, :]
    lag = qi + ATTN_BLOCK - kj
    band = (lag >= 0) & (lag <= span)
    has_prev = (jnp.arange(nb) > 0)[:, None, None] | (kj >= ATTN_BLOCK)[None]
    mask = band[None] & has_prev
    s = jnp.where(mask[None, None, :, None], s, NEG_INF)
    m = jnp.max(s, axis=-1, keepdims=True)
    p = jnp.exp(s - m)
    l = jnp.sum(p, axis=-1, keepdims=True)
    o = jnp.einsum('bdnhqk,bdnkhe->bdnqhe', p / l, vw)
    lse = jnp.swapaxes((m + jnp.log(l))[..., 0], -1, -2)

    def from_blocks(t):
        t = t.reshape((B, dilation, L) + t.shape[4:])
        t = jnp.swapaxes(t, 1, 2)
        return t.reshape((B, Sp) + t.shape[3:])[:, :S]

    return from_blocks(o), from_blocks(lse)


def dilated_attention_mixer(q, k, v, positions):
    B, S = q.shape[:2]
    q = partial_rope(q, positions)
    k = partial_rope(k, positions)
    outs, lses = [], []
    for g, (window, dilation) in enumerate(ATTN_GROUPS):
        sl = slice(g * ATTN_HEADS_PER_GROUP, (g + 1) * ATTN_HEADS_PER_GROUP)
        o, lse = dilated_band_attention(q[:, :, sl], k[:, :, sl], v[:, :, sl], dilation, window // dilation)
        outs.append(o)
        lses.append(lse)
    o = jnp.stack(outs)
    w = jax.nn.softmax(jnp.stack(lses), axis=0)
    out = jnp.sum(w[..., None] * o, axis=0)
    return out.reshape(B, S, ATTN_OUT_WIDTH).astype(q.dtype)


def token_shift(u, mu):
    prev = jnp.pad(u, ((0, 0), (1, 0), (0, 0)))[:, :-1]
    return u + (prev - u) * mu


def wkv7_scan(r, w, k, v, a, b):
    B, S, H, N = r.shape
    xs = tuple(jnp.moveaxis(t, 1, 0) for t in (r, w, k, v, a, b))

    def step(state, inp):
        r_t, w_t, k_t, v_t, a_t, b_t = inp
        sa = jnp.einsum('bhvk,bhk->bhv', state, a_t)
        state = (state * w_t[:, :, None, :] + sa[..., None] * b_t[:, :, None, :]
                 + v_t[..., None] * k_t[:, :, None, :])
        return state, jnp.einsum('bhvk,bhk->bhv', state, r_t)

    s0 = jnp.zeros((B, H, N, N), jnp.float32)
    _, ys = lax.scan(step, s0, xs)
    return jnp.moveaxis(ys, 0, 1)


def rwkv7_mixer(r, k, v, dw, da, dg, w0, w2, a0, a2, g2, k_k, k_a, r_k, gn_w, gn_b):
    B, S, C = r.shape
    H, N = RWKV_HEADS, HEAD_DIM
    f32 = jnp.float32
    w_log = -jax.nn.softplus(-(w0 + jnp.tanh(dw) @ w2).astype(f32)) - 0.5
    decay = jnp.exp(-jnp.exp(w_log))
    a = jax.nn.sigmoid((a0 + da @ a2).astype(f32))
    g = (jax.nn.sigmoid(dg) @ g2).astype(f32)
    kk = (k * k_k).astype(f32).reshape(B, S, H, N)
    kk = kk / jnp.maximum(jnp.sqrt(jnp.sum(kk * kk, axis=-1, keepdims=True)), 1e-12)
    k = k.astype(f32) * (1.0 + (a - 1.0) * k_a.astype(f32))
    heads = lambda t: t.astype(f32).reshape(B, S, H, N)
    rh, kh, vh, wh, ah = heads(r), heads(k), heads(v), heads(decay), heads(a)
    y = wkv7_scan(rh, wh, kh, vh, -kk, kk * ah)
    mu = jnp.mean(y, axis=-1, keepdims=True)
    var = jnp.mean(jnp.square(y - mu), axis=-1, keepdims=True)
    y = ((y - mu) * lax.rsqrt(var + GN_EPS)).reshape(B, S, C) * gn_w + gn_b
    bonus = jnp.sum(rh * kh * r_k.astype(f32), axis=-1, keepdims=True) * vh
    y = y + bonus.reshape(B, S, C)
    return (y * g).astype(r.dtype)


def setup_inputs(seed: int = 0) -> dict:
    key = jax.random.key(seed)
    ks = jax.random.split(key, 32)
    f32 = jnp.float32
    L, D = DEPTH, D_MODEL
    nrm = lambda kk, shape, scale: jax.random.normal(kk, shape, f32) * scale
    x = nrm(ks[0], (BATCH, SEQ, D), 1.0)
    c = nrm(ks[1], (BATCH, D), 1.0)
    offset = jax.random.randint(ks[2], (BATCH, 1), 0, 4096, dtype=jnp.int32)
    positions = offset + jnp.arange(SEQ, dtype=jnp.int32)[None, :]
    return {
        "x": x,
        "c": c,
        "positions": positions,
        "ada_w": nrm(ks[3], (L, D, 6 * D), 0.5 * D ** -0.5),
        "ada_b": nrm(ks[4], (L, 6 * D), 0.02),
        "norm_mix_pre": 1.0 + nrm(ks[5], (L, D), 0.05),
        "norm_mix_post": 1.0 + nrm(ks[6], (L, D), 0.05),
        "norm_ffn_pre": 1.0 + nrm(ks[7], (L, D), 0.05),
        "norm_ffn_post": 1.0 + nrm(ks[8], (L, D), 0.05),
        "w_in": nrm(ks[9], (L, D, IN_WIDTH), D ** -0.5),
        "shift_mu": jax.random.uniform(ks[10], (L, SHIFT_WIDTH), f32),
        "decay_w0": jax.random.uniform(ks[11], (L, RWKV_WIDTH), f32, -6.0, 1.0),
        "decay_w2": nrm(ks[12], (L, DECAY_LORA, RWKV_WIDTH), DECAY_LORA ** -0.5),
        "iclr_a0": nrm(ks[13], (L, RWKV_WIDTH), 0.1),
        "iclr_a2": nrm(ks[14], (L, ICLR_LORA, RWKV_WIDTH), ICLR_LORA ** -0.5),
        "gate_g2": nrm(ks[15], (L, GATE_LORA, RWKV_WIDTH), GATE_LORA ** -0.5),
        "k_k": 0.85 + nrm(ks[16], (L, RWKV_WIDTH), 0.05),
        "k_a": 1.0 + nrm(ks[17], (L, RWKV_WIDTH), 0.05),
        "r_k": nrm(ks[18], (L, RWKV_HEADS, HEAD_DIM), 0.1),
        "gn_w": 1.0 + nrm(ks[19], (L, RWKV_WIDTH), 0.05),
        "gn_b": nrm(ks[20], (L, RWKV_WIDTH), 0.02),
        "w_branch": jnp.concatenate([nrm(ks[21], (L, ATTN_OUT_WIDTH, D), ATTN_OUT_WIDTH ** -0.5),
                                     nrm(ks[22], (L, RWKV_WIDTH, D), RWKV_WIDTH ** -0.5)], axis=1),
        "w_out": nrm(ks[23], (L, D, D), D ** -0.5),
        "w_ff1": nrm(ks[24], (L, D, D_FF), D ** -0.5),
        "w_ff2": nrm(ks[25], (L, D_FF, D), D_FF ** -0.5),
    }


def reference(x, c, positions, ada_w, ada_b, norm_mix_pre, norm_mix_post, norm_ffn_pre,
              norm_ffn_post, w_in, shift_mu, decay_w0, decay_w2, iclr_a0, iclr_a2, gate_g2,
              k_k, k_a, r_k, gn_w, gn_b, w_branch, w_out, w_ff1, w_ff2):
    B, S, D = x.shape
    for l in range(DEPTH):
        mod = jax.nn.silu(c) @ ada_w[l] + ada_b[l]
        sh1, sc1, gt1, sh2, sc2, gt2 = [m[:, None, :] for m in jnp.split(mod, 6, axis=-1)]

        h = rms_norm(x, norm_mix_pre[l]) * (1.0 + sc1) + sh1
        proj = h @ w_in[l]
        attn_part = proj[..., :ATTN_QKV_WIDTH]
        rwkv_part = token_shift(proj[..., ATTN_QKV_WIDTH:ATTN_QKV_WIDTH + SHIFT_WIDTH], shift_mu[l])
        gate_part = proj[..., ATTN_QKV_WIDTH + SHIFT_WIDTH:]
        q, k, v = [t.reshape(B, S, ATTN_HEADS, HEAD_DIM) for t in jnp.split(attn_part, 3, axis=-1)]
        cuts = np.cumsum([RWKV_WIDTH, RWKV_WIDTH, RWKV_WIDTH, DECAY_LORA, ICLR_LORA])
        r_r, r_k_, r_v, dw, da, dg = jnp.split(rwkv_part, cuts, axis=-1)
        g_attn, g_rwkv = jnp.split(gate_part, 2, axis=-1)

        o_attn = dilated_attention_mixer(q, k, v, positions)
        o_rwkv = rwkv7_mixer(r_r, r_k_, r_v, dw, da, dg, decay_w0[l], decay_w2[l], iclr_a0[l],
                             iclr_a2[l], gate_g2[l], k_k[l], k_a[l], r_k[l], gn_w[l], gn_b[l])
        wb = w_branch[l]
        merged = (jax.nn.sigmoid(g_attn) * (o_attn @ wb[:ATTN_OUT_WIDTH])
                  + jax.nn.sigmoid(g_rwkv) * (o_rwkv @ wb[ATTN_OUT_WIDTH:]))
        mix_out = merged @ w_out[l]
        x = (x + gt1 * rms_norm(mix_out, norm_mix_post[l])).astype(x.dtype)

        h = rms_norm(x, norm_ffn_pre[l]) * (1.0 + sc2) + sh2
        f = jnp.square(jax.nn.relu(h @ w_ff1[l])) @ w_ff2[l]
        x = (x + gt2 * rms_norm(f, norm_ffn_post[l])).astype(x.dtype)
    return x
```

```python
import os
import types
import numpy as np
from contextlib import ExitStack
SUB = int(os.environ.get('SUB', '99'))
import concourse.bass as bass
import concourse.mybir as mybir
from concourse.bass_utils import run_bass_kernel_spmd

F32 = mybir.dt.float32
BF16 = mybir.dt.bfloat16
I32 = mybir.dt.int32
AF = mybir.ActivationFunctionType
ALU = mybir.AluOpType
AX = mybir.AxisListType

NDSEM = 8
SEQ = 8192
D = 1024
NBLK = 16
TB = 512
MAGIC = 12582912.0
NTA = 13


def freeze(fn):
    if fn is None or fn.__closure__ is None:
        return fn
    cells = []
    for c in fn.__closure__:
        try:
            cells.append(types.CellType(c.cell_contents))
        except ValueError:
            cells.append(c)
    return types.FunctionType(fn.__code__, fn.__globals__, fn.__name__, fn.__defaults__, tuple(cells))


class Sched:
    ENGS = ("pe", "act", "dve", "pool", "sp")

    def __init__(self, nc, same_engine_sync=True):
        self.nc = nc
        self.same = same_engine_sync
        self.ops = {e: [] for e in self.ENGS}
        self.cnt = {e: 0 for e in self.ENGS}
        self.seen = {e: {} for e in self.ENGS}
        self.state = {}
        self.dcnt = {}
        self.maxval = {}
        self.banks = {}

    def _bank(self, key):
        if isinstance(key, tuple):
            if key in self.banks:
                return self.banks[key]
            for sub in key:
                bk = self._bank(sub)
                if bk:
                    return bk
        return None

    def _bankkeys(self, reads, writes):
        out = set()
        for k in list(reads) + list(writes):
            bk = self._bank(k)
            if bk:
                out.add(("BANK", bk))
        return list(out)

    def _need(self, eng, ev, waits, cross_only=False):
        if ev is None:
            return
        key, val, peng = ev
        if peng == eng and key.startswith("e_") and (not self.same or eng == "pe" or cross_only):
            return
        if self.seen[eng].get(key, 0) >= val:
            return
        self.seen[eng][key] = val
        waits[key] = max(waits.get(key, 0), val)

    def _deps(self, eng, reads, writes):
        waits = {}
        for b in self._bankkeys(reads, writes):
            st = self.state.get(b)
            if st is not None:
                self._need(eng, st[0], waits, cross_only=True)
        for b in reads:
            st = self.state.get(b)
            if st is not None:
                self._need(eng, st[0], waits)
        for b in writes:
            st = self.state.get(b)
            if st is not None:
                self._need(eng, st[0], waits)
                for ev in st[1]:
                    self._need(eng, ev, waits)
        return waits

    def _commit(self, ev, reads, writes):
        self.maxval[ev[0]] = (max(self.maxval.get(ev[0], (0, None))[0], ev[1]), ev[2])
        for b in reads:
            st = self.state.setdefault(b, [None, []])
            st[1].append(ev)
            if len(st[1]) > 24:
                d = {}
                for e in st[1]:
                    if e[0] not in d or d[e[0]][1] < e[1]:
                        d[e[0]] = e
                st[1] = list(d.values())
        for b in writes:
            self.state[b] = [ev, []]
        for b in self._bankkeys(reads, writes):
            self.state[b] = [ev, []]

    def op(self, eng, fn, reads=(), writes=()):
        waits = self._deps(eng, reads, writes)
        self.cnt[eng] += 1
        key = "e_" + eng
        ev = (key, self.cnt[eng], eng)
        self.ops[eng].append((list(waits.items()), freeze(fn), (key, 1)))
        self._commit(ev, reads, writes)
        return ev

    def dma(self, q, out, in_, reads=(), writes=(), **kw):
        waits = self._deps(q, reads, writes)
        i = self.dcnt.get(q, 0)
        self.dcnt[q] = i + 1
        key = "d_%s_%d" % (q, i % NDSEM)
        val = 16 * (i // NDSEM + 1)
        if i >= NDSEM:
            self._need(q, (key, val - 16, q), waits)
        ev = (key, val, q)

        def fn(e, out=out, in_=in_, kw=kw):
            return e.dma_start(out=out, in_=in_, **kw)

        self.ops[q].append((list(waits.items()), fn, (key, 16)))
        self._commit(ev, reads, writes)
        return ev

    def custom(self, eng, fn, semkey, inc, reads=(), writes=()):
        waits = self._deps(eng, reads, writes)
        c = self.dcnt.get(semkey, 0) + inc
        self.dcnt[semkey] = c
        ev = (semkey, c, eng)
        self.ops[eng].append((list(waits.items()), freeze(fn), (semkey, inc)))
        self._commit(ev, reads, writes)
        return ev

    def barrier(self):
        for e in self.ENGS:
            waits = {}
            for key, (val, peng) in self.maxval.items():
                self._need(e, (key, val, peng), waits)
            if waits:
                self.ops[e].append((list(waits.items()), None, None))
        self.state = {}

    def final_wait(self, eng, bufs):
        waits = {}
        for b in bufs:
            st = self.state.get(b)
            if st is not None:
                self._need(eng, st[0], waits)
        self.ops[eng].append((list(waits.items()), None, None))

    def all_semkeys(self):
        keys = set()
        for e in self.ENGS:
            for waits, fn, inc in self.ops[e]:
                for k, _ in waits:
                    keys.add(k)
                if inc is not None:
                    keys.add(inc[0])
        return sorted(keys)

    def emit(self, block, sems):
        def mk(eng_name):
            def body(e):
                for waits, fn, inc in self.ops[eng_name]:
                    for k, v in waits:
                        e.wait_ge(sems[k], v)
                    if fn is None:
                        continue
                    ins = fn(e)
                    ins.then_inc(sems[inc[0]], inc[1])
            return body

        block.tensor(mk("pe"))
        block.scalar(mk("act"))
        block.vector(mk("dve"))
        block.gpsimd(mk("pool"))
        block.sync(mk("sp"))


class RPool:
    def __init__(self, name, tiles):
        self.name, self.tiles, self.i = name, tiles, 0

    def get(self):
        k = self.i % len(self.tiles)
        self.i += 1
        return self.tiles[k], (self.name, k)


def host_consts():
    p = np.arange(128)
    c = {}
    c["ident"] = np.eye(128, dtype=np.float32)
    r = (p % 64)[:, None]
    n = np.arange(64)[None, :]
    su = (n > r).astype(np.float32)
    ui = (n >= r).astype(np.float32)
    sl = (n < r).astype(np.float32)
    c["mask1"] = np.concatenate([su, ui], 1)
    c["masksl"] = sl
    c["identh"] = (n == r).astype(np.float32)
    c["blockones"] = ((p[:, None] // 64) == (p[None, :] // 64)).astype(np.float32)
    perm = np.zeros((128, 128), np.float32)
    for b in (0, 64):
        for i in range(8):
            perm[b + 8 + i, b + i] = -1.0
            perm[b + i, b + 8 + i] = 1.0
    c["perm"] = perm
    inv = 500000.0 ** (-(np.arange(8, dtype=np.float32)) * 2.0 / 16.0)
    inv2pi = np.zeros((128, 1), np.float32)
    for b in (0, 64):
        for i in range(16):
            inv2pi[b + i, 0] = np.float32(inv[i % 8])
    c["inv2pi"] = inv2pi
    seg = np.ones((128, 512), np.float32)
    seg[:, ::64] = 0.0
    c["segmask"] = seg
    qi = np.arange(128)[:, None]
    kj = np.arange(256)[None, :]
    lag = qi + 128 - kj
    band = (lag >= 0) & (lag <= 128)
    c["amask"] = np.where(band, 0.0, -1e30).astype(np.float32)
    c["amask0"] = np.where(band & (kj >= 128), 0.0, -1e30).astype(np.float32)
    return c


CONST_SHAPES = {"ident": [128, 128], "mask1": [128, 128], "masksl": [128, 64], "identh": [128, 64],
                "blockones": [128, 128], "perm": [128, 128], "inv2pi": [128, 1], "segmask": [128, 512],
                }
ACONST_SHAPES = {"amask": [128, 256], "amask0": [128, 256]}


def build(stage=99, mode="all"):
    nc = bass.Bass("TRN2", target_bir_lowering=False)

    def din(name, shape, dt=F32):
        return nc.dram_tensor(name, shape, dt, kind="ExternalInput").ap()

    x_b = din("x_b", [SEQ, D])
    x_q = din("x_q", [2048, D])
    pos_b = din("pos_b", [1, SEQ], I32)
    c_col = din("c_col", [128, 8])
    ada_w = din("ada_w", [D, 6 * D])
    ada_bc = din("ada_bc", [128, 48])
    gcols = din("gcols", [128, 2, 8])
    grows = din("grows", [1, 2 * D])
    w_in_A = din("w_in_A", [D, NTA * 128])
    w_in_G = din("w_in_G", [D, 2048])
    mu_A = din("mu_A", [128, 8])
    w2a = din("w2a", [128, 256])
    g2 = din("g2", [128, 256])
    qsel = din("qsel", [128, 4])
    chv = din("chv", [128, 2, 8])
    w_branch = din("w_branch", [1280, D])
    w_out = din("w_out", [D, D])
    w_ff1 = din("w_ff1", [D, 4 * D])
    w_ff2 = din("w_ff2", [4 * D, D])
    cin = {k: din("k_" + k, v) for k, v in list(CONST_SHAPES.items()) + list(ACONST_SHAPES.items())}
    out_d = nc.dram_tensor("out", [2048, D], F32, kind="ExternalOutput").ap()
    dbg = None
    if stage < 99:
        dbg = nc.dram_tensor("dbg", [4, 320, 2048], BF16, kind="ExternalOutput").ap()
    qkv_scr = nc.dram_tensor("qkv_scr", [128, 5, SEQ], BF16).ap()
    ex_in = nc.dram_tensor("ex_in", [1280, 2048], BF16, **({"kind": "ExternalOutput"} if mode == "A" else {}))
    ex_out = nc.dram_tensor("ex_out", [4 * 1280, 2048], BF16, **({"kind": "ExternalInput"} if mode == "C" else {}))
    h2_scr = nc.dram_tensor("h2_scr", [128, 8, 2048], BF16).ap()
    exg = [[nc.dram_tensor("exg%d_%d" % (qq, i), [4 * (64 if i == 0 else 128), 2048], BF16) for i in range(3)] for qq in range(4)]
    att_scr = nc.dram_tensor("att_scr", [3, SEQ, 66], F32).ap()
    x1_scr = nc.dram_tensor("x1_scr", [2048, D], F32).ap()

    S = Sched(nc)
    global LAST_SCHED
    LAST_SCHED = S
    es_top = ExitStack()
    DBG = {}

    def dd(name, ap, shape, dt, rk):
        if stage >= 99 or name in DBG:
            return
        t = nc.dram_tensor("dbg_" + name, shape, dt, kind="ExternalOutput").ap()
        DBG[name] = t
        S.dma("sp", t, ap, reads=rk, writes=[("dbgout", name)])
    with es_top:
        def mk_alloc(es):
            def sb(name, shape, dt=F32):
                return es.enter_context(nc.sbuf_tensor(name, shape, dt))

            def ps(name, shape, dt=F32):
                return es.enter_context(nc.psum_tensor(name, shape, dt))
            return sb, ps

        sbT, psT = mk_alloc(es_top)
        K = {}
        for k, shp in CONST_SHAPES.items():
            K[k] = sbT("c_" + k, shp)
            S.dma("sp", K[k][:], cin[k], writes=["c_" + k])
        identb = sbT("identb", [128, 128], BF16)
        S.op("dve", lambda e: e.tensor_copy(identb[:], K["ident"][:]), reads=["c_ident"], writes=["identb"])
        ccol = sbT("ccol", [128, 8])
        scol = sbT("scol", [128, 8])
        S.dma("sp", ccol[:], c_col, writes=["ccol"])
        S.op("act", lambda e: e.activation(scol[:], ccol[:], AF.Silu), reads=["ccol"], writes=["scol"])
        adab = sbT("adab", [128, 48])
        S.dma("sp", adab[:], ada_bc, writes=["adab"])
        gcol = sbT("gcol", [128, 2, 8])
        S.dma("sp", gcol[:], gcols, writes=["gcol"])
        modc = sbT("modc", [128, 48])

        def compute_mod(sec, sbl, psl, tagp):
            slab = sbl("adaslab" + tagp, [128, 8, 1024])
            for cch in range(8):
                S.dma("sp", slab[:, cch, :], ada_w[cch * 128:(cch + 1) * 128, sec * 1024:(sec + 1) * 1024],
                      writes=[("slab", tagp, cch)])
            pm = psl("pmod" + tagp, [128, 8])
            S.banks[("pmod", tagp)] = "pmod" + tagp
            for t in range(8):
                for cch in range(8):
                    S.op("pe", lambda e, t=t, cch=cch: e.matmul(pm[:, t:t + 1], slab[:, cch, t * 128:(t + 1) * 128],
                                                               scol[:, cch:cch + 1], start=(cch == 0), stop=(cch == 7)),
                         reads=[("slab", tagp, cch), "scol"], writes=[(("pmod", tagp), t)])
            S.op("dve", lambda e: e.tensor_tensor(modc[:, 8 * sec:8 * sec + 8], pm[:], adab[:, 8 * sec:8 * sec + 8], ALU.add),
                 reads=[(("pmod", tagp), t) for t in range(8)] + ["adab"], writes=[("modc", sec)])

        with ExitStack() as esA:
            sb, ps = mk_alloc(esA)
            with ExitStack() as es0:
                sb0, ps0 = mk_alloc(es0)
                compute_mod(0, sb0, ps0, "a")
                compute_mod(1, sb0, ps0, "b")
                S.barrier()
            s1c = sb("s1c", [128, 8])
            S.op("dve", lambda e: e.scalar_tensor_tensor(s1c[:], modc[:, 8:16], 1.0, gcol[:, 0, :], ALU.add, ALU.mult),
                 reads=[("modc", 1), "gcol"], writes=["s1c"])
            sh1 = modc[:, 0:8]

            wA = sb("wA", [128, 8, NTA * 128], BF16)
            with ExitStack() as esw:
                sbw, psw = mk_alloc(esw)
                wstage = RPool("wstage", [sbw("wstg%d" % i, [128, NTA * 128]) for i in range(2)])
                for cch in range(8):
                    t_, k_ = wstage.get()
                    S.dma("sp", t_[:], w_in_A[cch * 128:(cch + 1) * 128, :], writes=[k_])
                    S.op("pool" if cch % 2 else "act",
                         (lambda e, t_=t_, cch=cch: e.tensor_copy(wA[:, cch, :], t_[:])) if cch % 2 else
                         (lambda e, t_=t_, cch=cch: e.activation(wA[:, cch, :], t_[:], AF.Copy)),
                         reads=[k_], writes=[("wA", cch)])
                S.barrier()
            muA = sb("muA", [128, 8])
            S.dma("sp", muA[:], mu_A, writes=["muA"])
            w2a_f = sb("w2a_f", [128, 256]); g2_f = sb("g2_f", [128, 256])
            w2a_b = sb("w2a_b", [128, 256], BF16); g2_b = sb("g2_b", [128, 256], BF16)
            S.dma("sp", w2a_f[:], w2a, writes=["w2a_f"]); S.dma("sp", g2_f[:], g2, writes=["g2_f"])
            S.op("dve", lambda e: e.tensor_copy(w2a_b[:], w2a_f[:]), reads=["w2a_f"], writes=["w2a_b"])
            S.op("dve", lambda e: e.tensor_copy(g2_b[:], g2_f[:]), reads=["g2_f"], writes=["g2_b"])
            chvt = sb("chvt", [128, 2, 8])
            S.dma("sp", chvt[:], chv, writes=["chvt"])
            omka = sb("omka", [128, 2])
            S.op("dve", lambda e: e.tensor_scalar(omka[:], chvt[:, :, 3], -1.0, 1.0, ALU.mult, ALU.add), reads=["chvt"], writes=["omka"])
            bones_b = sb("bones_b", [128, 128], BF16); perm_b = sb("perm_b", [128, 128], BF16)
            S.op("dve", lambda e: e.tensor_copy(bones_b[:], K["blockones"][:]), reads=["c_blockones"], writes=["bones_b"])
            S.op("dve", lambda e: e.tensor_copy(perm_b[:], K["perm"][:]), reads=["c_perm"], writes=["perm_b"])
            mask1x4 = sb("mask1x4", [128, 4, 128]); maskslx4 = sb("maskslx4", [128, 4, 64]); identhx4 = sb("identhx4", [128, 4, 64])
            for ch in range(4):
                S.op("pool", lambda e, ch=ch: e.tensor_copy(mask1x4[:, ch, :], K["mask1"][:]), reads=["c_mask1"], writes=[("m1", ch)])
                S.op("pool", lambda e, ch=ch: e.tensor_copy(maskslx4[:, ch, :], K["masksl"][:]), reads=["c_masksl"], writes=[("msl", ch)])
                S.op("pool", lambda e, ch=ch: e.tensor_copy(identhx4[:, ch, :], K["identh"][:]), reads=["c_identh"], writes=[("idh", ch)])
            MK1 = [("m1", ch) for ch in range(4)]; MKSL = [("msl", ch) for ch in range(4)]; IDH = [("idh", ch) for ch in range(4)]

            xt_pool = RPool("xt", [sb("xt%d" % i, [128, D]) for i in range(2)])
            xn_pool = RPool("xn", [sb("xn%d" % i, [128, D], BF16) for i in range(4)])
            junk = sb("junk", [128, D], BF16)
            stat_pool = RPool("stat", [sb("stat%d" % i, [128, 4]) for i in range(8)])
            hT_pool = RPool("hT", [sb("hT%d" % i, [128, 8, TB], BF16) for i in range(1)])
            raw = [sb("raw%d" % i, [128, TB + 1]) for i in range(8)]
            for i in range(8):
                S.op("pool", lambda e, i=i: e.memset(raw[i][:], 0.0), writes=[("raw", i)])
            qst_pool = RPool("qst", [sb("qst%d" % i, [128, 5, TB], BF16) for i in range(1)])
            f32p = RPool("f", [sb("f%d" % i, [128, TB]) for i in range(22)])
            b16p = RPool("b", [sb("b%d" % i, [128, TB], BF16) for i in range(10)])
            posi = sb("posi", [128, TB], I32)
            AR = [RPool("AR%d" % p, [sb("AR%d_%d" % (p, i), [128, 8, 128], BF16) for i in range(1)]) for p in range(2)]
            KB = [RPool("KB%d" % p, [sb("KB%d_%d" % (p, i), [128, 8, 128], BF16) for i in range(1)]) for p in range(2)]
            gkp = RPool("Gk", [sb("Gk%d" % i, [128, 4, 128], BF16) for i in range(2)])
            gbp = RPool("Gb", [sb("Gb%d" % i, [128, 4, 128], BF16) for i in range(2)])
            glp = RPool("GL", [sb("GL%d" % i, [128, 4, 64], BF16) for i in range(2)])
            lmp = RPool("LM", [sb("LM%d" % i, [128, 2, 4, 64], BF16) for i in range(3)])
            xp = RPool("X", [sb("X%d" % i, [128, 4, 64], BF16) for i in range(3)])
            tmp_ = RPool("TM", [sb("TM%d" % i, [128, 4, 4, 64], BF16) for i in range(2)])
            zsp = RPool("Zs", [sb("Zs%d" % i, [128, 4, 128], BF16) for i in range(2)])
            aup = RPool("AU", [sb("AU%d" % i, [128, 4, 128], BF16) for i in range(2)])
            mctp = RPool("McT", [sb("McT%d" % i, [128, 4, 64], BF16) for i in range(2)])
            ncwp = RPool("NcW", [sb("NcW%d" % i, [128, 4, 64]) for i in range(2)])
            rhtp = RPool("RhT", [sb("RhT%d" % i, [128, 4, 64], BF16) for i in range(2)])
            Sb = [RPool("Sb%d" % p, [sb("Sb%d_%d" % (p, i), [128, 64], BF16) for i in range(3)]) for p in range(2)]
            ytm = [RPool("Ytm%d" % p, [sb("Ytm%d_%d" % (p, i), [128, 8, 64]) for i in range(1)]) for p in range(2)]
            ysq = sb("ysq", [128, 8, 64])
            ynp = RPool("yn", [sb("yn%d" % i, [128, 8, 64], BF16) for i in range(2)])
            gst = RPool("gst", [sb("gst%d" % i, [128, 8]) for i in range(8)])
            orw_pool = RPool("orw", [sb("orw%d" % i, [128, 2, TB], BF16) for i in range(2)])
            pp = RPool("pp", [ps("pp%d" % i, [128, TB]) for i in range(2)])
            ptx = RPool("ptx", [ps("ptx", [128, 4, 128], BF16)])
            ptb = RPool("ptb", [ps("ptb%d" % i, [128, 1024], BF16) for i in range(1)])
            pg = RPool("pg", [ps("pg%d" % i, [128, TB]) for i in range(3)])
            pys_t = ps("pys", [128, 8, 64])
            pys = RPool("pys", [pys_t[:, i, :] for i in range(8)])
            for i in range(2):
                S.banks[("pp", i)] = "pp%d" % i
            S.banks[("ptx", 0)] = "ptx"
            S.banks[("ptb", 0)] = "ptb"
            for i in range(3):
                S.banks[("pg", i)] = "pg%d" % i
            for i in range(8):
                S.banks[("pys", i)] = "pys"

            sstate = []
            for p in range(2):
                t_, k_ = Sb[p].get()
                S.op("pool", lambda e, t_=t_: e.memset(t_[:], 0.0), writes=[k_])
                sstate.append((t_, k_))

            exv = ex_in.ap().rearrange("(q r) t -> q r t", q=4)

            for tb in range((NBLK if stage >= 50 else 4) if mode != "C" else 0):
                if stage == 9:
                    break
                tok0 = tb * TB
                xns = []
                for tt in range(4):
                    xt, xk = xt_pool.get()
                    S.dma("sp", xt[:], x_b[tok0 + tt * 128: tok0 + (tt + 1) * 128, :], writes=[xk])
                    st, sk = stat_pool.get()
                    S.op("act", lambda e, xt=xt, st=st: e.activation(junk[:], xt[:], AF.Square, accum_out=st[:, 0:1]),
                         reads=[xk], writes=["junk", sk])
                    if SUB <= 1:
                        continue
                    S.op("dve", lambda e, st=st: e.tensor_scalar(st[:, 1:2], st[:, 0:1], 1.0 / D, 1e-6, ALU.mult, ALU.add), reads=[sk], writes=[sk])
                    S.op("act", lambda e, st=st: e.activation(st[:, 2:3], st[:, 1:2], AF.Sqrt), reads=[sk], writes=[sk])
                    S.op("dve", lambda e, st=st: e.reciprocal(st[:, 3:4], st[:, 2:3]), reads=[sk], writes=[sk])
                    if SUB <= 2:
                        continue
                    xn, nk = xn_pool.get()
                    S.op("dve", lambda e, xn=xn, xt=xt, st=st: e.tensor_scalar(xn[:], xt[:], st[:, 3:4], None, ALU.mult),
                         reads=[xk, sk], writes=[nk])
                    xns.append((xn, nk))
                if SUB <= 3:
                    continue
                hT, hk = hT_pool.get()
                for cch in range(8):
                    pt, pk = ptx.get()
                    for tt in range(4):
                        xn, nk = xns[tt]
                        S.op("pe", lambda e, pt=pt, xn=xn, tt=tt, cch=cch: e.transpose(pt[:, tt, :], xn[:, cch * 128:(cch + 1) * 128], identb[:]),
                             reads=[nk, "identb"], writes=[(pk, tt)])
                    if SUB <= 4:
                        continue
                    S.op("act", lambda e, pt=pt, hT=hT, cch=cch: e.activation(hT[:, cch, :], pt[:].rearrange("p a b -> p (a b)"), AF.Identity,
                                                                          bias=sh1[:, cch:cch + 1], scale=s1c[:, cch:cch + 1]),
                         reads=[(pk, tt) for tt in range(4)] + ["s1c", ("modc", 0)], writes=[(hk, cch)])
                HK = [(hk, cch) for cch in range(8)]
                dd("hT", hT[:, 0, :], [128, TB], BF16, HK)

                def proj(tile):
                    pt, pk = pp.get()
                    for cch in range(8):
                        S.op("pe", lambda e, pt=pt, cch=cch, tile=tile: e.matmul(pt[:], wA[:, cch, tile * 128:(tile + 1) * 128], hT[:, cch, :],
                                                                                start=(cch == 0), stop=(cch == 7)),
                             reads=[("wA", cch), (hk, cch)], writes=[pk])
                    return pt, pk

                if stage == 10:
                    continue
                S.dma("sp", posi[:], pos_b[:, tok0:tok0 + TB].partition_broadcast(128), writes=["posi"])
                posf, pfk = f32p.get()
                S.op("dve", lambda e, posf=posf: e.tensor_copy(posf[:], posi[:]), reads=["posi"], writes=[pfk])
                ysn, ysk = f32p.get()
                S.op("dve", lambda e, ysn=ysn, posf=posf: e.tensor_scalar(ysn[:], posf[:], K["inv2pi"][:, 0:1], float(1.0 / (2 * np.pi)), ALU.mult, ALU.mult),
                     reads=[pfk, "c_inv2pi"], writes=[ysk])
                ycs, yck = f32p.get()
                S.op("pool", lambda e, ycs=ycs, ysn=ysn: e.tensor_scalar(ycs[:], ysn[:], 0.25, None, ALU.add), reads=[ysk], writes=[yck])
                tabs = []
                for (yy, yk) in ((ysn, ysk), (ycs, yck)):
                    kk_, kkk = f32p.get()
                    S.op("dve", lambda e, kk_=kk_, yy=yy: e.tensor_scalar(kk_[:], yy[:], MAGIC, MAGIC, ALU.add, ALU.subtract), reads=[yk], writes=[kkk])
                    S.op("dve", lambda e, kk_=kk_, yy=yy: e.tensor_tensor(kk_[:], yy[:], kk_[:], ALU.subtract), reads=[yk, kkk], writes=[kkk])
                    tb_, tk_ = f32p.get()
                    S.op("act", lambda e, tb_=tb_, kk_=kk_: e.activation(tb_[:], kk_[:], AF.Sin, scale=float(2 * np.pi)), reads=[kkk], writes=[tk_])
                    tabs.append((tb_, tk_))
                (SS, ssk), (CC, cck) = tabs

                qst, qk = qst_pool.get()
                for tile in range(5):
                    pt, pk = proj(tile)
                    if tile == 4:
                        S.op("act", lambda e, pt=pt, tile=tile: e.activation(qst[:, tile, :], pt[:], AF.Copy), reads=[pk], writes=[(qk, tile)])
                        continue
                    qb, qbk = b16p.get()
                    S.op("act", lambda e, pt=pt, qb=qb: e.activation(qb[:], pt[:], AF.Copy), reads=[pk], writes=[qbk])
                    nrow = 128 if tile in (0, 2) else 64
                    pq, pqk = pg.get()
                    S.op("pe", lambda e, pq=pq, qb=qb: e.matmul(pq[:], perm_b[:], qb[:], start=True, stop=True), reads=["perm_b", qbk], writes=[pqk])
                    t1, t1k = f32p.get()
                    S.op("dve", lambda e, t1=t1, qb=qb, nrow=nrow: e.tensor_tensor(t1[0:nrow, :], qb[0:nrow, :], CC[0:nrow, :], ALU.mult), reads=[qbk, cck], writes=[t1k])
                    t2, t2k = f32p.get()
                    S.op("dve", lambda e, t2=t2, pq=pq, nrow=nrow: e.tensor_tensor(t2[0:nrow, :], pq[0:nrow, :], SS[0:nrow, :], ALU.mult), reads=[pqk, ssk], writes=[t2k])
                    S.op("pool", lambda e, t1=t1, t2=t2, nrow=nrow, tile=tile: e.tensor_tensor(qst[0:nrow, tile, :], t1[0:nrow, :], t2[0:nrow, :], ALU.add),
                         reads=[t1k, t2k], writes=[(qk, tile)])
                    if nrow == 64:
                        S.op("pool", lambda e, qb=qb, tile=tile: e.tensor_copy(qst[64:128, tile, :], qb[64:128, :]), reads=[qbk], writes=[(qk, tile, "b")])
                S.dma("sp", qkv_scr[:, :, tok0:tok0 + TB], qst[:], reads=[(qk, t) for t in range(5)] + [(qk, 1, "b"), (qk, 3, "b")], writes=[("qkv_scr", tb)])

                if stage == 11:
                    continue
                for i in range(8):
                    pt, pk = proj(5 + i)
                    S.op("pool", lambda e, i=i: e.tensor_copy(raw[i][:, 0:1], raw[i][:, TB:TB + 1]), reads=[("raw", i)], writes=[("raw", i)])
                    S.op("act", lambda e, i=i, pt=pt: e.activation(raw[i][:, 1:TB + 1], pt[:], AF.Copy), reads=[pk], writes=[("raw", i)])

                def shift(i, eng="dve"):
                    d_, dk = f32p.get()
                    S.op("pool", lambda e, d_=d_, i=i: e.tensor_tensor(d_[:], raw[i][:, 0:TB], raw[i][:, 1:TB + 1], ALU.subtract), reads=[("raw", i)], writes=[dk])
                    o_, ok = f32p.get()
                    S.op("dve", lambda e, d_=d_, o_=o_, i=i: e.scalar_tensor_tensor(o_[:], d_[:], muA[:, i:i + 1], raw[i][:, 1:TB + 1], ALU.mult, ALU.add),
                         reads=[dk, ("raw", i), "muA"], writes=[ok])
                    return o_, ok

                lora, lok = shift(6)
                dgs, dgk = shift(7)
                tdw, tdk = b16p.get()
                S.op("act", lambda e, tdw=tdw, lora=lora: e.activation(tdw[0:64, :], lora[0:64, :], AF.Tanh), reads=[lok], writes=[(tdk, 0)])
                S.op("act", lambda e, tdw=tdw, lora=lora: e.activation(tdw[64:128, :], lora[64:128, :], AF.Copy), reads=[lok], writes=[(tdk, 1)])
                sg, sgk = b16p.get()
                S.op("act", lambda e, sg=sg, dgs=dgs: e.activation(sg[:], dgs[:], AF.Sigmoid), reads=[dgk], writes=[sgk])
                dd("lora", lora[:], [128, TB], F32, [lok])
                dd("tdw", tdw[:], [128, TB], BF16, [(tdk, 0), (tdk, 1)])
                dd("w2a_b", w2a_b[:], [128, 256], BF16, ["w2a_b"])

                orw, ork = orw_pool.get()
                for p in range(2):
                    cv = lambda j, p=p: chvt[:, p, j:j + 1]
                    r_s, rk_ = shift(0 + p)
                    k_s, kk_ = shift(2 + p)
                    v_s, vk_ = shift(4 + p)
                    dd("raw0", raw[0][:, 0:TB], [128, TB], F32, [("raw", 0)])
                    dd("r_s", r_s[:], [128, TB], F32, [rk_])
                    dd("k_s", k_s[:], [128, TB], F32, [kk_])
                    pz, pzk = pg.get()
                    S.op("pe", lambda e, pz=pz, p=p: e.matmul(pz[:], w2a_b[0:64, p * 128:(p + 1) * 128], tdw[0:64, :], start=True, stop=True), reads=["w2a_b", (tdk, 0)], writes=[pzk])
                    lw, lwk = f32p.get()
                    S.op("act", lambda e, lw=lw, pz=pz, cv=cv: e.activation(lw[:], pz[:], AF.Sigmoid, bias=cv(0)), reads=[pzk, "chvt"], writes=[lwk])
                    S.op("pool", lambda e, lw=lw: e.tensor_scalar(lw[:], lw[:], float(-np.exp(-0.5)), None, ALU.mult), reads=[lwk], writes=[lwk])
                    pa, pak = pg.get()
                    S.op("pe", lambda e, pa=pa, p=p: e.matmul(pa[:], w2a_b[64:128, p * 128:(p + 1) * 128], tdw[64:128, :], start=True, stop=True), reads=["w2a_b", (tdk, 1)], writes=[pak])
                    icl, ick = f32p.get()
                    S.op("act", lambda e, icl=icl, pa=pa, cv=cv: e.activation(icl[:], pa[:], AF.Sigmoid, bias=cv(1)), reads=[pak, "chvt"], writes=[ick])
                    cw, cwk = f32p.get()
                    S.op("dve", lambda e, cw=cw, lw=lw: e.tensor_tensor_scan(cw[:], K["segmask"][:], lw[:], 0.0, ALU.mult, ALU.add), reads=["c_segmask", lwk], writes=[cwk])
                    cwe, cwek = f32p.get()
                    S.op("pool", lambda e, cwe=cwe, cw=cw, lw=lw: e.tensor_tensor(cwe[:], cw[:], lw[:], ALU.subtract), reads=[cwk, lwk], writes=[cwek])
                    Winc, wik = f32p.get(); Winv, wvk = f32p.get(); Wexc, wek = f32p.get()
                    S.op("act", lambda e, Winc=Winc, cw=cw: e.activation(Winc[:], cw[:], AF.Exp), reads=[cwk], writes=[wik])
                    S.op("act", lambda e, Winv=Winv, cw=cw: e.activation(Winv[:], cw[:], AF.Exp, scale=-1.0), reads=[cwk], writes=[wvk])
                    S.op("act", lambda e, Wexc=Wexc, cwe=cwe: e.activation(Wexc[:], cwe[:], AF.Exp), reads=[cwek], writes=[wek])
                    dd("lw", lw[:], [128, TB], F32, [lwk])
                    dd("icl", icl[:], [128, TB], F32, [ick])
                    dd("cw", cw[:], [128, TB], F32, [cwk])
                    dd("Winc", Winc[:], [128, TB], F32, [wik])
                    dd("Winv", Winv[:], [128, TB], F32, [wvk])
                    kk, kkk = f32p.get()
                    S.op("pool", lambda e, kk=kk, k_s=k_s, cv=cv: e.tensor_scalar(kk[:], k_s[:], cv(2), None, ALU.mult), reads=[kk_, "chvt"], writes=[kkk])
                    kk2, kk2k = b16p.get()
                    S.op("pool", lambda e, kk2=kk2, kk=kk: e.tensor_tensor(kk2[:], kk[:], kk[:], ALU.mult), reads=[kkk], writes=[kk2k])
                    pn, pnk = pg.get()
                    S.op("pe", lambda e, pn=pn, kk2=kk2: e.matmul(pn[:], bones_b[:], kk2[:], start=True, stop=True), reads=["bones_b", kk2k], writes=[pnk])
                    rn, rnk = f32p.get()
                    S.op("act", lambda e, rn=rn, pn=pn: e.activation(rn[:], pn[:], AF.Sqrt), reads=[pnk], writes=[rnk])
                    S.op("dve", lambda e, rn=rn: e.tensor_scalar(rn[:], rn[:], 1e-12, None, ALU.max), reads=[rnk], writes=[rnk])
                    S.op("dve", lambda e, rn=rn: e.reciprocal(rn[:], rn[:]), reads=[rnk], writes=[rnk])
                    S.op("dve", lambda e, kk=kk, rn=rn: e.tensor_tensor(kk[:], kk[:], rn[:], ALU.mult), reads=[kkk, rnk], writes=[kkk])
                    km, kmk = f32p.get()
                    S.op("dve", lambda e, km=km, icl=icl, cv=cv, p=p: e.tensor_scalar(km[:], icl[:], cv(3), omka[:, p:p + 1], ALU.mult, ALU.add), reads=[ick, "chvt", "omka"], writes=[kmk])
                    S.op("pool", lambda e, km=km, k_s=k_s: e.tensor_tensor(km[:], km[:], k_s[:], ALU.mult), reads=[kmk, kk_], writes=[kmk])
                    ar, ark = AR[p].get(); kb, kbk = KB[p].get()
                    v3 = lambda t: t[:].rearrange("p (c t) -> p c t", t=64)
                    S.op("dve", lambda e, ar=ar, r_s=r_s, Winc=Winc, v3=v3: e.tensor_tensor(ar[:, :, 64:128], v3(r_s), v3(Winc), ALU.mult), reads=[rk_, wik], writes=[(ark, "r")])
                    S.op("dve", lambda e, ar=ar, kk=kk, Wexc=Wexc, v3=v3: e.scalar_tensor_tensor(ar[:, :, 0:64], v3(kk), -1.0, v3(Wexc), ALU.mult, ALU.mult), reads=[kkk, wek], writes=[(ark, "a")])
                    S.op("pool", lambda e, kb=kb, km=km, Winv=Winv, v3=v3: e.tensor_tensor(kb[:, :, 0:64], v3(km), v3(Winv), ALU.mult), reads=[kmk, wvk], writes=[(kbk, "k")])
                    bt, btk = f32p.get()
                    S.op("pool", lambda e, bt=bt, kk=kk, icl=icl: e.tensor_tensor(bt[:], kk[:], icl[:], ALU.mult), reads=[kkk, ick], writes=[btk])
                    S.op("pool", lambda e, kb=kb, bt=bt, Winv=Winv, v3=v3: e.tensor_tensor(kb[:, :, 64:128], v3(bt), v3(Winv), ALU.mult), reads=[btk, wvk], writes=[(kbk, "b")])
                    ARK = [(ark, "r"), (ark, "a")]; KBK = [(kbk, "k"), (kbk, "b")]
                    dd("kk", kk[:], [128, TB], F32, [kkk])
                    dd("km", km[:], [128, TB], F32, [kmk])
                    dd("AR", ar[:], [128, 8, 128], BF16, ARK)
                    dd("KB", kb[:], [128, 8, 128], BF16, KBK)
                    vb, vbk = b16p.get()
                    S.op("act", lambda e, vb=vb, v_s=v_s: e.activation(vb[:], v_s[:], AF.Copy), reads=[vk_], writes=[vbk])
                    rkb, rkbk = b16p.get()
                    S.op("dve", lambda e, rkb=rkb, r_s=r_s, km=km, cv=cv: e.scalar_tensor_tensor(rkb[:], r_s[:], cv(4), km[:], ALU.mult, ALU.mult), reads=[rk_, kmk, "chvt"], writes=[rkbk])
                    pb, pbk = pg.get()
                    S.op("pe", lambda e, pb=pb, rkb=rkb: e.matmul(pb[:], bones_b[:], rkb[:], start=True, stop=True), reads=["bones_b", rkbk], writes=[pbk])
                    bon, bonk = f32p.get()
                    S.op("dve", lambda e, bon=bon, pb=pb, v_s=v_s: e.tensor_tensor(bon[:], pb[:], v_s[:], ALU.mult), reads=[pbk, vk_], writes=[bonk])

                    if stage == 12:
                        continue
                    yt, ytk = ytm[p].get()
                    for cg in range(2):
                        c0 = cg * 4
                        def gstage(lsel, rsel_ar, width, maskt, maskk, pool_, lhs_from_kb=True):
                            pgt, pgk = pg.get()
                            pv = pgt[:].rearrange("p (c w) -> p c w", w=128)[:, :, 0:width] if width == 128 else pgt[:, 0:256].rearrange("p (c w) -> p c w", w=64)
                            for ch in range(4):
                                for hb in (0, 64):
                                    if lhs_from_kb:
                                        S.op("pe", lambda e, pv=pv, ch=ch, hb=hb, lsel=lsel: e.matmul(pv[hb:hb + 64, ch, :], kb[hb:hb + 64, c0 + ch, lsel], ar[hb:hb + 64, c0 + ch, :], start=True, stop=True),
                                             reads=ARK + KBK, writes=[(pgk, ch, hb)])
                                    else:
                                        S.op("pe", lambda e, pv=pv, ch=ch, hb=hb: e.matmul(pv[hb:hb + 64, ch, :], ar[hb:hb + 64, c0 + ch, 0:64], kb[hb:hb + 64, c0 + ch, 64:128], start=True, stop=True),
                                             reads=ARK + KBK, writes=[(pgk, ch, hb)])
                            gt, gk_ = pool_.get()
                            S.op("dve", lambda e, gt=gt, pv=pv, maskt=maskt: e.tensor_tensor(gt[:], pv, maskt[:], ALU.mult),
                                 reads=[(pgk, ch, hb) for ch in range(4) for hb in (0, 64)] + maskk, writes=[gk_])
                            return gt, gk_
                        Gk, gkk = gstage(slice(0, 64), None, 128, mask1x4, MK1, gkp)
                        Gb, gbk = gstage(slice(64, 128), None, 128, mask1x4, MK1, gbp)
                        GL, glk = gstage(None, None, 64, maskslx4, MKSL, glp, lhs_from_kb=False)
                        dd("Gk", Gk[:], [128, 4, 128], BF16, [gkk])
                        dd("Gb", Gb[:], [128, 4, 128], BF16, [gbk])
                        dd("GL", GL[:], [128, 4, 64], BF16, [glk])
                        X, xk_ = xp.get()
                        S.op("pool", lambda e, X=X, Gb=Gb: e.tensor_tensor(X[:], Gb[:, :, 0:64], identhx4[:], ALU.add), reads=[gbk] + IDH, writes=[xk_])
                        Lk, Lkk = freeze(lambda ch, hb: GL[hb:hb + 64, ch, :]), glk
                        Mk, Mkk = freeze(lambda ch, hb: Gb[hb:hb + 64, ch, 0:64]), gbk
                        for lev in range(1, 6):
                            plt, plk = pg.get()
                            plv = plt[:].rearrange("p (a c w) -> p a c w", a=2, w=64)
                            for ch in range(4):
                                for hb in (0, 64):
                                    S.op("pe", lambda e, plv=plv, ch=ch, hb=hb, Lk=Lk, Mk=Mk: e.matmul(plv[hb:hb + 64, 0, ch, :], Mk(ch, hb), Lk(ch, hb), start=True, stop=True),
                                         reads=[Lkk, Mkk], writes=[(plk, 0, ch, hb)])
                                    if lev < 5:
                                        S.op("pe", lambda e, plv=plv, ch=ch, hb=hb, Lk=Lk, Mk=Mk: e.matmul(plv[hb:hb + 64, 1, ch, :], Lk(ch, hb), Mk(ch, hb), start=True, stop=True),
                                             reads=[Lkk, Mkk], writes=[(plk, 1, ch, hb)])
                            lm, lmk = lmp.get()
                            na = 2 if lev < 5 else 1
                            S.op("act", lambda e, lm=lm, plv=plv, na=na: e.activation(lm[:, 0:na, :, :], plv[:, 0:na, :, :], AF.Copy),
                                 reads=[(plk, a, ch, hb) for a in range(na) for ch in range(4) for hb in (0, 64)], writes=[lmk])
                            Lk, Lkk = (lambda ch, hb, lm=lm: lm[hb:hb + 64, 0, ch, :]), lmk
                            Mk, Mkk = (lambda ch, hb, lm=lm: lm[hb:hb + 64, 1, ch, :]), lmk
                            pxt, pxk = pg.get()
                            pxv = pxt[:, 0:256].rearrange("p (c w) -> p c w", w=64)
                            for ch in range(4):
                                for hb in (0, 64):
                                    S.op("pe", lambda e, pxv=pxv, ch=ch, hb=hb, Lk=Lk, X=X: e.matmul(pxv[hb:hb + 64, ch, :], Lk(ch, hb), X[hb:hb + 64, ch, :], start=True, stop=True),
                                         reads=[Lkk, xk_], writes=[(pxk, ch, hb)])
                            Xn, xnk = xp.get()
                            S.op("dve", lambda e, Xn=Xn, pxv=pxv, X=X: e.tensor_tensor(Xn[:], pxv, X[:], ALU.add),
                                 reads=[(pxk, ch, hb) for ch in range(4) for hb in (0, 64)] + [xk_], writes=[xnk])
                            X, xk_ = Xn, xnk
                        dd("X", X[:], [128, 4, 64], BF16, [xk_])
                        if stage == 13:
                            continue
                        ptt, ptk = ptb.get()
                        ptv = ptt[:].rearrange("p (c k w) -> p c k w", k=4, w=64)
                        for ch in range(4):
                            c = c0 + ch
                            srcs = [(freeze(lambda hb, c=c: vb[hb:hb + 64, c * 64:(c + 1) * 64]), [vbk]),
                                    (freeze(lambda hb, c=c: ar[hb:hb + 64, c, 0:64]), ARK),
                                    (freeze(lambda hb, c=c: kb[hb:hb + 64, c, 0:64]), KBK),
                                    (freeze(lambda hb, c=c: kb[hb:hb + 64, c, 64:128]), KBK)]
                            for kind, (sf, sk_) in enumerate(srcs):
                                for hb in (0, 64):
                                    S.op("pe", lambda e, ptv=ptv, ch=ch, kind=kind, hb=hb, sf=sf: e.transpose(ptv[hb:hb + 64, ch, kind, :], sf(hb), identb[hb:hb + 64, hb:hb + 64]),
                                         reads=sk_ + ["identb"], writes=[(ptk, ch, kind, hb)])
                        TM, tmk = tmp_.get()
                        S.op("act", lambda e, TM=TM, ptv=ptv: e.activation(TM[:], ptv, AF.Copy),
                             reads=[(ptk, ch, kind, hb) for ch in range(4) for kind in range(4) for hb in (0, 64)], writes=[tmk])
                        pzt, pzk2 = pg.get()
                        pzv = pzt[:, 0:256].rearrange("p (c w) -> p c w", w=64)
                        for ch in range(4):
                            for hb in (0, 64):
                                S.op("pe", lambda e, pzv=pzv, ch=ch, hb=hb, Gk=Gk, TM=TM: e.matmul(pzv[hb:hb + 64, ch, :], Gk[hb:hb + 64, ch, 0:64], TM[hb:hb + 64, ch, 0, :], start=True, stop=True),
                                     reads=[gkk, tmk], writes=[(pzk2, ch, hb)])
                        Zs, zsk = zsp.get()
                        S.op("pool", lambda e, Zs=Zs, TM=TM: e.tensor_copy(Zs[:, :, 0:64], TM[:, :, 1, :]), reads=[tmk], writes=[(zsk, 0)])
                        S.op("act", lambda e, Zs=Zs, pzv=pzv: e.activation(Zs[:, :, 64:128], pzv, AF.Copy),
                             reads=[(pzk2, ch, hb) for ch in range(4) for hb in (0, 64)], writes=[(zsk, 1)])
                        pat, pauk = pg.get()
                        pav = pat[:].rearrange("p (c w) -> p c w", w=128)
                        for ch in range(4):
                            for hb in (0, 64):
                                S.op("pe", lambda e, pav=pav, ch=ch, hb=hb, X=X, Zs=Zs: e.matmul(pav[hb:hb + 64, ch, :], X[hb:hb + 64, ch, :], Zs[hb:hb + 64, ch, :], start=True, stop=True),
                                     reads=[xk_, (zsk, 0), (zsk, 1)], writes=[(pauk, ch, hb)])
                        AU, auk = aup.get()
                        S.op("dve", lambda e, AU=AU, pav=pav: e.tensor_copy(AU[:], pav), reads=[(pauk, ch, hb) for ch in range(4) for hb in (0, 64)], writes=[auk])
                        dd("TM", TM[:], [128, 4, 4, 64], BF16, [tmk])
                        dd("AU", AU[:], [128, 4, 128], BF16, [auk])
                        pmt, pmk = pg.get()
                        pmv = pmt[:].rearrange("p (c w) -> p c w", w=128)
                        for ch in range(4):
                            for hb in (0, 64):
                                S.op("pe", lambda e, pmv=pmv, ch=ch, hb=hb, AU=AU, TM=TM: e.matmul(pmv[hb:hb + 64, ch, 0:64], AU[hb:hb + 64, ch, 0:64], TM[hb:hb + 64, ch, 3, :], start=True, stop=True),
                                     reads=[auk, tmk], writes=[(pmk, ch, hb, 0)])
                                S.op("pe", lambda e, pmv=pmv, ch=ch, hb=hb, AU=AU, TM=TM: e.matmul(pmv[hb:hb + 64, ch, 64:128], TM[hb:hb + 64, ch, 3, :], AU[hb:hb + 64, ch, 64:128], start=True, stop=False),
                                     reads=[auk, tmk], writes=[(pmk, ch, hb, 1)])
                                S.op("pe", lambda e, pmv=pmv, ch=ch, hb=hb, TM=TM: e.matmul(pmv[hb:hb + 64, ch, 64:128], TM[hb:hb + 64, ch, 2, :], TM[hb:hb + 64, ch, 0, :], start=False, stop=True),
                                     reads=[tmk], writes=[(pmk, ch, hb, 1)])
                        McT, mck = mctp.get()
                        S.op("dve", lambda e, McT=McT, pmv=pmv: e.tensor_tensor(McT[:], pmv[:, :, 0:64], identhx4[:], ALU.add),
                             reads=[(pmk, ch, hb, 0) for ch in range(4) for hb in (0, 64)] + IDH, writes=[mck])
                        NcW, nck = ncwp.get()
                        for ch in range(4):
                            wc = Winc[:, (c0 + ch) * 64 + 63:(c0 + ch) * 64 + 64]
                            S.op("dve", lambda e, NcW=NcW, pmv=pmv, ch=ch, wc=wc: e.tensor_scalar(NcW[:, ch, :], pmv[:, ch, 64:128], wc, None, ALU.mult),
                                 reads=[(pmk, ch, hb, 1) for hb in (0, 64)] + [wik], writes=[(nck, ch)])
                        prt, prk = pg.get()
                        prv = prt[:, 0:256].rearrange("p (c w) -> p c w", w=64)
                        for ch in range(4):
                            for hb in (0, 64):
                                S.op("pe", lambda e, prv=prv, ch=ch, hb=hb, AU=AU, Gb=Gb: e.matmul(prv[hb:hb + 64, ch, :], AU[hb:hb + 64, ch, 0:64], Gb[hb:hb + 64, ch, 64:128], start=True, stop=True),
                                     reads=[auk, gbk], writes=[(prk, ch, hb)])
                        RhT, rhk = rhtp.get()
                        S.op("dve", lambda e, RhT=RhT, prv=prv: e.tensor_tensor(RhT[:], prv, ar[:, c0:c0 + 4, 64:128], ALU.add),
                             reads=[(prk, ch, hb) for ch in range(4) for hb in (0, 64)] + ARK, writes=[rhk])
                        dd("McT", McT[:], [128, 4, 64], BF16, [mck])
                        dd("NcW", NcW[:], [128, 4, 64], F32, [(nck, ch) for ch in range(4)])
                        dd("RhT", RhT[:], [128, 4, 64], BF16, [rhk])
                        for ch in range(4):
                            c = c0 + ch
                            s0, s0k = sstate[p]
                            py, pyk = pys.get()
                            for hb in (0, 64):
                                S.op("pe", lambda e, py=py, hb=hb, ch=ch, Gb=Gb, AU=AU: e.matmul(py[hb:hb + 64, :], Gb[hb:hb + 64, ch, 64:128], AU[hb:hb + 64, ch, 64:128], start=True, stop=False),
                                     reads=[gbk, auk], writes=[(pyk, hb)])
                                S.op("pe", lambda e, py=py, hb=hb, ch=ch, Gk=Gk, TM=TM: e.matmul(py[hb:hb + 64, :], Gk[hb:hb + 64, ch, 64:128], TM[hb:hb + 64, ch, 0, :], start=False, stop=False),
                                     reads=[gkk, tmk], writes=[(pyk, hb)])
                                S.op("pe", lambda e, py=py, hb=hb, ch=ch, RhT=RhT, s0=s0: e.matmul(py[hb:hb + 64, :], RhT[hb:hb + 64, ch, :], s0[hb:hb + 64, :], start=False, stop=True),
                                     reads=[rhk, s0k], writes=[(pyk, hb)])
                            S.op("act", lambda e, py=py, yt=yt, c=c: e.activation(yt[:, c, :], py, AF.Copy), reads=[(pyk, 0), (pyk, 64)], writes=[(ytk, c)])
                            psn, psk = pys.get()
                            for hb in (0, 64):
                                S.op("pe", lambda e, psn=psn, hb=hb, ch=ch, McT=McT, s0=s0: e.matmul(psn[hb:hb + 64, :], McT[hb:hb + 64, ch, :], s0[hb:hb + 64, :], start=True, stop=True),
                                     reads=[mck, s0k], writes=[(psk, hb)])
                            s1_, s1k = Sb[p].get()
                            wc = Winc[:, c * 64 + 63:c * 64 + 64]
                            S.op("dve", lambda e, s1_=s1_, psn=psn, wc=wc, NcW=NcW, ch=ch: e.scalar_tensor_tensor(s1_[:], psn, wc, NcW[:, ch, :], ALU.mult, ALU.add),
                                 reads=[(psk, 0), (psk, 64), wik, (nck, ch)], writes=[s1k])
                            sstate[p] = (s1_, s1k)
                    if stage == 13:
                        continue
                    YK = [(ytk, c) for c in range(8)]
                    dd("yt", yt[:], [128, 8, 64], F32, YK)
                    dd("Sb", sstate[p][0][:], [128, 64], BF16, [sstate[p][1]])
                    g0, g0k = gst.get(); g1, g1k = gst.get(); g2_, g2k = gst.get(); g3, g3k = gst.get()
                    S.op("dve", lambda e, g0=g0, yt=yt: e.tensor_reduce(g0[:], yt[:], AX.X, ALU.add), reads=YK, writes=[g0k])
                    S.op("pool", lambda e, yt=yt: e.tensor_tensor(ysq[:], yt[:], yt[:], ALU.mult), reads=YK, writes=["ysq"])
                    S.op("dve", lambda e, g1=g1: e.tensor_reduce(g1[:], ysq[:], AX.X, ALU.add), reads=["ysq"], writes=[g1k])
                    S.op("dve", lambda e, g0=g0: e.tensor_scalar(g0[:], g0[:], 1.0 / 64, None, ALU.mult), reads=[g0k], writes=[g0k])
                    S.op("dve", lambda e, g2_=g2_, g0=g0: e.tensor_tensor(g2_[:], g0[:], g0[:], ALU.mult), reads=[g0k], writes=[g2k])
                    S.op("dve", lambda e, g1=g1, g2_=g2_: e.scalar_tensor_tensor(g1[:], g1[:], 1.0 / 64, g2_[:], ALU.mult, ALU.subtract), reads=[g1k, g2k], writes=[g1k])
                    S.op("dve", lambda e, g1=g1: e.tensor_scalar(g1[:], g1[:], 64e-5, None, ALU.add), reads=[g1k], writes=[g1k])
                    S.op("act", lambda e, g3=g3, g1=g1: e.activation(g3[:], g1[:], AF.Sqrt), reads=[g1k], writes=[g3k])
                    S.op("dve", lambda e, g3=g3: e.reciprocal(g3[:], g3[:]), reads=[g3k], writes=[g3k])
                    yn, ynk = ynp.get()
                    for c in range(8):
                        S.op("dve" if c % 2 else "pool", lambda e, yn=yn, yt=yt, c=c, g0=g0, g3=g3: e.tensor_scalar(yn[:, c, :], yt[:, c, :], g0[:, c:c + 1], g3[:, c:c + 1], ALU.subtract, ALU.mult),
                             reads=[(ytk, c), g0k, g3k], writes=[(ynk, c)])
                    pt2, pt2k = ptb.get()
                    p2v = pt2[:, 0:512].rearrange("p (c w) -> p c w", w=64)
                    for c in range(8):
                        for hb in (0, 64):
                            S.op("pe", lambda e, p2v=p2v, c=c, hb=hb, yn=yn: e.transpose(p2v[hb:hb + 64, c, :], yn[hb:hb + 64, c, :], identb[hb:hb + 64, hb:hb + 64]),
                                 reads=[(ynk, c), "identb"], writes=[(pt2k, c, hb)])
                    o1, o1k = f32p.get()
                    S.op("act", lambda e, o1=o1, pt2=pt2, cv=cv: e.activation(o1[:], pt2[:, 0:512], AF.Identity, bias=cv(6), scale=cv(5)),
                         reads=[(pt2k, c, hb) for c in range(8) for hb in (0, 64)] + ["chvt"], writes=[o1k])
                    S.op("pool", lambda e, o1=o1, bon=bon: e.tensor_tensor(o1[:], o1[:], bon[:], ALU.add), reads=[o1k, bonk], writes=[o1k])
                    dd("yn", yn[:], [128, 8, 64], BF16, [(ynk, c) for c in range(8)])
                    dd("o1", o1[:], [128, TB], F32, [o1k])
                    dd("bon", bon[:], [128, TB], F32, [bonk])
                    pgg, pggk = pg.get()
                    S.op("pe", lambda e, pgg=pgg, p=p: e.matmul(pgg[:], g2_b[:, p * 128:(p + 1) * 128], sg[:], start=True, stop=True), reads=["g2_b", sgk], writes=[pggk])
                    S.op("dve", lambda e, p=p, pgg=pgg, o1=o1: e.tensor_tensor(orw[:, p, :], pgg[:], o1[:], ALU.mult), reads=[pggk, o1k], writes=[(ork, p)])
                if stage in (12, 13):
                    continue
                qd = tb // 4
                tq = (tb % 4) * TB
                S.dma("sp", exv[qd, 64:320, tq:tq + TB].rearrange("(p c) t -> c p t", p=2), orw[:], reads=[(ork, 0), (ork, 1)], writes=[("ex_in", tb)])
            S.barrier()


        if stage >= 30 and mode != "C":
          with ExitStack() as esB:
            sb, ps = mk_alloc(esB)
            exv = ex_in.ap().rearrange("(q r) t -> q r t", q=4)
            qkv = sb("qkv", [128, 5, SEQ], BF16)
            for t5 in range(5):
                for hf in range(4):
                    S.dma("sp", qkv[:, t5, hf * 2048:(hf + 1) * 2048], qkv_scr[:, t5, hf * 2048:(hf + 1) * 2048], writes=[("qkv", t5, hf)])
            QKV = [("qkv", t5, hf) for t5 in range(5) for hf in range(4)]
            am = sb("am", [128, 256]); am0 = sb("am0", [128, 256])
            S.dma("sp", am[:], cin["amask"], writes=["am"]); S.dma("sp", am0[:], cin["amask0"], writes=["am0"])
            ps_s = RPool("ps_s", [ps("ps_s%d" % i, [128, 256]) for i in range(2)])
            ps_pt = RPool("ps_pt", [ps("ps_pt", [128, 2, 128], BF16)])
            ps_v = RPool("ps_v", [ps("ps_v", [128, 64], BF16)])
            ps_o = RPool("ps_o", [ps("ps_o%d" % i, [128, 64]) for i in range(2)])
            ps_m = RPool("ps_m", [ps("ps_m", [64, 4, 128], BF16)])
            for i in range(2):
                S.banks[("ps_s", i)] = "ps_s%d" % i
                S.banks[("ps_o", i)] = "ps_o%d" % i
            S.banks[("ps_pt", 0)] = "ps_pt"; S.banks[("ps_v", 0)] = "ps_v"; S.banks[("ps_m", 0)] = "ps_m"
            smp = RPool("sm", [sb("sm%d" % i, [128, 256]) for i in range(3)])
            pbp = RPool("pb", [sb("pb%d" % i, [128, 256], BF16) for i in range(3)])
            ptp = RPool("ptp", [sb("ptp%d" % i, [128, 2, 128], BF16) for i in range(3)])
            vtp = RPool("vt", [sb("vt%d" % i, [128, 64], BF16) for i in range(4)])
            stp = RPool("st66", [sb("st66_%d" % i, [128, 66]) for i in range(4)])
            sts = RPool("sts", [sb("sts%d" % i, [128, 2]) for i in range(4)])
            heads = [(0, 0, 2, 0, 4, 0), (0, 64, 2, 64, 4, 64), (1, 0, 3, 0, 1, 64)]
            for g, dil in enumerate((1, 4, 16)):
                qt_, qb_, kt_, kb_, vt_, vb_ = heads[g]
                nb = SEQ // (128 * dil)
                for r in range(dil):
                    vprev = None
                    for n in range(nb):
                        q0 = 128 * n * dil + r
                        k0 = 128 * (n - 1) * dil + r if n > 0 else q0
                        qs = qkv[qb_:qb_ + 64, qt_, q0:q0 + 127 * dil + 1:dil]
                        if n > 0:
                            ks = qkv[kb_:kb_ + 64, kt_, k0:k0 + 255 * dil + 1:dil]
                        pS, pSk = ps_s.get()
                        if n > 0:
                            S.op("pe", lambda e: e.matmul(pS[:], qs, ks, start=True, stop=True), reads=QKV, writes=[pSk])
                        else:
                            kc = qkv[kb_:kb_ + 64, kt_, q0:q0 + 127 * dil + 1:dil]
                            S.op("pe", lambda e: e.matmul(pS[:, 0:128], qs, kc, start=True, stop=True), reads=QKV, writes=[(pSk, 0)])
                            S.op("pe", lambda e: e.matmul(pS[:, 128:256], qs, kc, start=True, stop=True), reads=QKV, writes=[(pSk, 1)])
                        mk_ = am if n > 0 else am0
                        sm, smk = smp.get()
                        S.op("dve", lambda e: e.tensor_tensor(sm[:], pS[:], mk_[:], ALU.add), reads=[pSk, (pSk, 0), (pSk, 1), "am", "am0"], writes=[smk])
                        st, stk = stp.get()
                        ss_, ssk_ = sts.get()
                        S.op("dve", lambda e: e.tensor_reduce(st[:, 64:65], sm[:], AX.X, ALU.max), reads=[smk], writes=[(stk, "m")])
                        S.op("dve", lambda e: e.tensor_scalar(ss_[:, 0:1], st[:, 64:65], -0.125, None, ALU.mult), reads=[(stk, "m")], writes=[ssk_])
                        pb, pbk = pbp.get()
                        S.op("act", lambda e: e.activation(pb[:], sm[:], AF.Exp, bias=ss_[:, 0:1], scale=0.125, accum_out=st[:, 65:66]), reads=[smk, ssk_], writes=[pbk, (stk, "l")])
                        ppt, pptk = ps_pt.get()
                        for hf in range(2):
                            S.op("pe", lambda e, hf=hf: e.transpose(ppt[:, hf, :], pb[:, hf * 128:(hf + 1) * 128], identb[:]), reads=[pbk, "identb"], writes=[(pptk, hf)])
                        pts, ptsk = ptp.get()
                        S.op("act", lambda e: e.activation(pts[:], ppt[:], AF.Copy), reads=[(pptk, 0), (pptk, 1)], writes=[ptsk])
                        pv_, pvk = ps_v.get()
                        vsl = qkv[vb_:vb_ + 64, vt_, q0:q0 + 127 * dil + 1:dil]
                        S.op("pe", lambda e: e.transpose(pv_[:], vsl, identb[vb_:vb_ + 64, vb_:vb_ + 64]), reads=QKV + ["identb"], writes=[pvk])
                        vcur, vck = vtp.get()
                        S.op("dve", lambda e: e.tensor_copy(vcur[:], pv_[:]), reads=[pvk], writes=[vck])
                        if vprev is None:
                            vprev = (vcur, vck)
                        vp, vpk = vprev
                        po, pok = ps_o.get()
                        S.op("pe", lambda e: e.matmul(po[:], pts[:, 0, :], vp[:], start=True, stop=False), reads=[ptsk, vpk], writes=[pok])
                        S.op("pe", lambda e: e.matmul(po[:], pts[:, 1, :], vcur[:], start=False, stop=True), reads=[ptsk, vck], writes=[pok])
                        S.op("dve", lambda e: e.tensor_copy(st[:, 0:64], po[:]), reads=[pok], writes=[(stk, "o")])
                        S.dma("sp", att_scr[g, q0:q0 + 127 * dil + 1:dil, :], st[:], reads=[(stk, "o"), (stk, "m"), (stk, "l")], writes=[("att_scr", g, r, n)])
                        vprev = (vcur, vck)
            S.barrier()
            mgp = RPool("mg", [sb("mg%d" % i, [128, 3, 66]) for i in range(3)])
            mws = RPool("mw", [sb("mw%d" % i, [128, 8]) for i in range(4)])
            mo = RPool("mo", [sb("mo%d" % i, [128, 64]) for i in range(3)])
            mob = RPool("mob", [sb("mob%d" % i, [128, 64], BF16) for i in range(3)])
            oat = RPool("oat", [sb("oat%d" % i, [64, 4, 128], BF16) for i in range(2)])
            for tb in range(NBLK):
                pm_, pmk_ = ps_m.get()
                for tt in range(4):
                    T0 = tb * TB + tt * 128
                    mg, mgk = mgp.get()
                    S.dma("sp", mg[:], att_scr[:, T0:T0 + 128, :].rearrange("g t c -> t g c"), writes=[mgk])
                    w, wk = mws.get()
                    S.op("dve", lambda e: e.tensor_reduce(w[:, 3:4], mg[:, :, 64], AX.X, ALU.max), reads=[mgk], writes=[wk])
                    S.op("dve", lambda e: e.tensor_scalar(w[:, 3:4], w[:, 3:4], -0.125, None, ALU.mult), reads=[wk], writes=[wk])
                    S.op("act", lambda e: e.activation(w[:, 0:3], mg[:, :, 64], AF.Exp, bias=w[:, 3:4], scale=0.125), reads=[mgk, wk], writes=[wk])
                    S.op("dve", lambda e: e.tensor_tensor(w[:, 4:7], w[:, 0:3], mg[:, :, 65], ALU.mult), reads=[wk, mgk], writes=[wk])
                    S.op("dve", lambda e: e.tensor_reduce(w[:, 7:8], w[:, 4:7], AX.X, ALU.add), reads=[wk], writes=[wk])
                    S.op("dve", lambda e: e.reciprocal(w[:, 7:8], w[:, 7:8]), reads=[wk], writes=[wk])
                    o_, ok_ = mo.get()
                    S.op("dve", lambda e: e.tensor_scalar(o_[:], mg[:, 0, 0:64], w[:, 0:1], None, ALU.mult), reads=[mgk, wk], writes=[ok_])
                    S.op("dve", lambda e: e.scalar_tensor_tensor(o_[:], mg[:, 1, 0:64], w[:, 1:2], o_[:], ALU.mult, ALU.add), reads=[mgk, wk, ok_], writes=[ok_])
                    S.op("dve", lambda e: e.scalar_tensor_tensor(o_[:], mg[:, 2, 0:64], w[:, 2:3], o_[:], ALU.mult, ALU.add), reads=[mgk, wk, ok_], writes=[ok_])
                    ob, obk = mob.get()
                    S.op("dve", lambda e: e.tensor_scalar(ob[:], o_[:], w[:, 7:8], None, ALU.mult), reads=[ok_, wk], writes=[obk])
                    S.op("pe", lambda e, tt=tt: e.transpose(pm_[:, tt, :], ob[:], identb[:]), reads=[obk, "identb"], writes=[(pmk_, tt)])
                oa, oak = oat.get()
                S.op("act", lambda e: e.activation(oa[:], pm_[:], AF.Copy), reads=[(pmk_, tt) for tt in range(4)], writes=[oak])
                qd = tb // 4
                tq = (tb % 4) * TB
                S.dma("sp", exv[qd, 0:64, tq:tq + TB], oa[:].rearrange("p a b -> p (a b)"), reads=[oak], writes=[("ex_in_a", tb)])
            S.barrier()

        if mode == "A":
            keys = S.all_semkeys()
            sems = {k: es_top.enter_context(nc.semaphore(k.replace("_", ""))) for k in keys}
            with nc.Block() as block:
                S.emit(block, sems)
            return nc
        if stage >= 40 and mode == "all":
            for qq in range(4):
                for i, (ra, rb) in enumerate(((0, 64), (64, 192), (192, 320))):
                    S.custom("pool", lambda e: e.collective_compute("AllGather", ALU.bypass, replica_groups=[[0, 1, 2, 3], [4, 5, 6, 7]],
                                                                    ins=[ex_in.ap()[320 * qq + ra:320 * qq + rb, :]], outs=[exg[qq][i].ap()]),
                             "cc", 1, reads=[], writes=[("exg", qq, i)])
            S.barrier()

        if stage >= 50 and mode != "A":
          with ExitStack() as esC:
            sbC, psC = mk_alloc(esC)
            for (sa, sb_) in ((2, 3), (4, 5)):
                with ExitStack() as es0:
                    sb0, ps0 = mk_alloc(es0)
                    compute_mod(sa, sb0, ps0, "m%d" % sa)
                    compute_mod(sb_, sb0, ps0, "m%d" % sb_)
                    S.barrier()
            s1c2 = sbC("s1c2", [128, 8]); s2c = sbC("s2c", [128, 8])
            S.op("dve", lambda e: e.scalar_tensor_tensor(s1c2[:], modc[:, 8:16], 1.0, gcol[:, 0, :], ALU.add, ALU.mult), writes=["s1c2"])
            S.op("dve", lambda e: e.scalar_tensor_tensor(s2c[:], modc[:, 32:40], 1.0, gcol[:, 1, :], ALU.add, ALU.mult), writes=["s2c"])
            sh1 = modc[:, 0:8]; sh2 = modc[:, 24:32]
            G = [sbC("G1", [128, D]), sbC("G2", [128, D])]
            with ExitStack() as es0:
                sb0, ps0 = mk_alloc(es0)
                ones = sb0("ones", [128, 128])
                S.op("pool", lambda e: e.memset(ones[:], 1.0), writes=["ones"])
                grow = sb0("grow", [128, 2 * D])
                S.dma("sp", grow[:], grows.partition_broadcast(128), writes=["grow"])
                crp = RPool("cr", [sb0("cr%d" % i, [128, 128]) for i in range(2)])
                pbc = ps0("pbc", [128, D])
                S.banks[("pbc",)] = "pbc"
                for gi, sec in enumerate((2, 5)):
                    for c in range(8):
                        cr, crk = crp.get()
                        S.op("dve", lambda e: e.tensor_scalar(cr[:], ones[:], modc[:, sec * 8 + c:sec * 8 + c + 1], None, ALU.mult), reads=["ones"], writes=[crk])
                        S.op("pe", lambda e: e.matmul(pbc[:, c * 128:(c + 1) * 128], cr[:], K["ident"][:], start=True, stop=True), reads=[crk], writes=[(("pbc",), c)])
                    S.op("dve", lambda e: e.tensor_tensor(G[gi][:], pbc[:], grow[:, gi * D:(gi + 1) * D], ALU.mult),
                         reads=[(("pbc",), c) for c in range(8)] + ["grow"], writes=[("G", gi)])
                S.barrier()

            def load_cast(dst_fn, src_fn, n, width, stg, tag):
                for i in range(n):
                    t_, k_ = stg.get()
                    S.dma("sp", t_[:, 0:width], src_fn(i), writes=[k_])
                    if i % 2:
                        S.op("pool", lambda e: e.tensor_copy(dst_fn(i), t_[:, 0:width]), reads=[k_], writes=[(tag, i)])
                    else:
                        S.op("act", lambda e: e.activation(dst_fn(i), t_[:, 0:width], AF.Copy), reads=[k_], writes=[(tag, i)])

            with ExitStack() as es1:
                sb, ps = mk_alloc(es1)
                stg = RPool("stg", [sb("stg%d" % i, [128, 2048]) for i in range(2)])
                wG = sb("wG", [128, 8, 2048], BF16); wbr = sb("wbr", [128, 10, D], BF16); wo = sb("wo", [128, 8, D], BF16)
                load_cast(lambda i: wG[:, i, :], lambda i: w_in_G[i * 128:(i + 1) * 128, :], 8, 2048, stg, "wG")
                load_cast(lambda i: wbr[:, i, :], lambda i: w_branch[i * 128:(i + 1) * 128, :], 10, D, stg, "wbr")
                load_cast(lambda i: wo[:, i, :], lambda i: w_out[i * 128:(i + 1) * 128, :], 8, D, stg, "wo")
                S.barrier()
                xt_pool = RPool("cxt", [sb("cxt%d" % i, [128, D]) for i in range(2)])
                xn_pool = RPool("cxn", [sb("cxn%d" % i, [128, D], BF16) for i in range(4)])
                junk = sb("cjunk", [128, D], BF16)
                stat_pool = RPool("cstat", [sb("cstat%d" % i, [128, 4]) for i in range(8)])
                hT = sb("chT", [128, 8, TB], BF16)
                sg = sb("csg", [128, 16, TB], BF16)
                oall = sb("oall", [128, 10, TB], BF16)
                candp = RPool("cand", [sb("cand%d" % i, [128, 10, TB], BF16) for i in range(2)])
                qs_t = sb("qs_t", [128, 4])
                S.dma("sp", qs_t[:], qsel, writes=["qs_t"])
                mT = sb("mT", [128, 8, TB], BF16)
                tmpp = RPool("ctmp", [sb("ctmp%d" % i, [128, TB]) for i in range(3)])
                big = RPool("cbig", [sb("cbig%d" % i, [128, D]) for i in range(2)])
                x1p = RPool("cx1", [sb("cx1_%d" % i, [128, D]) for i in range(2)])
                xn2p = RPool("cxn2", [sb("cxn2_%d" % i, [128, D], BF16) for i in range(2)])
                h2sp = RPool("ch2s", [sb("ch2s%d" % i, [128, 8, 128], BF16) for i in range(2)])
                pacc = RPool("pacc", [ps("pacc%d" % i, [128, TB]) for i in range(3)])
                pmix = RPool("pmix", [ps("pmix", [128, D])])
                ptx = RPool("cptx", [ps("cptx", [128, 4, 128], BF16)])
                pt8 = RPool("cpt8", [ps("cpt8", [128, 8, 128], BF16)])
                for i in range(3):
                    S.banks[("pacc", i)] = "pacc%d" % i
                S.banks[("pmix", 0)] = "pmix"; S.banks[("cptx", 0)] = "cptx"; S.banks[("cpt8", 0)] = "cpt8"
                exo = ex_out.ap().rearrange("(r q c) t -> r q c t", r=4, q=4)

                def rstd_of(st, sk):
                    S.op("dve", lambda e: e.tensor_scalar(st[:, 1:2], st[:, 0:1], 1.0 / D, 1e-6, ALU.mult, ALU.add), reads=[sk], writes=[sk])
                    S.op("act", lambda e: e.activation(st[:, 2:3], st[:, 1:2], AF.Sqrt), reads=[sk], writes=[sk])
                    S.op("dve", lambda e: e.reciprocal(st[:, 3:4], st[:, 2:3]), reads=[sk], writes=[sk])

                for tb in range(4):
                    tok0 = tb * TB
                    xns = []
                    for tt in range(4):
                        xt, xk = xt_pool.get()
                        S.dma("sp", xt[:], x_q[tok0 + tt * 128: tok0 + (tt + 1) * 128, :], writes=[xk])
                        st, sk = stat_pool.get()
                        S.op("act", lambda e: e.activation(junk[:], xt[:], AF.Square, accum_out=st[:, 0:1]), reads=[xk], writes=["cjunk", sk])
                        rstd_of(st, sk)
                        xn, nk = xn_pool.get()
                        S.op("dve", lambda e: e.tensor_scalar(xn[:], xt[:], st[:, 3:4], None, ALU.mult), reads=[xk, sk], writes=[nk])
                        xns.append((xn, nk))
                    for cch in range(8):
                        pt, pk = ptx.get()
                        for tt in range(4):
                            xn, nk = xns[tt]
                            S.op("pe", lambda e: e.transpose(pt[:, tt, :], xn[:, cch * 128:(cch + 1) * 128], identb[:]), reads=[nk, "identb"], writes=[(pk, tt)])
                        S.op("act", lambda e: e.activation(hT[:, cch, :], pt[:].rearrange("p a b -> p (a b)"), AF.Identity, bias=sh1[:, cch:cch + 1], scale=s1c2[:, cch:cch + 1]),
                             reads=[(pk, tt) for tt in range(4)] + ["s1c2"], writes=[("chT", cch)])
                    HK = [("chT", cch) for cch in range(8)]
                    for ct in range(16):
                        pa, pak = pacc.get()
                        for cch in range(8):
                            S.op("pe", lambda e: e.matmul(pa[:], wG[:, cch, ct * 128:(ct + 1) * 128], hT[:, cch, :], start=(cch == 0), stop=(cch == 7)), reads=[("chT", cch)], writes=[pak])
                        S.op("act", lambda e: e.activation(sg[:, ct, :], pa[:], AF.Sigmoid), reads=[pak], writes=[("csg", ct)])
                    for qq in range(4):
                        cd, cdk = candp.get()
                        for r in range(4):
                            if mode == "C":
                                S.dma("sp", cd[64 * (r % 2):64 * (r % 2) + 64, r // 2, :], exo[r, qq, 0:64, tok0:tok0 + TB], writes=[(cdk, "a", r)])
                                S.dma("sp", cd[:, 2 + 2 * r:4 + 2 * r, :], exo[r, qq, 64:320, tok0:tok0 + TB].rearrange("(p c) t -> c p t", p=2), writes=[(cdk, "r", r)])
                            else:
                                S.dma("sp", cd[64 * (r % 2):64 * (r % 2) + 64, r // 2, :], exg[qq][0].ap()[64 * r:64 * r + 64, tok0:tok0 + TB], writes=[(cdk, "a", r)])
                                for p in range(2):
                                    S.dma("sp", cd[:, 2 + 2 * r + p, :], exg[qq][1 + p].ap()[128 * r:128 * r + 128, tok0:tok0 + TB], writes=[(cdk, "r", r, p)])
                        CK = [(cdk, "a", r) for r in range(4)] + [(cdk, "r", r) for r in range(4)] + [(cdk, "r", r, p) for r in range(4) for p in range(2)]
                        if qq == 0:
                            S.op("dve", lambda e: e.tensor_scalar(oall[:], cd[:], qs_t[:, 0:1], None, ALU.mult), reads=CK + ["qs_t"], writes=["oall"])
                        else:
                            S.op("dve", lambda e: e.scalar_tensor_tensor(oall[:].rearrange("p a t -> p (a t)"), cd[:].rearrange("p a t -> p (a t)"), qs_t[:, qq:qq + 1],
                                                                         oall[:].rearrange("p a t -> p (a t)"), ALU.mult, ALU.add), reads=CK + ["qs_t", "oall"], writes=["oall"])
                    for ct in range(8):
                        pA, pAk = pacc.get()
                        for a in range(2):
                            S.op("pe", lambda e: e.matmul(pA[:], wbr[:, a, ct * 128:(ct + 1) * 128], oall[:, a, :], start=(a == 0), stop=(a == 1)),
                                 reads=["oall"], writes=[pAk])
                        t1, t1k = tmpp.get()
                        S.op("dve", lambda e: e.tensor_tensor(t1[:], pA[:], sg[:, ct, :], ALU.mult), reads=[pAk, ("csg", ct)], writes=[t1k])
                        pR, pRk = pacc.get()
                        for k in range(8):
                            S.op("pe", lambda e: e.matmul(pR[:], wbr[:, 2 + k, ct * 128:(ct + 1) * 128], oall[:, 2 + k, :], start=(k == 0), stop=(k == 7)), reads=["oall"], writes=[pRk])
                        t2, t2k = tmpp.get()
                        S.op("dve", lambda e: e.tensor_tensor(t2[:], pR[:], sg[:, 8 + ct, :], ALU.mult), reads=[pRk, ("csg", 8 + ct)], writes=[t2k])
                        S.op("pool", lambda e: e.tensor_tensor(mT[:, ct, :], t1[:], t2[:], ALU.add), reads=[t1k, t2k], writes=[("mT", ct)])
                    for tt in range(4):
                        T0 = tok0 + tt * 128
                        pm, pmk = pmix.get()
                        for half in range(2):
                            for k in range(8):
                                S.op("pe", lambda e: e.matmul(pm[:, half * 512:(half + 1) * 512], mT[:, k, tt * 128:(tt + 1) * 128], wo[:, k, half * 512:(half + 1) * 512],
                                                              start=(k == 0), stop=(k == 7)), reads=[("mT", k)], writes=[(pmk, half)])
                        st, sk = stat_pool.get()
                        S.op("act", lambda e: e.activation(junk[:], pm[:], AF.Square, accum_out=st[:, 0:1]), reads=[(pmk, 0), (pmk, 1)], writes=["cjunk", sk])
                        rstd_of(st, sk)
                        xt, xk = xt_pool.get()
                        S.dma("sp", xt[:], x_q[T0:T0 + 128, :], writes=[xk])
                        bg, bgk = big.get()
                        S.op("dve", lambda e: e.scalar_tensor_tensor(bg[:], pm[:], st[:, 3:4], G[0][:], ALU.mult, ALU.mult), reads=[(pmk, 0), (pmk, 1), sk, ("G", 0)], writes=[bgk])
                        x1, x1k = x1p.get()
                        S.op("pool", lambda e: e.tensor_tensor(x1[:], bg[:], xt[:], ALU.add), reads=[bgk, xk], writes=[x1k])
                        S.dma("sp", x1_scr[T0:T0 + 128, :], x1[:], reads=[x1k], writes=[("x1_scr", T0)])
                        st2, sk2 = stat_pool.get()
                        S.op("act", lambda e: e.activation(junk[:], x1[:], AF.Square, accum_out=st2[:, 0:1]), reads=[x1k], writes=["cjunk", sk2])
                        rstd_of(st2, sk2)
                        xn2, xn2k = xn2p.get()
                        S.op("dve", lambda e: e.tensor_scalar(xn2[:], x1[:], st2[:, 3:4], None, ALU.mult), reads=[x1k, sk2], writes=[xn2k])
                        p8, p8k = pt8.get()
                        for cch in range(8):
                            S.op("pe", lambda e: e.transpose(p8[:, cch, :], xn2[:, cch * 128:(cch + 1) * 128], identb[:]), reads=[xn2k, "identb"], writes=[(p8k, cch)])
                        h2s, h2k = h2sp.get()
                        for cch in range(8):
                            S.op("act", lambda e: e.activation(h2s[:, cch, :], p8[:, cch, :], AF.Identity, bias=sh2[:, cch:cch + 1], scale=s2c[:, cch:cch + 1]),
                                 reads=[(p8k, cch), "s2c"], writes=[(h2k, cch)])
                        S.dma("sp", h2_scr[:, :, T0:T0 + 128], h2s[:], reads=[(h2k, cch) for cch in range(8)], writes=[("h2_scr", T0)])
                S.barrier()

            with ExitStack() as es2:
                sb, ps = mk_alloc(es2)
                w1 = sb("w1", [128, 8, 4 * D], BF16); w2 = sb("w2", [128, 32, D], BF16)
                with ExitStack() as esw:
                    sbw, psw = mk_alloc(esw)
                    stg = RPool("stg2", [sbw("stgb%d" % i, [128, 2048]) for i in range(3)])
                    load_cast(lambda i: w1[:, i // 2, (i % 2) * 2048:(i % 2 + 1) * 2048], lambda i: w_ff1[(i // 2) * 128:(i // 2 + 1) * 128, (i % 2) * 2048:(i % 2 + 1) * 2048], 16, 2048, stg, "w1")
                    load_cast(lambda i: w2[:, 2 * i:2 * i + 2, :].rearrange("p a n -> p (a n)"), lambda i: w_ff2[i * 256:(i + 1) * 256, :].rearrange("(a p) n -> p a n", p=128), 16, 2048, stg, "w2")
                    S.barrier()
                BL = 256
                h2p = RPool("h2b", [sb("h2b%d" % i, [128, 8, BL], BF16) for i in range(2)])
                sqp = RPool("sq", [sb("sq%d" % i, [128, 32, BL], BF16) for i in range(1)])
                rp = RPool("rl", [sb("rl%d" % i, [128, BL]) for i in range(3)])
                x1p = RPool("dx1", [sb("dx1_%d" % i, [128, D]) for i in range(2)])
                big = RPool("dbig", [sb("dbig%d" % i, [128, D]) for i in range(2)])
                outp = RPool("dout", [sb("dout%d" % i, [128, D]) for i in range(2)])
                junk = sb("djunk", [128, D], BF16)
                stat_pool = RPool("dstat", [sb("dstat%d" % i, [128, 4]) for i in range(4)])
                pf = RPool("pf", [ps("pf%d" % i, [128, BL]) for i in range(3)])
                pmix = RPool("pmx", [ps("pmx%d" % i, [128, D]) for i in range(2)])
                for i in range(3):
                    S.banks[("pf", i)] = "pf%d" % i
                for i in range(2):
                    S.banks[("pmx", i)] = "pmx%d" % i
                OUTK = []
                for blk in range(2048 // BL):
                    b0 = blk * BL
                    h2b, h2bk = h2p.get()
                    S.dma("sp", h2b[:], h2_scr[:, :, b0:b0 + BL], writes=[h2bk])
                    sq, sqk = sqp.get()
                    for ft in range(32):
                        pp_, ppk = pf.get()
                        for k in range(8):
                            S.op("pe", lambda e: e.matmul(pp_[:], w1[:, k, ft * 128:(ft + 1) * 128], h2b[:, k, :], start=(k == 0), stop=(k == 7)), reads=[h2bk], writes=[ppk])
                        rl, rlk = rp.get()
                        S.op("act", lambda e: e.activation(rl[:], pp_[:], AF.Relu), reads=[ppk], writes=[rlk])
                        S.op("pool" if ft % 2 else "dve", lambda e: e.tensor_tensor(sq[:, ft, :], rl[:], rl[:], ALU.mult), reads=[rlk], writes=[(sqk, ft)])
                    for tt in range(BL // 128):
                        T0 = b0 + tt * 128
                        pm, pmk = pmix.get()
                        for half in range(2):
                            for k in range(32):
                                S.op("pe", lambda e: e.matmul(pm[:, half * 512:(half + 1) * 512], sq[:, k, tt * 128:(tt + 1) * 128], w2[:, k, half * 512:(half + 1) * 512],
                                                              start=(k == 0), stop=(k == 31)), reads=[(sqk, k)], writes=[(pmk, half)])
                        st, sk = stat_pool.get()
                        S.op("act", lambda e: e.activation(junk[:], pm[:], AF.Square, accum_out=st[:, 0:1]), reads=[(pmk, 0), (pmk, 1)], writes=["djunk", sk])
                        S.op("dve", lambda e: e.tensor_scalar(st[:, 1:2], st[:, 0:1], 1.0 / D, 1e-6, ALU.mult, ALU.add), reads=[sk], writes=[sk])
                        S.op("act", lambda e: e.activation(st[:, 2:3], st[:, 1:2], AF.Sqrt), reads=[sk], writes=[sk])
                        S.op("dve", lambda e: e.reciprocal(st[:, 3:4], st[:, 2:3]), reads=[sk], writes=[sk])
                        x1, x1k = x1p.get()
                        S.dma("sp", x1[:], x1_scr[T0:T0 + 128, :], writes=[x1k])
                        bg, bgk = big.get()
                        S.op("dve", lambda e: e.scalar_tensor_tensor(bg[:], pm[:], st[:, 3:4], G[1][:], ALU.mult, ALU.mult), reads=[(pmk, 0), (pmk, 1), sk], writes=[bgk])
                        ot, otk = outp.get()
                        S.op("pool", lambda e: e.tensor_tensor(ot[:], bg[:], x1[:], ALU.add), reads=[bgk, x1k], writes=[otk])
                        S.dma("sp", out_d[T0:T0 + 128, :], ot[:], reads=[otk], writes=[("out", T0)])
                        OUTK.append(("out", T0))
                S.final_wait("sp", OUTK)
            keys = S.all_semkeys()
            sems = {k: es_top.enter_context(nc.semaphore(k.replace("_", ""))) for k in keys}
            with nc.Block() as block:
                S.emit(block, sems)
            return nc
        if stage < 99:
            with ExitStack() as esd:
                sbd, psd = mk_alloc(esd)
                db = sbd("dbgbuf", [128, 2048], BF16)
                for qd in range(4):
                    for r0 in (0, 128, 256):
                        n = min(128, 320 - r0)
                        S.dma("sp", db[0:n, :], (ex_out if stage >= 40 else ex_in).ap().rearrange("(q r) t -> q r t", q=4)[qd, r0:r0 + n, :], writes=["db"])
                        S.dma("sp", dbg[qd, r0:r0 + n, :], db[0:n, :], reads=["db"], writes=[("dbg", qd, r0)])
                S.final_wait("sp", [("dbg", qd, r0) for qd in range(4) for r0 in (0, 128, 256)] + [("dbgout", n) for n in DBG])
            keys = S.all_semkeys()
            sems = {k: es_top.enter_context(nc.semaphore(k.replace("_", ""))) for k in keys}
            with nc.Block() as block:
                S.emit(block, sems)
            return nc
    return nc


def core_inputs(inputs, b, j):
    f = lambda a: np.ascontiguousarray(a)
    x = inputs["x"]; L = 0
    m = {}
    m["x_b"] = f(x[b])
    m["x_q"] = f(x[b, 2048 * j:2048 * (j + 1)])
    m["pos_b"] = f(inputs["positions"][b][None, :].astype(np.int32))
    m["c_col"] = f(inputs["c"][b].reshape(8, 128).T)
    m["ada_w"] = f(inputs["ada_w"][L])
    m["ada_bc"] = f(inputs["ada_b"][L].reshape(48, 128).T)
    col = lambda v: v.reshape(8, 128).T
    m["gcols"] = f(np.stack([col(inputs["norm_mix_pre"][L]), col(inputs["norm_ffn_pre"][L])], 1))
    m["grows"] = f(np.concatenate([inputs["norm_mix_post"][L], inputs["norm_ffn_post"][L]])[None, :])
    w_in = inputs["w_in"][L]
    ha = [4 * g + j for g in range(3)]
    qc = lambda h: np.arange(64 * h, 64 * h + 64)
    kc = lambda h: 768 + qc(h)
    vc = lambda h: 1536 + qc(h)
    R0 = 2304
    cols = [qc(ha[0]), qc(ha[1]), qc(ha[2]), vc(ha[2]), kc(ha[0]), kc(ha[1]), kc(ha[2]), kc(ha[2]), vc(ha[0]), vc(ha[1])]
    rw = []
    for sec in range(3):
        rw.append(R0 + sec * 1024 + 256 * j + np.arange(256))
    rw.append(R0 + 3072 + np.arange(256))
    cols = np.concatenate(cols + rw)
    assert cols.shape[0] == NTA * 128
    m["w_in_A"] = f(w_in[:, cols])
    m["w_in_G"] = f(w_in[:, R0 + 3328:R0 + 3328 + 2048])
    mu = inputs["shift_mu"][L][cols[640:] - R0]
    m["mu_A"] = f(mu.reshape(8, 128).T)
    my = 256 * j + np.arange(256)
    m["w2a"] = f(np.concatenate([inputs["decay_w2"][L][:, my], inputs["iclr_a2"][L][:, my]], 0))
    m["g2"] = f(inputs["gate_g2"][L][:, my])
    vecs = [inputs["decay_w0"][L], inputs["iclr_a0"][L], inputs["k_k"][L], inputs["k_a"][L], inputs["r_k"][L].reshape(-1),
            inputs["gn_w"][L], inputs["gn_b"][L], inputs["gn_b"][L]]
    chv = np.stack([v[my].reshape(2, 128) for v in vecs], -1)
    m["chv"] = f(chv.transpose(1, 0, 2))
    qs = np.zeros((128, 4), np.float32); qs[:, j] = 1.0
    m["qsel"] = qs
    m["w_branch"] = f(inputs["w_branch"][L])
    m["w_out"] = f(inputs["w_out"][L])
    m["w_ff1"] = f(inputs["w_ff1"][L])
    m["w_ff2"] = f(inputs["w_ff2"][L])
    for k, v in host_consts().items():
        m["k_" + k] = f(v)
    return m


def kernel(**inputs):
    inputs = {k: np.asarray(v) for k, v in inputs.items()}
    in_maps = [core_inputs(inputs, c // 4, c % 4) for c in range(8)]
    nc = build(99, "all")
    res = run_bass_kernel_spmd(nc, in_maps, core_ids=list(range(8)))
    out = np.zeros((2, SEQ, D), np.float32)
    for c in range(8):
        out[c // 4, 2048 * (c % 4):2048 * (c % 4 + 1)] = res.results[c]["out"]
    return out
```

```python
import os
import types
import numpy as np
from contextlib import ExitStack
SUB = int(os.environ.get('SUB', '99'))
import concourse.bass as bass
import concourse.mybir as mybir
from concourse.bass_utils import run_bass_kernel_spmd

F32 = mybir.dt.float32
BF16 = mybir.dt.bfloat16
I32 = mybir.dt.int32
AF = mybir.ActivationFunctionType
ALU = mybir.AluOpType
AX = mybir.AxisListType

NDSEM = 8
SEQ = 8192
D = 1024
NBLK = 16
TB = 512
MAGIC = 12582912.0
NTA = 13


def freeze(fn):
    if fn is None or fn.__closure__ is None:
        return fn
    cells = []
    for c in fn.__closure__:
        try:
            cells.append(types.CellType(c.cell_contents))
        except ValueError:
            cells.append(c)
    return types.FunctionType(fn.__code__, fn.__globals__, fn.__name__, fn.__defaults__, tuple(cells))


class Sched:
    ENGS = ("pe", "act", "dve", "pool", "sp")

    def __init__(self, nc, same_engine_sync=True):
        self.nc = nc
        self.same = same_engine_sync
        self.ops = {e: [] for e in self.ENGS}
        self.cnt = {e: 0 for e in self.ENGS}
        self.seen = {e: {} for e in self.ENGS}
        self.state = {}
        self.dcnt = {}
        self.maxval = {}
        self.banks = {}

    def _bank(self, key):
        if isinstance(key, tuple):
            if key in self.banks:
                return self.banks[key]
            for sub in key:
                bk = self._bank(sub)
                if bk:
                    return bk
        return None

    def _bankkeys(self, reads, writes):
        out = set()
        for k in list(reads) + list(writes):
            bk = self._bank(k)
            if bk:
                out.add(("BANK", bk))
        return list(out)

    def _need(self, eng, ev, waits, cross_only=False):
        if ev is None:
            return
        key, val, peng = ev
        if peng == eng and key.startswith("e_") and (not self.same or eng == "pe" or cross_only):
            return
        if self.seen[eng].get(key, 0) >= val:
            return
        self.seen[eng][key] = val
        waits[key] = max(waits.get(key, 0), val)

    def _deps(self, eng, reads, writes):
        waits = {}
        for b in self._bankkeys(reads, writes):
            st = self.state.get(b)
            if st is not None:
                self._need(eng, st[0], waits, cross_only=True)
        for b in reads:
            st = self.state.get(b)
            if st is not None:
                self._need(eng, st[0], waits)
        for b in writes:
            st = self.state.get(b)
            if st is not None:
                self._need(eng, st[0], waits)
                for ev in st[1]:
                    self._need(eng, ev, waits)
        return waits

    def _commit(self, ev, reads, writes):
        self.maxval[ev[0]] = (max(self.maxval.get(ev[0], (0, None))[0], ev[1]), ev[2])
        for b in reads:
            st = self.state.setdefault(b, [None, []])
            st[1].append(ev)
            if len(st[1]) > 24:
                d = {}
                for e in st[1]:
                    if e[0] not in d or d[e[0]][1] < e[1]:
                        d[e[0]] = e
                st[1] = list(d.values())
        for b in writes:
            self.state[b] = [ev, []]
        for b in self._bankkeys(reads, writes):
            self.state[b] = [ev, []]

    def op(self, eng, fn, reads=(), writes=()):
        waits = self._deps(eng, reads, writes)
        self.cnt[eng] += 1
        key = "e_" + eng
        ev = (key, self.cnt[eng], eng)
        self.ops[eng].append((list(waits.items()), freeze(fn), (key, 1)))
        self._commit(ev, reads, writes)
        return ev

    def dma(self, q, out, in_, reads=(), writes=(), **kw):
        waits = self._deps(q, reads, writes)
        i = self.dcnt.get(q, 0)
        self.dcnt[q] = i + 1
        key = "d_%s_%d" % (q, i % NDSEM)
        val = 16 * (i // NDSEM + 1)
        if i >= NDSEM:
            self._need(q, (key, val - 16, q), waits)
        ev = (key, val, q)

        def fn(e, out=out, in_=in_, kw=kw):
            return e.dma_start(out=out, in_=in_, **kw)

        self.ops[q].append((list(waits.items()), fn, (key, 16)))
        self._commit(ev, reads, writes)
        return ev

    def custom(self, eng, fn, semkey, inc, reads=(), writes=()):
        waits = self._deps(eng, reads, writes)
        c = self.dcnt.get(semkey, 0) + inc
        self.dcnt[semkey] = c
        ev = (semkey, c, eng)
        self.ops[eng].append((list(waits.items()), freeze(fn), (semkey, inc)))
        self._commit(ev, reads, writes)
        return ev

    def barrier(self):
        for e in self.ENGS:
            waits = {}
            for key, (val, peng) in self.maxval.items():
                self._need(e, (key, val, peng), waits)
            if waits:
                self.ops[e].append((list(waits.items()), None, None))
        self.state = {}

    def final_wait(self, eng, bufs):
        waits = {}
        for b in bufs:
            st = self.state.get(b)
            if st is not None:
                self._need(eng, st[0], waits)
        self.ops[eng].append((list(waits.items()), None, None))

    def all_semkeys(self):
        keys = set()
        for e in self.ENGS:
            for waits, fn, inc in self.ops[e]:
                for k, _ in waits:
                    keys.add(k)
                if inc is not None:
                    keys.add(inc[0])
        return sorted(keys)

    def emit(self, block, sems):
        def mk(eng_name):
            def body(e):
                for waits, fn, inc in self.ops[eng_name]:
                    for k, v in waits:
                        e.wait_ge(sems[k], v)
                    if fn is None:
                        continue
                    ins = fn(e)
                    ins.then_inc(sems[inc[0]], inc[1])
            return body

        block.tensor(mk("pe"))
        block.scalar(mk("act"))
        block.vector(mk("dve"))
        block.gpsimd(mk("pool"))
        block.sync(mk("sp"))


class RPool:
    def __init__(self, name, tiles):
        self.name, self.tiles, self.i = name, tiles, 0

    def get(self):
        k = self.i % len(self.tiles)
        self.i += 1
        return self.tiles[k], (self.name, k)


def host_consts():
    p = np.arange(128)
    c = {}
    c["ident"] = np.eye(128, dtype=np.float32)
    r = (p % 64)[:, None]
    n = np.arange(64)[None, :]
    su = (n > r).astype(np.float32)
    ui = (n >= r).astype(np.float32)
    sl = (n < r).astype(np.float32)
    c["mask1"] = np.concatenate([su, ui], 1)
    c["masksl"] = sl
    c["identh"] = (n == r).astype(np.float32)
    c["blockones"] = ((p[:, None] // 64) == (p[None, :] // 64)).astype(np.float32)
    perm = np.zeros((128, 128), np.float32)
    for b in (0, 64):
        for i in range(8):
            perm[b + 8 + i, b + i] = -1.0
            perm[b + i, b + 8 + i] = 1.0
    c["perm"] = perm
    inv = 500000.0 ** (-(np.arange(8, dtype=np.float32)) * 2.0 / 16.0)
    inv2pi = np.zeros((128, 1), np.float32)
    for b in (0, 64):
        for i in range(16):
            inv2pi[b + i, 0] = np.float32(inv[i % 8])
    c["inv2pi"] = inv2pi
    seg = np.ones((128, 512), np.float32)
    seg[:, ::64] = 0.0
    c["segmask"] = seg
    qi = np.arange(128)[:, None]
    kj = np.arange(256)[None, :]
    lag = qi + 128 - kj
    band = (lag >= 0) & (lag <= 128)
    c["amask"] = np.where(band, 0.0, -1e30).astype(np.float32)
    c["amask0"] = np.where(band & (kj >= 128), 0.0, -1e30).astype(np.float32)
    return c


CONST_SHAPES = {"ident": [128, 128], "mask1": [128, 128], "masksl": [128, 64], "identh": [128, 64],
                "blockones": [128, 128], "perm": [128, 128], "inv2pi": [128, 1], "segmask": [128, 512],
                }
ACONST_SHAPES = {"amask": [128, 256], "amask0": [128, 256]}


def build(stage=99, mode="all"):
    nc = bass.Bass("TRN2", target_bir_lowering=False)

    def din(name, shape, dt=F32):
        return nc.dram_tensor(name, shape, dt, kind="ExternalInput").ap()

    x_b = din("x_b", [SEQ, D])
    x_q = din("x_q", [2048, D])
    pos_b = din("pos_b", [1, SEQ], I32)
    c_col = din("c_col", [128, 8])
    ada_w = din("ada_w", [D, 6 * D])
    ada_bc = din("ada_bc", [128, 48])
    gcols = din("gcols", [128, 2, 8])
    grows = din("grows", [1, 2 * D])
    w_in_A = din("w_in_A", [D, NTA * 128])
    w_in_G = din("w_in_G", [D, 2048])
    mu_A = din("mu_A", [128, 8])
    w2a = din("w2a", [128, 256])
    g2 = din("g2", [128, 256])
    qsel = din("qsel", [128, 4])
    chv = din("chv", [128, 2, 8])
    w_branch = din("w_branch", [1280, D])
    w_out = din("w_out", [D, D])
    w_ff1 = din("w_ff1", [D, 4 * D])
    w_ff2 = din("w_ff2", [4 * D, D])
    cin = {k: din("k_" + k, v) for k, v in list(CONST_SHAPES.items()) + list(ACONST_SHAPES.items())}
    out_d = nc.dram_tensor("out", [2048, D], F32, kind="ExternalOutput").ap()
    dbg = None
    if stage < 99:
        dbg = nc.dram_tensor("dbg", [4, 320, 2048], BF16, kind="ExternalOutput").ap()
    qkv_scr = nc.dram_tensor("qkv_scr", [128, 5, SEQ], BF16).ap()
    ex_in = nc.dram_tensor("ex_in", [1280, 2048], BF16, **({"kind": "ExternalOutput"} if mode == "A" else {}))
    ex_out = nc.dram_tensor("ex_out", [4 * 1280, 2048], BF16, **({"kind": "ExternalInput"} if mode == "C" else {}))
    h2_scr = nc.dram_tensor("h2_scr", [128, 8, 2048], BF16).ap()
    exg = [[nc.dram_tensor("exg%d_%d" % (qq, i), [4 * (64 if i == 0 else 128), 2048], BF16) for i in range(3)] for qq in range(4)]
    att_scr = nc.dram_tensor("att_scr", [3, SEQ, 66], F32).ap()
    x1_scr = nc.dram_tensor("x1_scr", [2048, D], F32).ap()

    S = Sched(nc)
    global LAST_SCHED
    LAST_SCHED = S
    es_top = ExitStack()
    DBG = {}

    def dd(name, ap, shape, dt, rk):
        if stage >= 99 or name in DBG:
            return
        t = nc.dram_tensor("dbg_" + name, shape, dt, kind="ExternalOutput").ap()
        DBG[name] = t
        S.dma("sp", t, ap, reads=rk, writes=[("dbgout", name)])
    with es_top:
        def mk_alloc(es):
            def sb(name, shape, dt=F32):
                return es.enter_context(nc.sbuf_tensor(name, shape, dt))

            def ps(name, shape, dt=F32):
                return es.enter_context(nc.psum_tensor(name, shape, dt))
            return sb, ps

        sbT, psT = mk_alloc(es_top)
        K = {}
        for k, shp in CONST_SHAPES.items():
            K[k] = sbT("c_" + k, shp)
            S.dma("sp", K[k][:], cin[k], writes=["c_" + k])
        identb = sbT("identb", [128, 128], BF16)
        S.op("dve", lambda e: e.tensor_copy(identb[:], K["ident"][:]), reads=["c_ident"], writes=["identb"])
        ccol = sbT("ccol", [128, 8])
        scol = sbT("scol", [128, 8])
        S.dma("sp", ccol[:], c_col, writes=["ccol"])
        S.op("act", lambda e: e.activation(scol[:], ccol[:], AF.Silu), reads=["ccol"], writes=["scol"])
        adab = sbT("adab", [128, 48])
        S.dma("sp", adab[:], ada_bc, writes=["adab"])
        gcol = sbT("gcol", [128, 2, 8])
        S.dma("sp", gcol[:], gcols, writes=["gcol"])
        modc = sbT("modc", [128, 48])

        def compute_mod(sec, sbl, psl, tagp):
            slab = sbl("adaslab" + tagp, [128, 8, 1024])
            for cch in range(8):
                S.dma("sp", slab[:, cch, :], ada_w[cch * 128:(cch + 1) * 128, sec * 1024:(sec + 1) * 1024],
                      writes=[("slab", tagp, cch)])
            pm = psl("pmod" + tagp, [128, 8])
            S.banks[("pmod", tagp)] = "pmod" + tagp
            for t in range(8):
                for cch in range(8):
                    S.op("pe", lambda e, t=t, cch=cch: e.matmul(pm[:, t:t + 1], slab[:, cch, t * 128:(t + 1) * 128],
                                                               scol[:, cch:cch + 1], start=(cch == 0), stop=(cch == 7)),
                         reads=[("slab", tagp, cch), "scol"], writes=[(("pmod", tagp), t)])
            S.op("dve", lambda e: e.tensor_tensor(modc[:, 8 * sec:8 * sec + 8], pm[:], adab[:, 8 * sec:8 * sec + 8], ALU.add),
                 reads=[(("pmod", tagp), t) for t in range(8)] + ["adab"], writes=[("modc", sec)])

        with ExitStack() as esA:
            sb, ps = mk_alloc(esA)
            with ExitStack() as es0:
                sb0, ps0 = mk_alloc(es0)
                compute_mod(0, sb0, ps0, "a")
                compute_mod(1, sb0, ps0, "b")
                S.barrier()
            s1c = sb("s1c", [128, 8])
            S.op("dve", lambda e: e.scalar_tensor_tensor(s1c[:], modc[:, 8:16], 1.0, gcol[:, 0, :], ALU.add, ALU.mult),
                 reads=[("modc", 1), "gcol"], writes=["s1c"])
            sh1 = modc[:, 0:8]

            wA = sb("wA", [128, 8, NTA * 128], BF16)
            with ExitStack() as esw:
                sbw, psw = mk_alloc(esw)
                wstage = RPool("wstage", [sbw("wstg%d" % i, [128, NTA * 128]) for i in range(2)])
                for cch in range(8):
                    t_, k_ = wstage.get()
                    S.dma("sp", t_[:], w_in_A[cch * 128:(cch + 1) * 128, :], writes=[k_])
                    S.op("pool" if cch % 2 else "act",
                         (lambda e, t_=t_, cch=cch: e.tensor_copy(wA[:, cch, :], t_[:])) if cch % 2 else
                         (lambda e, t_=t_, cch=cch: e.activation(wA[:, cch, :], t_[:], AF.Copy)),
                         reads=[k_], writes=[("wA", cch)])
                S.barrier()
            muA = sb("muA", [128, 8])
            S.dma("sp", muA[:], mu_A, writes=["muA"])
            w2a_f = sb("w2a_f", [128, 256]); g2_f = sb("g2_f", [128, 256])
            w2a_b = sb("w2a_b", [128, 256], BF16); g2_b = sb("g2_b", [128, 256], BF16)
            S.dma("sp", w2a_f[:], w2a, writes=["w2a_f"]); S.dma("sp", g2_f[:], g2, writes=["g2_f"])
            S.op("dve", lambda e: e.tensor_copy(w2a_b[:], w2a_f[:]), reads=["w2a_f"], writes=["w2a_b"])
            S.op("dve", lambda e: e.tensor_copy(g2_b[:], g2_f[:]), reads=["g2_f"], writes=["g2_b"])
            chvt = sb("chvt", [128, 2, 8])
            S.dma("sp", chvt[:], chv, writes=["chvt"])
            omka = sb("omka", [128, 2])
            S.op("dve", lambda e: e.tensor_scalar(omka[:], chvt[:, :, 3], -1.0, 1.0, ALU.mult, ALU.add), reads=["chvt"], writes=["omka"])
            bones_b = sb("bones_b", [128, 128], BF16); perm_b = sb("perm_b", [128, 128], BF16)
            S.op("dve", lambda e: e.tensor_copy(bones_b[:], K["blockones"][:]), reads=["c_blockones"], writes=["bones_b"])
            S.op("dve", lambda e: e.tensor_copy(perm_b[:], K["perm"][:]), reads=["c_perm"], writes=["perm_b"])
            mask1x4 = sb("mask1x4", [128, 4, 128]); maskslx4 = sb("maskslx4", [128, 4, 64]); identhx4 = sb("identhx4", [128, 4, 64])
            for ch in range(4):
                S.op("pool", lambda e, ch=ch: e.tensor_copy(mask1x4[:, ch, :], K["mask1"][:]), reads=["c_mask1"], writes=[("m1", ch)])
                S.op("pool", lambda e, ch=ch: e.tensor_copy(maskslx4[:, ch, :], K["masksl"][:]), reads=["c_masksl"], writes=[("msl", ch)])
                S.op("pool", lambda e, ch=ch: e.tensor_copy(identhx4[:, ch, :], K["identh"][:]), reads=["c_identh"], writes=[("idh", ch)])
            MK1 = [("m1", ch) for ch in range(4)]; MKSL = [("msl", ch) for ch in range(4)]; IDH = [("idh", ch) for ch in range(4)]

            xt_pool = RPool("xt", [sb("xt%d" % i, [128, D]) for i in range(2)])
            xn_pool = RPool("xn", [sb("xn%d" % i, [128, D], BF16) for i in range(4)])
            junk = sb("junk", [128, D], BF16)
            stat_pool = RPool("stat", [sb("stat%d" % i, [128, 4]) for i in range(8)])
            hT_pool = RPool("hT", [sb("hT%d" % i, [128, 8, TB], BF16) for i in range(1)])
            raw = [sb("raw%d" % i, [128, TB + 1]) for i in range(8)]
            for i in range(8):
                S.op("pool", lambda e, i=i: e.memset(raw[i][:], 0.0), writes=[("raw", i)])
            qst_pool = RPool("qst", [sb("qst%d" % i, [128, 5, TB], BF16) for i in range(1)])
            f32p = RPool("f", [sb("f%d" % i, [128, TB]) for i in range(22)])
            b16p = RPool("b", [sb("b%d" % i, [128, TB], BF16) for i in range(10)])
            posi = sb("posi", [128, TB], I32)
            AR = [RPool("AR%d" % p, [sb("AR%d_%d" % (p, i), [128, 8, 128], BF16) for i in range(1)]) for p in range(2)]
            KB = [RPool("KB%d" % p, [sb("KB%d_%d" % (p, i), [128, 8, 128], BF16) for i in range(1)]) for p in range(2)]
            gkp = RPool("Gk", [sb("Gk%d" % i, [128, 4, 128], BF16) for i in range(2)])
            gbp = RPool("Gb", [sb("Gb%d" % i, [128, 4, 128], BF16) for i in range(2)])
            glp = RPool("GL", [sb("GL%d" % i, [128, 4, 64], BF16) for i in range(2)])
            lmp = RPool("LM", [sb("LM%d" % i, [128, 2, 4, 64], BF16) for i in range(6)])
            xp = RPool("X", [sb("X%d" % i, [128, 4, 64], BF16) for i in range(6)])
            tmp_ = RPool("TM", [sb("TM%d" % i, [128, 4, 4, 64], BF16) for i in range(2)])
            zsp = RPool("Zs", [sb("Zs%d" % i, [128, 4, 128], BF16) for i in range(2)])
            aup = RPool("AU", [sb("AU%d" % i, [128, 4, 128], BF16) for i in range(2)])
            mctp = RPool("McT", [sb("McT%d" % i, [128, 4, 64], BF16) for i in range(2)])
            ncwp = RPool("NcW", [sb("NcW%d" % i, [128, 4, 64]) for i in range(2)])
            rhtp = RPool("RhT", [sb("RhT%d" % i, [128, 4, 64], BF16) for i in range(2)])
            Sb = [RPool("Sb%d" % p, [sb("Sb%d_%d" % (p, i), [128, 64], BF16) for i in range(3)]) for p in range(2)]
            ytm = [RPool("Ytm%d" % p, [sb("Ytm%d_%d" % (p, i), [128, 8, 64]) for i in range(1)]) for p in range(2)]
            ysq = sb("ysq", [128, 8, 64])
            ynp = RPool("yn", [sb("yn%d" % i, [128, 8, 64], BF16) for i in range(2)])
            gst = RPool("gst", [sb("gst%d" % i, [128, 8]) for i in range(8)])
            orw_pool = RPool("orw", [sb("orw%d" % i, [128, 2, TB], BF16) for i in range(2)])
            pp = RPool("pp", [ps("pp%d" % i, [128, TB]) for i in range(2)])
            ptx = RPool("ptx", [ps("ptx", [128, 4, 128], BF16)])
            ptb = RPool("ptb", [ps("ptb%d" % i, [128, 1024], BF16) for i in range(1)])
            pg = RPool("pg", [ps("pg%d" % i, [128, TB]) for i in range(3)])
            pys_t = ps("pys", [128, 8, 64])
            pys = RPool("pys", [pys_t[:, i, :] for i in range(8)])
            for i in range(2):
                S.banks[("pp", i)] = "pp%d" % i
            S.banks[("ptx", 0)] = "ptx"
            S.banks[("ptb", 0)] = "ptb"
            for i in range(3):
                S.banks[("pg", i)] = "pg%d" % i
            for i in range(8):
                S.banks[("pys", i)] = "pys"

            sstate = []
            for p in range(2):
                t_, k_ = Sb[p].get()
                S.op("pool", lambda e, t_=t_: e.memset(t_[:], 0.0), writes=[k_])
                sstate.append((t_, k_))

            exv = ex_in.ap().rearrange("(q r) t -> q r t", q=4)

            for tb in range((NBLK if stage >= 50 else 4) if mode != "C" else 0):
                if stage == 9:
                    break
                tok0 = tb * TB
                xns = []
                for tt in range(4):
                    xt, xk = xt_pool.get()
                    S.dma("sp", xt[:], x_b[tok0 + tt * 128: tok0 + (tt + 1) * 128, :], writes=[xk])
                    st, sk = stat_pool.get()
                    S.op("act", lambda e, xt=xt, st=st: e.activation(junk[:], xt[:], AF.Square, accum_out=st[:, 0:1]),
                         reads=[xk], writes=["junk", sk])
                    if SUB <= 1:
                        continue
                    S.op("dve", lambda e, st=st: e.tensor_scalar(st[:, 1:2], st[:, 0:1], 1.0 / D, 1e-6, ALU.mult, ALU.add), reads=[sk], writes=[sk])
                    S.op("act", lambda e, st=st: e.activation(st[:, 2:3], st[:, 1:2], AF.Sqrt), reads=[sk], writes=[sk])
                    S.op("dve", lambda e, st=st: e.reciprocal(st[:, 3:4], st[:, 2:3]), reads=[sk], writes=[sk])
                    if SUB <= 2:
                        continue
                    xn, nk = xn_pool.get()
                    S.op("dve", lambda e, xn=xn, xt=xt, st=st: e.tensor_scalar(xn[:], xt[:], st[:, 3:4], None, ALU.mult),
                         reads=[xk, sk], writes=[nk])
                    xns.append((xn, nk))
                if SUB <= 3:
                    continue
                hT, hk = hT_pool.get()
                for cch in range(8):
                    pt, pk = ptx.get()
                    for tt in range(4):
                        xn, nk = xns[tt]
                        S.op("pe", lambda e, pt=pt, xn=xn, tt=tt, cch=cch: e.transpose(pt[:, tt, :], xn[:, cch * 128:(cch + 1) * 128], identb[:]),
                             reads=[nk, "identb"], writes=[(pk, tt)])
                    if SUB <= 4:
                        continue
                    S.op("act", lambda e, pt=pt, hT=hT, cch=cch: e.activation(hT[:, cch, :], pt[:].rearrange("p a b -> p (a b)"), AF.Identity,
                                                                          bias=sh1[:, cch:cch + 1], scale=s1c[:, cch:cch + 1]),
                         reads=[(pk, tt) for tt in range(4)] + ["s1c", ("modc", 0)], writes=[(hk, cch)])
                HK = [(hk, cch) for cch in range(8)]
                dd("hT", hT[:, 0, :], [128, TB], BF16, HK)

                def proj(tile):
                    pt, pk = pp.get()
                    for cch in range(8):
                        S.op("pe", lambda e, pt=pt, cch=cch, tile=tile: e.matmul(pt[:], wA[:, cch, tile * 128:(tile + 1) * 128], hT[:, cch, :],
                                                                                start=(cch == 0), stop=(cch == 7)),
                             reads=[("wA", cch), (hk, cch)], writes=[pk])
                    return pt, pk

                if stage == 10:
                    continue
                S.dma("sp", posi[:], pos_b[:, tok0:tok0 + TB].partition_broadcast(128), writes=["posi"])
                posf, pfk = f32p.get()
                S.op("dve", lambda e, posf=posf: e.tensor_copy(posf[:], posi[:]), reads=["posi"], writes=[pfk])
                ysn, ysk = f32p.get()
                S.op("dve", lambda e, ysn=ysn, posf=posf: e.tensor_scalar(ysn[:], posf[:], K["inv2pi"][:, 0:1], float(1.0 / (2 * np.pi)), ALU.mult, ALU.mult),
                     reads=[pfk, "c_inv2pi"], writes=[ysk])
                ycs, yck = f32p.get()
                S.op("pool", lambda e, ycs=ycs, ysn=ysn: e.tensor_scalar(ycs[:], ysn[:], 0.25, None, ALU.add), reads=[ysk], writes=[yck])
                tabs = []
                for (yy, yk) in ((ysn, ysk), (ycs, yck)):
                    kk_, kkk = f32p.get()
                    S.op("dve", lambda e, kk_=kk_, yy=yy: e.tensor_scalar(kk_[:], yy[:], MAGIC, MAGIC, ALU.add, ALU.subtract), reads=[yk], writes=[kkk])
                    S.op("dve", lambda e, kk_=kk_, yy=yy: e.tensor_tensor(kk_[:], yy[:], kk_[:], ALU.subtract), reads=[yk, kkk], writes=[kkk])
                    tb_, tk_ = f32p.get()
                    S.op("act", lambda e, tb_=tb_, kk_=kk_: e.activation(tb_[:], kk_[:], AF.Sin, scale=float(2 * np.pi)), reads=[kkk], writes=[tk_])
                    tabs.append((tb_, tk_))
                (SS, ssk), (CC, cck) = tabs

                qst, qk = qst_pool.get()
                for tile in range(5):
                    pt, pk = proj(tile)
                    if tile == 4:
                        S.op("act", lambda e, pt=pt, tile=tile: e.activation(qst[:, tile, :], pt[:], AF.Copy), reads=[pk], writes=[(qk, tile)])
                        continue
                    qb, qbk = b16p.get()
                    S.op("act", lambda e, pt=pt, qb=qb: e.activation(qb[:], pt[:], AF.Copy), reads=[pk], writes=[qbk])
                    nrow = 128 if tile in (0, 2) else 64
                    pq, pqk = pg.get()
                    S.op("pe", lambda e, pq=pq, qb=qb: e.matmul(pq[:], perm_b[:], qb[:], start=True, stop=True), reads=["perm_b", qbk], writes=[pqk])
                    t1, t1k = f32p.get()
                    S.op("dve", lambda e, t1=t1, qb=qb, nrow=nrow: e.tensor_tensor(t1[0:nrow, :], qb[0:nrow, :], CC[0:nrow, :], ALU.mult), reads=[qbk, cck], writes=[t1k])
                    t2, t2k = f32p.get()
                    S.op("dve", lambda e, t2=t2, pq=pq, nrow=nrow: e.tensor_tensor(t2[0:nrow, :], pq[0:nrow, :], SS[0:nrow, :], ALU.mult), reads=[pqk, ssk], writes=[t2k])
                    S.op("pool", lambda e, t1=t1, t2=t2, nrow=nrow, tile=tile: e.tensor_tensor(qst[0:nrow, tile, :], t1[0:nrow, :], t2[0:nrow, :], ALU.add),
                         reads=[t1k, t2k], writes=[(qk, tile)])
                    if nrow == 64:
                        S.op("pool", lambda e, qb=qb, tile=tile: e.tensor_copy(qst[64:128, tile, :], qb[64:128, :]), reads=[qbk], writes=[(qk, tile, "b")])
                S.dma("sp", qkv_scr[:, :, tok0:tok0 + TB], qst[:], reads=[(qk, t) for t in range(5)] + [(qk, 1, "b"), (qk, 3, "b")], writes=[("qkv_scr", tb)])

                if stage == 11:
                    continue
                for i in range(8):
                    pt, pk = proj(5 + i)
                    S.op("pool", lambda e, i=i: e.tensor_copy(raw[i][:, 0:1], raw[i][:, TB:TB + 1]), reads=[("raw", i)], writes=[("raw", i)])
                    S.op("act", lambda e, i=i, pt=pt: e.activation(raw[i][:, 1:TB + 1], pt[:], AF.Copy), reads=[pk], writes=[("raw", i)])

                def shift(i, eng="dve"):
                    d_, dk = f32p.get()
                    S.op("pool", lambda e, d_=d_, i=i: e.tensor_tensor(d_[:], raw[i][:, 0:TB], raw[i][:, 1:TB + 1], ALU.subtract), reads=[("raw", i)], writes=[dk])
                    o_, ok = f32p.get()
                    S.op("dve", lambda e, d_=d_, o_=o_, i=i: e.scalar_tensor_tensor(o_[:], d_[:], muA[:, i:i + 1], raw[i][:, 1:TB + 1], ALU.mult, ALU.add),
                         reads=[dk, ("raw", i), "muA"], writes=[ok])
                    return o_, ok

                lora, lok = shift(6)
                dgs, dgk = shift(7)
                tdw, tdk = b16p.get()
                S.op("act", lambda e, tdw=tdw, lora=lora: e.activation(tdw[0:64, :], lora[0:64, :], AF.Tanh), reads=[lok], writes=[(tdk, 0)])
                S.op("act", lambda e, tdw=tdw, lora=lora: e.activation(tdw[64:128, :], lora[64:128, :], AF.Copy), reads=[lok], writes=[(tdk, 1)])
                sg, sgk = b16p.get()
                S.op("act", lambda e, sg=sg, dgs=dgs: e.activation(sg[:], dgs[:], AF.Sigmoid), reads=[dgk], writes=[sgk])
                dd("lora", lora[:], [128, TB], F32, [lok])
                dd("tdw", tdw[:], [128, TB], BF16, [(tdk, 0), (tdk, 1)])
                dd("w2a_b", w2a_b[:], [128, 256], BF16, ["w2a_b"])

                orw, ork = orw_pool.get()
                for p in range(2):
                    cv = lambda j, p=p: chvt[:, p, j:j + 1]
                    r_s, rk_ = shift(0 + p)
                    k_s, kk_ = shift(2 + p)
                    v_s, vk_ = shift(4 + p)
                    dd("raw0", raw[0][:, 0:TB], [128, TB], F32, [("raw", 0)])
                    dd("r_s", r_s[:], [128, TB], F32, [rk_])
                    dd("k_s", k_s[:], [128, TB], F32, [kk_])
                    pz, pzk = pg.get()
                    S.op("pe", lambda e, pz=pz, p=p: e.matmul(pz[:], w2a_b[0:64, p * 128:(p + 1) * 128], tdw[0:64, :], start=True, stop=True), reads=["w2a_b", (tdk, 0)], writes=[pzk])
                    lw, lwk = f32p.get()
                    S.op("act", lambda e, lw=lw, pz=pz, cv=cv: e.activation(lw[:], pz[:], AF.Sigmoid, bias=cv(0)), reads=[pzk, "chvt"], writes=[lwk])
                    S.op("pool", lambda e, lw=lw: e.tensor_scalar(lw[:], lw[:], float(-np.exp(-0.5)), None, ALU.mult), reads=[lwk], writes=[lwk])
                    pa, pak = pg.get()
                    S.op("pe", lambda e, pa=pa, p=p: e.matmul(pa[:], w2a_b[64:128, p * 128:(p + 1) * 128], tdw[64:128, :], start=True, stop=True), reads=["w2a_b", (tdk, 1)], writes=[pak])
                    icl, ick = f32p.get()
                    S.op("act", lambda e, icl=icl, pa=pa, cv=cv: e.activation(icl[:], pa[:], AF.Sigmoid, bias=cv(1)), reads=[pak, "chvt"], writes=[ick])
                    cw, cwk = f32p.get()
                    S.op("dve", lambda e, cw=cw, lw=lw: e.tensor_tensor_scan(cw[:], K["segmask"][:], lw[:], 0.0, ALU.mult, ALU.add), reads=["c_segmask", lwk], writes=[cwk])
                    cwe, cwek = f32p.get()
                    S.op("pool", lambda e, cwe=cwe, cw=cw, lw=lw: e.tensor_tensor(cwe[:], cw[:], lw[:], ALU.subtract), reads=[cwk, lwk], writes=[cwek])
                    Winc, wik = f32p.get(); Winv, wvk = f32p.get(); Wexc, wek = f32p.get()
                    S.op("act", lambda e, Winc=Winc, cw=cw: e.activation(Winc[:], cw[:], AF.Exp), reads=[cwk], writes=[wik])
                    S.op("act", lambda e, Winv=Winv, cw=cw: e.activation(Winv[:], cw[:], AF.Exp, scale=-1.0), reads=[cwk], writes=[wvk])
                    S.op("act", lambda e, Wexc=Wexc, cwe=cwe: e.activation(Wexc[:], cwe[:], AF.Exp), reads=[cwek], writes=[wek])
                    dd("lw", lw[:], [128, TB], F32, [lwk])
                    dd("icl", icl[:], [128, TB], F32, [ick])
                    dd("cw", cw[:], [128, TB], F32, [cwk])
                    dd("Winc", Winc[:], [128, TB], F32, [wik])
                    dd("Winv", Winv[:], [128, TB], F32, [wvk])
                    kk, kkk = f32p.get()
                    S.op("pool", lambda e, kk=kk, k_s=k_s, cv=cv: e.tensor_scalar(kk[:], k_s[:], cv(2), None, ALU.mult), reads=[kk_, "chvt"], writes=[kkk])
                    kk2, kk2k = b16p.get()
                    S.op("pool", lambda e, kk2=kk2, kk=kk: e.tensor_tensor(kk2[:], kk[:], kk[:], ALU.mult), reads=[kkk], writes=[kk2k])
                    pn, pnk = pg.get()
                    S.op("pe", lambda e, pn=pn, kk2=kk2: e.matmul(pn[:], bones_b[:], kk2[:], start=True, stop=True), reads=["bones_b", kk2k], writes=[pnk])
                    rn, rnk = f32p.get()
                    S.op("act", lambda e, rn=rn, pn=pn: e.activation(rn[:], pn[:], AF.Sqrt), reads=[pnk], writes=[rnk])
                    S.op("dve", lambda e, rn=rn: e.tensor_scalar(rn[:], rn[:], 1e-12, None, ALU.max), reads=[rnk], writes=[rnk])
                    S.op("dve", lambda e, rn=rn: e.reciprocal(rn[:], rn[:]), reads=[rnk], writes=[rnk])
                    S.op("dve", lambda e, kk=kk, rn=rn: e.tensor_tensor(kk[:], kk[:], rn[:], ALU.mult), reads=[kkk, rnk], writes=[kkk])
                    km, kmk = f32p.get()
                    S.op("dve", lambda e, km=km, icl=icl, cv=cv, p=p: e.tensor_scalar(km[:], icl[:], cv(3), omka[:, p:p + 1], ALU.mult, ALU.add), reads=[ick, "chvt", "omka"], writes=[kmk])
                    S.op("pool", lambda e, km=km, k_s=k_s: e.tensor_tensor(km[:], km[:], k_s[:], ALU.mult), reads=[kmk, kk_], writes=[kmk])
                    ar, ark = AR[p].get(); kb, kbk = KB[p].get()
                    v3 = lambda t: t[:].rearrange("p (c t) -> p c t", t=64)
                    S.op("dve", lambda e, ar=ar, r_s=r_s, Winc=Winc, v3=v3: e.tensor_tensor(ar[:, :, 64:128], v3(r_s), v3(Winc), ALU.mult), reads=[rk_, wik], writes=[(ark, "r")])
                    S.op("dve", lambda e, ar=ar, kk=kk, Wexc=Wexc, v3=v3: e.scalar_tensor_tensor(ar[:, :, 0:64], v3(kk), -1.0, v3(Wexc), ALU.mult, ALU.mult), reads=[kkk, wek], writes=[(ark, "a")])
                    S.op("pool", lambda e, kb=kb, km=km, Winv=Winv, v3=v3: e.tensor_tensor(kb[:, :, 0:64], v3(km), v3(Winv), ALU.mult), reads=[kmk, wvk], writes=[(kbk, "k")])
                    bt, btk = f32p.get()
                    S.op("pool", lambda e, bt=bt, kk=kk, icl=icl: e.tensor_tensor(bt[:], kk[:], icl[:], ALU.mult), reads=[kkk, ick], writes=[btk])
                    S.op("pool", lambda e, kb=kb, bt=bt, Winv=Winv, v3=v3: e.tensor_tensor(kb[:, :, 64:128], v3(bt), v3(Winv), ALU.mult), reads=[btk, wvk], writes=[(kbk, "b")])
                    ARK = [(ark, "r"), (ark, "a")]; KBK = [(kbk, "k"), (kbk, "b")]
                    dd("kk", kk[:], [128, TB], F32, [kkk])
                    dd("km", km[:], [128, TB], F32, [kmk])
                    dd("AR", ar[:], [128, 8, 128], BF16, ARK)
                    dd("KB", kb[:], [128, 8, 128], BF16, KBK)
                    vb, vbk = b16p.get()
                    S.op("act", lambda e, vb=vb, v_s=v_s: e.activation(vb[:], v_s[:], AF.Copy), reads=[vk_], writes=[vbk])
                    rkb, rkbk = b16p.get()
                    S.op("dve", lambda e, rkb=rkb, r_s=r_s, km=km, cv=cv: e.scalar_tensor_tensor(rkb[:], r_s[:], cv(4), km[:], ALU.mult, ALU.mult), reads=[rk_, kmk, "chvt"], writes=[rkbk])
                    pb, pbk = pg.get()
                    S.op("pe", lambda e, pb=pb, rkb=rkb: e.matmul(pb[:], bones_b[:], rkb[:], start=True, stop=True), reads=["bones_b", rkbk], writes=[pbk])
                    bon, bonk = f32p.get()
                    S.op("dve", lambda e, bon=bon, pb=pb, v_s=v_s: e.tensor_tensor(bon[:], pb[:], v_s[:], ALU.mult), reads=[pbk, vk_], writes=[bonk])

                    if stage == 12:
                        continue
                    yt, ytk = ytm[p].get()
                    def cg_pipe(cg):
                        c0 = cg * 4
                        def gstage(lsel, rsel_ar, width, maskt, maskk, pool_, lhs_from_kb=True):
                            pgt, pgk = pg.get()
                            pv = pgt[:].rearrange("p (c w) -> p c w", w=128)[:, :, 0:width] if width == 128 else pgt[:, 0:256].rearrange("p (c w) -> p c w", w=64)
                            for ch in range(4):
                                for hb in (0, 64):
                                    if lhs_from_kb:
                                        S.op("pe", lambda e, pv=pv, ch=ch, hb=hb, lsel=lsel: e.matmul(pv[hb:hb + 64, ch, :], kb[hb:hb + 64, c0 + ch, lsel], ar[hb:hb + 64, c0 + ch, :], start=True, stop=True),
                                             reads=ARK + KBK, writes=[(pgk, ch, hb)])
                                    else:
                                        S.op("pe", lambda e, pv=pv, ch=ch, hb=hb: e.matmul(pv[hb:hb + 64, ch, :], ar[hb:hb + 64, c0 + ch, 0:64], kb[hb:hb + 64, c0 + ch, 64:128], start=True, stop=True),
                                             reads=ARK + KBK, writes=[(pgk, ch, hb)])
                            gt, gk_ = pool_.get()
                            S.op("dve", lambda e, gt=gt, pv=pv, maskt=maskt: e.tensor_tensor(gt[:], pv, maskt[:], ALU.mult),
                                 reads=[(pgk, ch, hb) for ch in range(4) for hb in (0, 64)] + maskk, writes=[gk_])
                            return gt, gk_
                        Gk, gkk = gstage(slice(0, 64), None, 128, mask1x4, MK1, gkp)
                        Gb, gbk = gstage(slice(64, 128), None, 128, mask1x4, MK1, gbp)
                        GL, glk = gstage(None, None, 64, maskslx4, MKSL, glp, lhs_from_kb=False)
                        dd("Gk", Gk[:], [128, 4, 128], BF16, [gkk])
                        dd("Gb", Gb[:], [128, 4, 128], BF16, [gbk])
                        dd("GL", GL[:], [128, 4, 64], BF16, [glk])
                        yield
                        X, xk_ = xp.get()
                        S.op("pool", lambda e, X=X, Gb=Gb: e.tensor_tensor(X[:], Gb[:, :, 0:64], identhx4[:], ALU.add), reads=[gbk] + IDH, writes=[xk_])
                        Lk, Lkk = freeze(lambda ch, hb: GL[hb:hb + 64, ch, :]), glk
                        Mk, Mkk = freeze(lambda ch, hb: Gb[hb:hb + 64, ch, 0:64]), gbk
                        for lev in range(1, 6):
                            yield
                            plt, plk = pg.get()
                            plv = plt[:].rearrange("p (a c w) -> p a c w", a=2, w=64)
                            for ch in range(4):
                                for hb in (0, 64):
                                    S.op("pe", lambda e, plv=plv, ch=ch, hb=hb, Lk=Lk, Mk=Mk: e.matmul(plv[hb:hb + 64, 0, ch, :], Mk(ch, hb), Lk(ch, hb), start=True, stop=True),
                                         reads=[Lkk, Mkk], writes=[(plk, 0, ch, hb)])
                                    if lev < 5:
                                        S.op("pe", lambda e, plv=plv, ch=ch, hb=hb, Lk=Lk, Mk=Mk: e.matmul(plv[hb:hb + 64, 1, ch, :], Lk(ch, hb), Mk(ch, hb), start=True, stop=True),
                                             reads=[Lkk, Mkk], writes=[(plk, 1, ch, hb)])
                            lm, lmk = lmp.get()
                            na = 2 if lev < 5 else 1
                            S.op("act", lambda e, lm=lm, plv=plv, na=na: e.activation(lm[:, 0:na, :, :], plv[:, 0:na, :, :], AF.Copy),
                                 reads=[(plk, a, ch, hb) for a in range(na) for ch in range(4) for hb in (0, 64)], writes=[lmk])
                            Lk, Lkk = (lambda ch, hb, lm=lm: lm[hb:hb + 64, 0, ch, :]), lmk
                            Mk, Mkk = (lambda ch, hb, lm=lm: lm[hb:hb + 64, 1, ch, :]), lmk
                            pxt, pxk = pg.get()
                            pxv = pxt[:, 0:256].rearrange("p (c w) -> p c w", w=64)
                            for ch in range(4):
                                for hb in (0, 64):
                                    S.op("pe", lambda e, pxv=pxv, ch=ch, hb=hb, Lk=Lk, X=X: e.matmul(pxv[hb:hb + 64, ch, :], Lk(ch, hb), X[hb:hb + 64, ch, :], start=True, stop=True),
                                         reads=[Lkk, xk_], writes=[(pxk, ch, hb)])
                            Xn, xnk = xp.get()
                            S.op("dve", lambda e, Xn=Xn, pxv=pxv, X=X: e.tensor_tensor(Xn[:], pxv, X[:], ALU.add),
                                 reads=[(pxk, ch, hb) for ch in range(4) for hb in (0, 64)] + [xk_], writes=[xnk])
                            X, xk_ = Xn, xnk
                        dd("X", X[:], [128, 4, 64], BF16, [xk_])
                        yield
                        if stage == 13:
                            return
                        ptt, ptk = ptb.get()
                        ptv = ptt[:].rearrange("p (c k w) -> p c k w", k=4, w=64)
                        for ch in range(4):
                            c = c0 + ch
                            srcs = [(freeze(lambda hb, c=c: vb[hb:hb + 64, c * 64:(c + 1) * 64]), [vbk]),
                                    (freeze(lambda hb, c=c: ar[hb:hb + 64, c, 0:64]), ARK),
                                    (freeze(lambda hb, c=c: kb[hb:hb + 64, c, 0:64]), KBK),
                                    (freeze(lambda hb, c=c: kb[hb:hb + 64, c, 64:128]), KBK)]
                            for kind, (sf, sk_) in enumerate(srcs):
                                for hb in (0, 64):
                                    S.op("pe", lambda e, ptv=ptv, ch=ch, kind=kind, hb=hb, sf=sf: e.transpose(ptv[hb:hb + 64, ch, kind, :], sf(hb), identb[hb:hb + 64, hb:hb + 64]),
                                         reads=sk_ + ["identb"], writes=[(ptk, ch, kind, hb)])
                        TM, tmk = tmp_.get()
                        S.op("act", lambda e, TM=TM, ptv=ptv: e.activation(TM[:], ptv, AF.Copy),
                             reads=[(ptk, ch, kind, hb) for ch in range(4) for kind in range(4) for hb in (0, 64)], writes=[tmk])
                        yield
                        pzt, pzk2 = pg.get()
                        pzv = pzt[:, 0:256].rearrange("p (c w) -> p c w", w=64)
                        for ch in range(4):
                            for hb in (0, 64):
                                S.op("pe", lambda e, pzv=pzv, ch=ch, hb=hb, Gk=Gk, TM=TM: e.matmul(pzv[hb:hb + 64, ch, :], Gk[hb:hb + 64, ch, 0:64], TM[hb:hb + 64, ch, 0, :], start=True, stop=True),
                                     reads=[gkk, tmk], writes=[(pzk2, ch, hb)])
                        Zs, zsk = zsp.get()
                        S.op("pool", lambda e, Zs=Zs, TM=TM: e.tensor_copy(Zs[:, :, 0:64], TM[:, :, 1, :]), reads=[tmk], writes=[(zsk, 0)])
                        S.op("act", lambda e, Zs=Zs, pzv=pzv: e.activation(Zs[:, :, 64:128], pzv, AF.Copy),
                             reads=[(pzk2, ch, hb) for ch in range(4) for hb in (0, 64)], writes=[(zsk, 1)])
                        yield
                        pat, pauk = pg.get()
                        pav = pat[:].rearrange("p (c w) -> p c w", w=128)
                        for ch in range(4):
                            for hb in (0, 64):
                                S.op("pe", lambda e, pav=pav, ch=ch, hb=hb, X=X, Zs=Zs: e.matmul(pav[hb:hb + 64, ch, :], X[hb:hb + 64, ch, :], Zs[hb:hb + 64, ch, :], start=True, stop=True),
                                     reads=[xk_, (zsk, 0), (zsk, 1)], writes=[(pauk, ch, hb)])
                        AU, auk = aup.get()
                        S.op("dve", lambda e, AU=AU, pav=pav: e.tensor_copy(AU[:], pav), reads=[(pauk, ch, hb) for ch in range(4) for hb in (0, 64)], writes=[auk])
                        dd("TM", TM[:], [128, 4, 4, 64], BF16, [tmk])
                        dd("AU", AU[:], [128, 4, 128], BF16, [auk])
                        yield
                        pmt, pmk = pg.get()
                        pmv = pmt[:].rearrange("p (c w) -> p c w", w=128)
                        for ch in range(4):
                            for hb in (0, 64):
                                S.op("pe", lambda e, pmv=pmv, ch=ch, hb=hb, AU=AU, TM=TM: e.matmul(pmv[hb:hb + 64, ch, 0:64], AU[hb:hb + 64, ch, 0:64], TM[hb:hb + 64, ch, 3, :], start=True, stop=True),
                                     reads=[auk, tmk], writes=[(pmk, ch, hb, 0)])
                                S.op("pe", lambda e, pmv=pmv, ch=ch, hb=hb, AU=AU, TM=TM: e.matmul(pmv[hb:hb + 64, ch, 64:128], TM[hb:hb + 64, ch, 3, :], AU[hb:hb + 64, ch, 64:128], start=True, stop=False),
                                     reads=[auk, tmk], writes=[(pmk, ch, hb, 1)])
                                S.op("pe", lambda e, pmv=pmv, ch=ch, hb=hb, TM=TM: e.matmul(pmv[hb:hb + 64, ch, 64:128], TM[hb:hb + 64, ch, 2, :], TM[hb:hb + 64, ch, 0, :], start=False, stop=True),
                                     reads=[tmk], writes=[(pmk, ch, hb, 1)])
                        McT, mck = mctp.get()
                        S.op("dve", lambda e, McT=McT, pmv=pmv: e.tensor_tensor(McT[:], pmv[:, :, 0:64], identhx4[:], ALU.add),
                             reads=[(pmk, ch, hb, 0) for ch in range(4) for hb in (0, 64)] + IDH, writes=[mck])
                        NcW, nck = ncwp.get()
                        for ch in range(4):
                            wc = Winc[:, (c0 + ch) * 64 + 63:(c0 + ch) * 64 + 64]
                            S.op("dve", lambda e, NcW=NcW, pmv=pmv, ch=ch, wc=wc: e.tensor_scalar(NcW[:, ch, :], pmv[:, ch, 64:128], wc, None, ALU.mult),
                                 reads=[(pmk, ch, hb, 1) for hb in (0, 64)] + [wik], writes=[(nck, ch)])
                        yield
                        prt, prk = pg.get()
                        prv = prt[:, 0:256].rearrange("p (c w) -> p c w", w=64)
                        for ch in range(4):
                            for hb in (0, 64):
                                S.op("pe", lambda e, prv=prv, ch=ch, hb=hb, AU=AU, Gb=Gb: e.matmul(prv[hb:hb + 64, ch, :], AU[hb:hb + 64, ch, 0:64], Gb[hb:hb + 64, ch, 64:128], start=True, stop=True),
                                     reads=[auk, gbk], writes=[(prk, ch, hb)])
                        RhT, rhk = rhtp.get()
                        S.op("dve", lambda e, RhT=RhT, prv=prv: e.tensor_tensor(RhT[:], prv, ar[:, c0:c0 + 4, 64:128], ALU.add),
                             reads=[(prk, ch, hb) for ch in range(4) for hb in (0, 64)] + ARK, writes=[rhk])
                        dd("McT", McT[:], [128, 4, 64], BF16, [mck])
                        dd("NcW", NcW[:], [128, 4, 64], F32, [(nck, ch) for ch in range(4)])
                        dd("RhT", RhT[:], [128, 4, 64], BF16, [rhk])
                        yield
                        for ch in range(4):
                            c = c0 + ch
                            s0, s0k = sstate[p]
                            py, pyk = pys.get()
                            for hb in (0, 64):
                                S.op("pe", lambda e, py=py, hb=hb, ch=ch, Gb=Gb, AU=AU: e.matmul(py[hb:hb + 64, :], Gb[hb:hb + 64, ch, 64:128], AU[hb:hb + 64, ch, 64:128], start=True, stop=False),
                                     reads=[gbk, auk], writes=[(pyk, hb)])
                                S.op("pe", lambda e, py=py, hb=hb, ch=ch, Gk=Gk, TM=TM: e.matmul(py[hb:hb + 64, :], Gk[hb:hb + 64, ch, 64:128], TM[hb:hb + 64, ch, 0, :], start=False, stop=False),
                                     reads=[gkk, tmk], writes=[(pyk, hb)])
                                S.op("pe", lambda e, py=py, hb=hb, ch=ch, RhT=RhT, s0=s0: e.matmul(py[hb:hb + 64, :], RhT[hb:hb + 64, ch, :], s0[hb:hb + 64, :], start=False, stop=True),
                                     reads=[rhk, s0k], writes=[(pyk, hb)])
                            S.op("act", lambda e, py=py, yt=yt, c=c: e.activation(yt[:, c, :], py, AF.Copy), reads=[(pyk, 0), (pyk, 64)], writes=[(ytk, c)])
                            psn, psk = pys.get()
                            for hb in (0, 64):
                                S.op("pe", lambda e, psn=psn, hb=hb, ch=ch, McT=McT, s0=s0: e.matmul(psn[hb:hb + 64, :], McT[hb:hb + 64, ch, :], s0[hb:hb + 64, :], start=True, stop=True),
                                     reads=[mck, s0k], writes=[(psk, hb)])
                            s1_, s1k = Sb[p].get()
                            wc = Winc[:, c * 64 + 63:c * 64 + 64]
                            S.op("dve", lambda e, s1_=s1_, psn=psn, wc=wc, NcW=NcW, ch=ch: e.scalar_tensor_tensor(s1_[:], psn, wc, NcW[:, ch, :], ALU.mult, ALU.add),
                                 reads=[(psk, 0), (psk, 64), wik, (nck, ch)], writes=[s1k])
                            sstate[p] = (s1_, s1k)
                    pipes = [cg_pipe(0), cg_pipe(1)]
                    while pipes:
                        for g_ in list(pipes):
                            try:
                                next(g_)
                            except StopIteration:
                                pipes.remove(g_)
                    if stage == 13:
                        continue
                    YK = [(ytk, c) for c in range(8)]
                    dd("yt", yt[:], [128, 8, 64], F32, YK)
                    dd("Sb", sstate[p][0][:], [128, 64], BF16, [sstate[p][1]])
                    g0, g0k = gst.get(); g1, g1k = gst.get(); g2_, g2k = gst.get(); g3, g3k = gst.get()
                    S.op("dve", lambda e, g0=g0, yt=yt: e.tensor_reduce(g0[:], yt[:], AX.X, ALU.add), reads=YK, writes=[g0k])
                    S.op("pool", lambda e, yt=yt: e.tensor_tensor(ysq[:], yt[:], yt[:], ALU.mult), reads=YK, writes=["ysq"])
                    S.op("dve", lambda e, g1=g1: e.tensor_reduce(g1[:], ysq[:], AX.X, ALU.add), reads=["ysq"], writes=[g1k])
                    S.op("dve", lambda e, g0=g0: e.tensor_scalar(g0[:], g0[:], 1.0 / 64, None, ALU.mult), reads=[g0k], writes=[g0k])
                    S.op("dve", lambda e, g2_=g2_, g0=g0: e.tensor_tensor(g2_[:], g0[:], g0[:], ALU.mult), reads=[g0k], writes=[g2k])
                    S.op("dve", lambda e, g1=g1, g2_=g2_: e.scalar_tensor_tensor(g1[:], g1[:], 1.0 / 64, g2_[:], ALU.mult, ALU.subtract), reads=[g1k, g2k], writes=[g1k])
                    S.op("dve", lambda e, g1=g1: e.tensor_scalar(g1[:], g1[:], 64e-5, None, ALU.add), reads=[g1k], writes=[g1k])
                    S.op("act", lambda e, g3=g3, g1=g1: e.activation(g3[:], g1[:], AF.Sqrt), reads=[g1k], writes=[g3k])
                    S.op("dve", lambda e, g3=g3: e.reciprocal(g3[:], g3[:]), reads=[g3k], writes=[g3k])
                    yn, ynk = ynp.get()
                    for c in range(8):
                        S.op("dve" if c % 2 else "pool", lambda e, yn=yn, yt=yt, c=c, g0=g0, g3=g3: e.tensor_scalar(yn[:, c, :], yt[:, c, :], g0[:, c:c + 1], g3[:, c:c + 1], ALU.subtract, ALU.mult),
                             reads=[(ytk, c), g0k, g3k], writes=[(ynk, c)])
                    pt2, pt2k = ptb.get()
                    p2v = pt2[:, 0:512].rearrange("p (c w) -> p c w", w=64)
                    for c in range(8):
                        for hb in (0, 64):
                            S.op("pe", lambda e, p2v=p2v, c=c, hb=hb, yn=yn: e.transpose(p2v[hb:hb + 64, c, :], yn[hb:hb + 64, c, :], identb[hb:hb + 64, hb:hb + 64]),
                                 reads=[(ynk, c), "identb"], writes=[(pt2k, c, hb)])
                    o1, o1k = f32p.get()
                    S.op("act", lambda e, o1=o1, pt2=pt2, cv=cv: e.activation(o1[:], pt2[:, 0:512], AF.Identity, bias=cv(6), scale=cv(5)),
                         reads=[(pt2k, c, hb) for c in range(8) for hb in (0, 64)] + ["chvt"], writes=[o1k])
                    S.op("pool", lambda e, o1=o1, bon=bon: e.tensor_tensor(o1[:], o1[:], bon[:], ALU.add), reads=[o1k, bonk], writes=[o1k])
                    dd("yn", yn[:], [128, 8, 64], BF16, [(ynk, c) for c in range(8)])
                    dd("o1", o1[:], [128, TB], F32, [o1k])
                    dd("bon", bon[:], [128, TB], F32, [bonk])
                    pgg, pggk = pg.get()
                    S.op("pe", lambda e, pgg=pgg, p=p: e.matmul(pgg[:], g2_b[:, p * 128:(p + 1) * 128], sg[:], start=True, stop=True), reads=["g2_b", sgk], writes=[pggk])
                    S.op("dve", lambda e, p=p, pgg=pgg, o1=o1: e.tensor_tensor(orw[:, p, :], pgg[:], o1[:], ALU.mult), reads=[pggk, o1k], writes=[(ork, p)])
                if stage in (12, 13):
                    continue
                qd = tb // 4
                tq = (tb % 4) * TB
                S.dma("sp", exv[qd, 64:320, tq:tq + TB].rearrange("(p c) t -> c p t", p=2), orw[:], reads=[(ork, 0), (ork, 1)], writes=[("ex_in", tb)])
            S.barrier()


        if stage >= 30 and mode != "C":
          with ExitStack() as esB:
            sb, ps = mk_alloc(esB)
            exv = ex_in.ap().rearrange("(q r) t -> q r t", q=4)
            qkv = sb("qkv", [128, 5, SEQ], BF16)
            for t5 in range(5):
                for hf in range(4):
                    S.dma("sp", qkv[:, t5, hf * 2048:(hf + 1) * 2048], qkv_scr[:, t5, hf * 2048:(hf + 1) * 2048], writes=[("qkv", t5, hf)])
            QKV = [("qkv", t5, hf) for t5 in range(5) for hf in range(4)]
            am = sb("am", [128, 256]); am0 = sb("am0", [128, 256])
            S.dma("sp", am[:], cin["amask"], writes=["am"]); S.dma("sp", am0[:], cin["amask0"], writes=["am0"])
            ps_s = RPool("ps_s", [ps("ps_s%d" % i, [128, 256]) for i in range(2)])
            ps_pt = RPool("ps_pt", [ps("ps_pt", [128, 2, 128], BF16)])
            ps_v = RPool("ps_v", [ps("ps_v", [128, 64], BF16)])
            ps_o = RPool("ps_o", [ps("ps_o%d" % i, [128, 64]) for i in range(2)])
            ps_m = RPool("ps_m", [ps("ps_m", [64, 4, 128], BF16)])
            for i in range(2):
                S.banks[("ps_s", i)] = "ps_s%d" % i
                S.banks[("ps_o", i)] = "ps_o%d" % i
            S.banks[("ps_pt", 0)] = "ps_pt"; S.banks[("ps_v", 0)] = "ps_v"; S.banks[("ps_m", 0)] = "ps_m"
            smp = RPool("sm", [sb("sm%d" % i, [128, 256]) for i in range(3)])
            pbp = RPool("pb", [sb("pb%d" % i, [128, 256], BF16) for i in range(3)])
            ptp = RPool("ptp", [sb("ptp%d" % i, [128, 2, 128], BF16) for i in range(3)])
            vtp = RPool("vt", [sb("vt%d" % i, [128, 64], BF16) for i in range(4)])
            stp = RPool("st66", [sb("st66_%d" % i, [128, 66]) for i in range(4)])
            sts = RPool("sts", [sb("sts%d" % i, [128, 2]) for i in range(4)])
            heads = [(0, 0, 2, 0, 4, 0), (0, 64, 2, 64, 4, 64), (1, 0, 3, 0, 1, 64)]
            for g, dil in enumerate((1, 4, 16)):
                qt_, qb_, kt_, kb_, vt_, vb_ = heads[g]
                nb = SEQ // (128 * dil)
                for r in range(dil):
                    vprev = None
                    for n in range(nb):
                        q0 = 128 * n * dil + r
                        k0 = 128 * (n - 1) * dil + r if n > 0 else q0
                        qs = qkv[qb_:qb_ + 64, qt_, q0:q0 + 127 * dil + 1:dil]
                        if n > 0:
                            ks = qkv[kb_:kb_ + 64, kt_, k0:k0 + 255 * dil + 1:dil]
                        pS, pSk = ps_s.get()
                        if n > 0:
                            S.op("pe", lambda e: e.matmul(pS[:], qs, ks, start=True, stop=True), reads=QKV, writes=[pSk])
                        else:
                            kc = qkv[kb_:kb_ + 64, kt_, q0:q0 + 127 * dil + 1:dil]
                            S.op("pe", lambda e: e.matmul(pS[:, 0:128], qs, kc, start=True, stop=True), reads=QKV, writes=[(pSk, 0)])
                            S.op("pe", lambda e: e.matmul(pS[:, 128:256], qs, kc, start=True, stop=True), reads=QKV, writes=[(pSk, 1)])
                        mk_ = am if n > 0 else am0
                        sm, smk = smp.get()
                        S.op("dve", lambda e: e.tensor_tensor(sm[:], pS[:], mk_[:], ALU.add), reads=[pSk, (pSk, 0), (pSk, 1), "am", "am0"], writes=[smk])
                        st, stk = stp.get()
                        ss_, ssk_ = sts.get()
                        S.op("dve", lambda e: e.tensor_reduce(st[:, 64:65], sm[:], AX.X, ALU.max), reads=[smk], writes=[(stk, "m")])
                        S.op("dve", lambda e: e.tensor_scalar(ss_[:, 0:1], st[:, 64:65], -0.125, None, ALU.mult), reads=[(stk, "m")], writes=[ssk_])
                        pb, pbk = pbp.get()
                        S.op("act", lambda e: e.activation(pb[:], sm[:], AF.Exp, bias=ss_[:, 0:1], scale=0.125, accum_out=st[:, 65:66]), reads=[smk, ssk_], writes=[pbk, (stk, "l")])
                        ppt, pptk = ps_pt.get()
                        for hf in range(2):
                            S.op("pe", lambda e, hf=hf: e.transpose(ppt[:, hf, :], pb[:, hf * 128:(hf + 1) * 128], identb[:]), reads=[pbk, "identb"], writes=[(pptk, hf)])
                        pts, ptsk = ptp.get()
                        S.op("act", lambda e: e.activation(pts[:], ppt[:], AF.Copy), reads=[(pptk, 0), (pptk, 1)], writes=[ptsk])
                        pv_, pvk = ps_v.get()
                        vsl = qkv[vb_:vb_ + 64, vt_, q0:q0 + 127 * dil + 1:dil]
                        S.op("pe", lambda e: e.transpose(pv_[:], vsl, identb[vb_:vb_ + 64, vb_:vb_ + 64]), reads=QKV + ["identb"], writes=[pvk])
                        vcur, vck = vtp.get()
                        S.op("dve", lambda e: e.tensor_copy(vcur[:], pv_[:]), reads=[pvk], writes=[vck])
                        if vprev is None:
                            vprev = (vcur, vck)
                        vp, vpk = vprev
                        po, pok = ps_o.get()
                        S.op("pe", lambda e: e.matmul(po[:], pts[:, 0, :], vp[:], start=True, stop=False), reads=[ptsk, vpk], writes=[pok])
                        S.op("pe", lambda e: e.matmul(po[:], pts[:, 1, :], vcur[:], start=False, stop=True), reads=[ptsk, vck], writes=[pok])
                        S.op("dve", lambda e: e.tensor_copy(st[:, 0:64], po[:]), reads=[pok], writes=[(stk, "o")])
                        S.dma("sp", att_scr[g, q0:q0 + 127 * dil + 1:dil, :], st[:], reads=[(stk, "o"), (stk, "m"), (stk, "l")], writes=[("att_scr", g, r, n)])
                        vprev = (vcur, vck)
            S.barrier()
            mgp = RPool("mg", [sb("mg%d" % i, [128, 3, 66]) for i in range(3)])
            mws = RPool("mw", [sb("mw%d" % i, [128, 8]) for i in range(4)])
            mo = RPool("mo", [sb("mo%d" % i, [128, 64]) for i in range(3)])
            mob = RPool("mob", [sb("mob%d" % i, [128, 64], BF16) for i in range(3)])
            oat = RPool("oat", [sb("oat%d" % i, [64, 4, 128], BF16) for i in range(2)])
            for tb in range(NBLK):
                pm_, pmk_ = ps_m.get()
                for tt in range(4):
                    T0 = tb * TB + tt * 128
                    mg, mgk = mgp.get()
                    S.dma("sp", mg[:], att_scr[:, T0:T0 + 128, :].rearrange("g t c -> t g c"), writes=[mgk])
                    w, wk = mws.get()
                    S.op("dve", lambda e: e.tensor_reduce(w[:, 3:4], mg[:, :, 64], AX.X, ALU.max), reads=[mgk], writes=[wk])
                    S.op("dve", lambda e: e.tensor_scalar(w[:, 3:4], w[:, 3:4], -0.125, None, ALU.mult), reads=[wk], writes=[wk])
                    S.op("act", lambda e: e.activation(w[:, 0:3], mg[:, :, 64], AF.Exp, bias=w[:, 3:4], scale=0.125), reads=[mgk, wk], writes=[wk])
                    S.op("dve", lambda e: e.tensor_tensor(w[:, 4:7], w[:, 0:3], mg[:, :, 65], ALU.mult), reads=[wk, mgk], writes=[wk])
                    S.op("dve", lambda e: e.tensor_reduce(w[:, 7:8], w[:, 4:7], AX.X, ALU.add), reads=[wk], writes=[wk])
                    S.op("dve", lambda e: e.reciprocal(w[:, 7:8], w[:, 7:8]), reads=[wk], writes=[wk])
                    o_, ok_ = mo.get()
                    S.op("dve", lambda e: e.tensor_scalar(o_[:], mg[:, 0, 0:64], w[:, 0:1], None, ALU.mult), reads=[mgk, wk], writes=[ok_])
                    S.op("dve", lambda e: e.scalar_tensor_tensor(o_[:], mg[:, 1, 0:64], w[:, 1:2], o_[:], ALU.mult, ALU.add), reads=[mgk, wk, ok_], writes=[ok_])
                    S.op("dve", lambda e: e.scalar_tensor_tensor(o_[:], mg[:, 2, 0:64], w[:, 2:3], o_[:], ALU.mult, ALU.add), reads=[mgk, wk, ok_], writes=[ok_])
                    ob, obk = mob.get()
                    S.op("dve", lambda e: e.tensor_scalar(ob[:], o_[:], w[:, 7:8], None, ALU.mult), reads=[ok_, wk], writes=[obk])
                    S.op("pe", lambda e, tt=tt: e.transpose(pm_[:, tt, :], ob[:], identb[:]), reads=[obk, "identb"], writes=[(pmk_, tt)])
                oa, oak = oat.get()
                S.op("act", lambda e: e.activation(oa[:], pm_[:], AF.Copy), reads=[(pmk_, tt) for tt in range(4)], writes=[oak])
                qd = tb // 4
                tq = (tb % 4) * TB
                S.dma("sp", exv[qd, 0:64, tq:tq + TB], oa[:].rearrange("p a b -> p (a b)"), reads=[oak], writes=[("ex_in_a", tb)])
            S.barrier()

        if mode == "A":
            keys = S.all_semkeys()
            sems = {k: es_top.enter_context(nc.semaphore(k.replace("_", ""))) for k in keys}
            with nc.Block() as block:
                S.emit(block, sems)
            return nc
        if stage >= 40 and mode == "all":
            for qq in range(4):
                for i, (ra, rb) in enumerate(((0, 64), (64, 192), (192, 320))):
                    S.custom("pool", lambda e: e.collective_compute("AllGather", ALU.bypass, replica_groups=[[0, 1, 2, 3], [4, 5, 6, 7]],
                                                                    ins=[ex_in.ap()[320 * qq + ra:320 * qq + rb, :]], outs=[exg[qq][i].ap()]),
                             "cc", 1, reads=[], writes=[("exg", qq, i)])
            S.barrier()

        if stage >= 50 and mode != "A":
          with ExitStack() as esC:
            sbC, psC = mk_alloc(esC)
            for (sa, sb_) in ((2, 3), (4, 5)):
                with ExitStack() as es0:
                    sb0, ps0 = mk_alloc(es0)
                    compute_mod(sa, sb0, ps0, "m%d" % sa)
                    compute_mod(sb_, sb0, ps0, "m%d" % sb_)
                    S.barrier()
            s1c2 = sbC("s1c2", [128, 8]); s2c = sbC("s2c", [128, 8])
            S.op("dve", lambda e: e.scalar_tensor_tensor(s1c2[:], modc[:, 8:16], 1.0, gcol[:, 0, :], ALU.add, ALU.mult), writes=["s1c2"])
            S.op("dve", lambda e: e.scalar_tensor_tensor(s2c[:], modc[:, 32:40], 1.0, gcol[:, 1, :], ALU.add, ALU.mult), writes=["s2c"])
            sh1 = modc[:, 0:8]; sh2 = modc[:, 24:32]
            G = [sbC("G1", [128, D]), sbC("G2", [128, D])]
            with ExitStack() as es0:
                sb0, ps0 = mk_alloc(es0)
                ones = sb0("ones", [128, 128])
                S.op("pool", lambda e: e.memset(ones[:], 1.0), writes=["ones"])
                grow = sb0("grow", [128, 2 * D])
                S.dma("sp", grow[:], grows.partition_broadcast(128), writes=["grow"])
                crp = RPool("cr", [sb0("cr%d" % i, [128, 128]) for i in range(2)])
                pbc = ps0("pbc", [128, D])
                S.banks[("pbc",)] = "pbc"
                for gi, sec in enumerate((2, 5)):
                    for c in range(8):
                        cr, crk = crp.get()
                        S.op("dve", lambda e: e.tensor_scalar(cr[:], ones[:], modc[:, sec * 8 + c:sec * 8 + c + 1], None, ALU.mult), reads=["ones"], writes=[crk])
                        S.op("pe", lambda e: e.matmul(pbc[:, c * 128:(c + 1) * 128], cr[:], K["ident"][:], start=True, stop=True), reads=[crk], writes=[(("pbc",), c)])
                    S.op("dve", lambda e: e.tensor_tensor(G[gi][:], pbc[:], grow[:, gi * D:(gi + 1) * D], ALU.mult),
                         reads=[(("pbc",), c) for c in range(8)] + ["grow"], writes=[("G", gi)])
                S.barrier()

            def load_cast(dst_fn, src_fn, n, width, stg, tag):
                for i in range(n):
                    t_, k_ = stg.get()
                    S.dma("sp", t_[:, 0:width], src_fn(i), writes=[k_])
                    if i % 2:
                        S.op("pool", lambda e: e.tensor_copy(dst_fn(i), t_[:, 0:width]), reads=[k_], writes=[(tag, i)])
                    else:
                        S.op("act", lambda e: e.activation(dst_fn(i), t_[:, 0:width], AF.Copy), reads=[k_], writes=[(tag, i)])

            with ExitStack() as es1:
                sb, ps = mk_alloc(es1)
                stg = RPool("stg", [sb("stg%d" % i, [128, 2048]) for i in range(2)])
                wG = sb("wG", [128, 8, 2048], BF16); wbr = sb("wbr", [128, 10, D], BF16); wo = sb("wo", [128, 8, D], BF16)
                load_cast(lambda i: wG[:, i, :], lambda i: w_in_G[i * 128:(i + 1) * 128, :], 8, 2048, stg, "wG")
                load_cast(lambda i: wbr[:, i, :], lambda i: w_branch[i * 128:(i + 1) * 128, :], 10, D, stg, "wbr")
                load_cast(lambda i: wo[:, i, :], lambda i: w_out[i * 128:(i + 1) * 128, :], 8, D, stg, "wo")
                S.barrier()
                xt_pool = RPool("cxt", [sb("cxt%d" % i, [128, D]) for i in range(2)])
                xn_pool = RPool("cxn", [sb("cxn%d" % i, [128, D], BF16) for i in range(4)])
                junk = sb("cjunk", [128, D], BF16)
                stat_pool = RPool("cstat", [sb("cstat%d" % i, [128, 4]) for i in range(8)])
                hT = sb("chT", [128, 8, TB], BF16)
                sg = sb("csg", [128, 16, TB], BF16)
                oall = sb("oall", [128, 10, TB], BF16)
                candp = RPool("cand", [sb("cand%d" % i, [128, 10, TB], BF16) for i in range(2)])
                qs_t = sb("qs_t", [128, 4])
                S.dma("sp", qs_t[:], qsel, writes=["qs_t"])
                mT = sb("mT", [128, 8, TB], BF16)
                tmpp = RPool("ctmp", [sb("ctmp%d" % i, [128, TB]) for i in range(3)])
                big = RPool("cbig", [sb("cbig%d" % i, [128, D]) for i in range(2)])
                x1p = RPool("cx1", [sb("cx1_%d" % i, [128, D]) for i in range(2)])
                xn2p = RPool("cxn2", [sb("cxn2_%d" % i, [128, D], BF16) for i in range(2)])
                h2sp = RPool("ch2s", [sb("ch2s%d" % i, [128, 8, 128], BF16) for i in range(2)])
                pacc = RPool("pacc", [ps("pacc%d" % i, [128, TB]) for i in range(3)])
                pmix = RPool("pmix", [ps("pmix", [128, D])])
                ptx = RPool("cptx", [ps("cptx", [128, 4, 128], BF16)])
                pt8 = RPool("cpt8", [ps("cpt8", [128, 8, 128], BF16)])
                for i in range(3):
                    S.banks[("pacc", i)] = "pacc%d" % i
                S.banks[("pmix", 0)] = "pmix"; S.banks[("cptx", 0)] = "cptx"; S.banks[("cpt8", 0)] = "cpt8"
                exo = ex_out.ap().rearrange("(r q c) t -> r q c t", r=4, q=4)

                def rstd_of(st, sk):
                    S.op("dve", lambda e: e.tensor_scalar(st[:, 1:2], st[:, 0:1], 1.0 / D, 1e-6, ALU.mult, ALU.add), reads=[sk], writes=[sk])
                    S.op("act", lambda e: e.activation(st[:, 2:3], st[:, 1:2], AF.Sqrt), reads=[sk], writes=[sk])
                    S.op("dve", lambda e: e.reciprocal(st[:, 3:4], st[:, 2:3]), reads=[sk], writes=[sk])

                for tb in range(4):
                    tok0 = tb * TB
                    xns = []
                    for tt in range(4):
                        xt, xk = xt_pool.get()
                        S.dma("sp", xt[:], x_q[tok0 + tt * 128: tok0 + (tt + 1) * 128, :], writes=[xk])
                        st, sk = stat_pool.get()
                        S.op("act", lambda e: e.activation(junk[:], xt[:], AF.Square, accum_out=st[:, 0:1]), reads=[xk], writes=["cjunk", sk])
                        rstd_of(st, sk)
                        xn, nk = xn_pool.get()
                        S.op("dve", lambda e: e.tensor_scalar(xn[:], xt[:], st[:, 3:4], None, ALU.mult), reads=[xk, sk], writes=[nk])
                        xns.append((xn, nk))
                    for cch in range(8):
                        pt, pk = ptx.get()
                        for tt in range(4):
                            xn, nk = xns[tt]
                            S.op("pe", lambda e: e.transpose(pt[:, tt, :], xn[:, cch * 128:(cch + 1) * 128], identb[:]), reads=[nk, "identb"], writes=[(pk, tt)])
                        S.op("act", lambda e: e.activation(hT[:, cch, :], pt[:].rearrange("p a b -> p (a b)"), AF.Identity, bias=sh1[:, cch:cch + 1], scale=s1c2[:, cch:cch + 1]),
                             reads=[(pk, tt) for tt in range(4)] + ["s1c2"], writes=[("chT", cch)])
                    HK = [("chT", cch) for cch in range(8)]
                    for ct in range(16):
                        pa, pak = pacc.get()
                        for cch in range(8):
                            S.op("pe", lambda e: e.matmul(pa[:], wG[:, cch, ct * 128:(ct + 1) * 128], hT[:, cch, :], start=(cch == 0), stop=(cch == 7)), reads=[("chT", cch)], writes=[pak])
                        S.op("act", lambda e: e.activation(sg[:, ct, :], pa[:], AF.Sigmoid), reads=[pak], writes=[("csg", ct)])
                    for qq in range(4):
                        cd, cdk = candp.get()
                        for r in range(4):
                            if mode == "C":
                                S.dma("sp", cd[64 * (r % 2):64 * (r % 2) + 64, r // 2, :], exo[r, qq, 0:64, tok0:tok0 + TB], writes=[(cdk, "a", r)])
                                S.dma("sp", cd[:, 2 + 2 * r:4 + 2 * r, :], exo[r, qq, 64:320, tok0:tok0 + TB].rearrange("(p c) t -> c p t", p=2), writes=[(cdk, "r", r)])
                            else:
                                S.dma("sp", cd[64 * (r % 2):64 * (r % 2) + 64, r // 2, :], exg[qq][0].ap()[64 * r:64 * r + 64, tok0:tok0 + TB], writes=[(cdk, "a", r)])
                                for p in range(2):
                                    S.dma("sp", cd[:, 2 + 2 * r + p, :], exg[qq][1 + p].ap()[128 * r:128 * r + 128, tok0:tok0 + TB], writes=[(cdk, "r", r, p)])
                        CK = [(cdk, "a", r) for r in range(4)] + [(cdk, "r", r) for r in range(4)] + [(cdk, "r", r, p) for r in range(4) for p in range(2)]
                        if qq == 0:
                            S.op("dve", lambda e: e.tensor_scalar(oall[:], cd[:], qs_t[:, 0:1], None, ALU.mult), reads=CK + ["qs_t"], writes=["oall"])
                        else:
                            S.op("dve", lambda e: e.scalar_tensor_tensor(oall[:].rearrange("p a t -> p (a t)"), cd[:].rearrange("p a t -> p (a t)"), qs_t[:, qq:qq + 1],
                                                                         oall[:].rearrange("p a t -> p (a t)"), ALU.mult, ALU.add), reads=CK + ["qs_t", "oall"], writes=["oall"])
                    for ct in range(8):
                        pA, pAk = pacc.get()
                        for a in range(2):
                            S.op("pe", lambda e: e.matmul(pA[:], wbr[:, a, ct * 128:(ct + 1) * 128], oall[:, a, :], start=(a == 0), stop=(a == 1)),
                                 reads=["oall"], writes=[pAk])
                        t1, t1k = tmpp.get()
                        S.op("dve", lambda e: e.tensor_tensor(t1[:], pA[:], sg[:, ct, :], ALU.mult), reads=[pAk, ("csg", ct)], writes=[t1k])
                        pR, pRk = pacc.get()
                        for k in range(8):
                            S.op("pe", lambda e: e.matmul(pR[:], wbr[:, 2 + k, ct * 128:(ct + 1) * 128], oall[:, 2 + k, :], start=(k == 0), stop=(k == 7)), reads=["oall"], writes=[pRk])
                        t2, t2k = tmpp.get()
                        S.op("dve", lambda e: e.tensor_tensor(t2[:], pR[:], sg[:, 8 + ct, :], ALU.mult), reads=[pRk, ("csg", 8 + ct)], writes=[t2k])
                        S.op("pool", lambda e: e.tensor_tensor(mT[:, ct, :], t1[:], t2[:], ALU.add), reads=[t1k, t2k], writes=[("mT", ct)])
                    for tt in range(4):
                        T0 = tok0 + tt * 128
                        pm, pmk = pmix.get()
                        for half in range(2):
                            for k in range(8):
                                S.op("pe", lambda e: e.matmul(pm[:, half * 512:(half + 1) * 512], mT[:, k, tt * 128:(tt + 1) * 128], wo[:, k, half * 512:(half + 1) * 512],
                                                              start=(k == 0), stop=(k == 7)), reads=[("mT", k)], writes=[(pmk, half)])
                        st, sk = stat_pool.get()
                        S.op("act", lambda e: e.activation(junk[:], pm[:], AF.Square, accum_out=st[:, 0:1]), reads=[(pmk, 0), (pmk, 1)], writes=["cjunk", sk])
                        rstd_of(st, sk)
                        xt, xk = xt_pool.get()
                        S.dma("sp", xt[:], x_q[T0:T0 + 128, :], writes=[xk])
                        bg, bgk = big.get()
                        S.op("dve", lambda e: e.scalar_tensor_tensor(bg[:], pm[:], st[:, 3:4], G[0][:], ALU.mult, ALU.mult), reads=[(pmk, 0), (pmk, 1), sk, ("G", 0)], writes=[bgk])
                        x1, x1k = x1p.get()
                        S.op("pool", lambda e: e.tensor_tensor(x1[:], bg[:], xt[:], ALU.add), reads=[bgk, xk], writes=[x1k])
                        S.dma("sp", x1_scr[T0:T0 + 128, :], x1[:], reads=[x1k], writes=[("x1_scr", T0)])
                        st2, sk2 = stat_pool.get()
                        S.op("act", lambda e: e.activation(junk[:], x1[:], AF.Square, accum_out=st2[:, 0:1]), reads=[x1k], writes=["cjunk", sk2])
                        rstd_of(st2, sk2)
                        xn2, xn2k = xn2p.get()
                        S.op("dve", lambda e: e.tensor_scalar(xn2[:], x1[:], st2[:, 3:4], None, ALU.mult), reads=[x1k, sk2], writes=[xn2k])
                        p8, p8k = pt8.get()
                        for cch in range(8):
                            S.op("pe", lambda e: e.transpose(p8[:, cch, :], xn2[:, cch * 128:(cch + 1) * 128], identb[:]), reads=[xn2k, "identb"], writes=[(p8k, cch)])
                        h2s, h2k = h2sp.get()
                        for cch in range(8):
                            S.op("act", lambda e: e.activation(h2s[:, cch, :], p8[:, cch, :], AF.Identity, bias=sh2[:, cch:cch + 1], scale=s2c[:, cch:cch + 1]),
                                 reads=[(p8k, cch), "s2c"], writes=[(h2k, cch)])
                        S.dma("sp", h2_scr[:, :, T0:T0 + 128], h2s[:], reads=[(h2k, cch) for cch in range(8)], writes=[("h2_scr", T0)])
                S.barrier()

            with ExitStack() as es2:
                sb, ps = mk_alloc(es2)
                w1 = sb("w1", [128, 8, 4 * D], BF16); w2 = sb("w2", [128, 32, D], BF16)
                with ExitStack() as esw:
                    sbw, psw = mk_alloc(esw)
                    stg = RPool("stg2", [sbw("stgb%d" % i, [128, 2048]) for i in range(3)])
                    load_cast(lambda i: w1[:, i // 2, (i % 2) * 2048:(i % 2 + 1) * 2048], lambda i: w_ff1[(i // 2) * 128:(i // 2 + 1) * 128, (i % 2) * 2048:(i % 2 + 1) * 2048], 16, 2048, stg, "w1")
                    load_cast(lambda i: w2[:, 2 * i:2 * i + 2, :].rearrange("p a n -> p (a n)"), lambda i: w_ff2[i * 256:(i + 1) * 256, :].rearrange("(a p) n -> p a n", p=128), 16, 2048, stg, "w2")
                    S.barrier()
                BL = 256
                h2p = RPool("h2b", [sb("h2b%d" % i, [128, 8, BL], BF16) for i in range(2)])
                sqp = RPool("sq", [sb("sq%d" % i, [128, 32, BL], BF16) for i in range(1)])
                rp = RPool("rl", [sb("rl%d" % i, [128, BL]) for i in range(3)])
                x1p = RPool("dx1", [sb("dx1_%d" % i, [128, D]) for i in range(2)])
                big = RPool("dbig", [sb("dbig%d" % i, [128, D]) for i in range(2)])
                outp = RPool("dout", [sb("dout%d" % i, [128, D]) for i in range(2)])
                junk = sb("djunk", [128, D], BF16)
                stat_pool = RPool("dstat", [sb("dstat%d" % i, [128, 4]) for i in range(4)])
                pf = RPool("pf", [ps("pf%d" % i, [128, BL]) for i in range(3)])
                pmix = RPool("pmx", [ps("pmx%d" % i, [128, D]) for i in range(2)])
                for i in range(3):
                    S.banks[("pf", i)] = "pf%d" % i
                for i in range(2):
                    S.banks[("pmx", i)] = "pmx%d" % i
                OUTK = []
                for blk in range(2048 // BL):
                    b0 = blk * BL
                    h2b, h2bk = h2p.get()
                    S.dma("sp", h2b[:], h2_scr[:, :, b0:b0 + BL], writes=[h2bk])
                    sq, sqk = sqp.get()
                    for ft in range(32):
                        pp_, ppk = pf.get()
                        for k in range(8):
                            S.op("pe", lambda e: e.matmul(pp_[:], w1[:, k, ft * 128:(ft + 1) * 128], h2b[:, k, :], start=(k == 0), stop=(k == 7)), reads=[h2bk], writes=[ppk])
                        rl, rlk = rp.get()
                        S.op("act", lambda e: e.activation(rl[:], pp_[:], AF.Relu), reads=[ppk], writes=[rlk])
                        S.op("pool" if ft % 2 else "dve", lambda e: e.tensor_tensor(sq[:, ft, :], rl[:], rl[:], ALU.mult), reads=[rlk], writes=[(sqk, ft)])
                    for tt in range(BL // 128):
                        T0 = b0 + tt * 128
                        pm, pmk = pmix.get()
                        for half in range(2):
                            for k in range(32):
                                S.op("pe", lambda e: e.matmul(pm[:, half * 512:(half + 1) * 512], sq[:, k, tt * 128:(tt + 1) * 128], w2[:, k, half * 512:(half + 1) * 512],
                                                              start=(k == 0), stop=(k == 31)), reads=[(sqk, k)], writes=[(pmk, half)])
                        st, sk = stat_pool.get()
                        S.op("act", lambda e: e.activation(junk[:], pm[:], AF.Square, accum_out=st[:, 0:1]), reads=[(pmk, 0), (pmk, 1)], writes=["djunk", sk])
                        S.op("dve", lambda e: e.tensor_scalar(st[:, 1:2], st[:, 0:1], 1.0 / D, 1e-6, ALU.mult, ALU.add), reads=[sk], writes=[sk])
                        S.op("act", lambda e: e.activation(st[:, 2:3], st[:, 1:2], AF.Sqrt), reads=[sk], writes=[sk])
                        S.op("dve", lambda e: e.reciprocal(st[:, 3:4], st[:, 2:3]), reads=[sk], writes=[sk])
                        x1, x1k = x1p.get()
                        S.dma("sp", x1[:], x1_scr[T0:T0 + 128, :], writes=[x1k])
                        bg, bgk = big.get()
                        S.op("dve", lambda e: e.scalar_tensor_tensor(bg[:], pm[:], st[:, 3:4], G[1][:], ALU.mult, ALU.mult), reads=[(pmk, 0), (pmk, 1), sk], writes=[bgk])
                        ot, otk = outp.get()
                        S.op("pool", lambda e: e.tensor_tensor(ot[:], bg[:], x1[:], ALU.add), reads=[bgk, x1k], writes=[otk])
                        S.dma("sp", out_d[T0:T0 + 128, :], ot[:], reads=[otk], writes=[("out", T0)])
                        OUTK.append(("out", T0))
                S.final_wait("sp", OUTK)
            keys = S.all_semkeys()
            sems = {k: es_top.enter_context(nc.semaphore(k.replace("_", ""))) for k in keys}
            with nc.Block() as block:
                S.emit(block, sems)
            return nc
        if stage < 99:
            with ExitStack() as esd:
                sbd, psd = mk_alloc(esd)
                db = sbd("dbgbuf", [128, 2048], BF16)
                for qd in range(4):
                    for r0 in (0, 128, 256):
                        n = min(128, 320 - r0)
                        S.dma("sp", db[0:n, :], (ex_out if stage >= 40 else ex_in).ap().rearrange("(q r) t -> q r t", q=4)[qd, r0:r0 + n, :], writes=["db"])
                        S.dma("sp", dbg[qd, r0:r0 + n, :], db[0:n, :], reads=["db"], writes=[("dbg", qd, r0)])
                S.final_wait("sp", [("dbg", qd, r0) for qd in range(4) for r0 in (0, 128, 256)] + [("dbgout", n) for n in DBG])
            keys = S.all_semkeys()
            sems = {k: es_top.enter_context(nc.semaphore(k.replace("_", ""))) for k in keys}
            with nc.Block() as block:
                S.emit(block, sems)
            return nc
    return nc


def core_inputs(inputs, b, j):
    f = lambda a: np.ascontiguousarray(a)
    x = inputs["x"]; L = 0
    m = {}
    m["x_b"] = f(x[b])
    m["x_q"] = f(x[b, 2048 * j:2048 * (j + 1)])
    m["pos_b"] = f(inputs["positions"][b][None, :].astype(np.int32))
    m["c_col"] = f(inputs["c"][b].reshape(8, 128).T)
    m["ada_w"] = f(inputs["ada_w"][L])
    m["ada_bc"] = f(inputs["ada_b"][L].reshape(48, 128).T)
    col = lambda v: v.reshape(8, 128).T
    m["gcols"] = f(np.stack([col(inputs["norm_mix_pre"][L]), col(inputs["norm_ffn_pre"][L])], 1))
    m["grows"] = f(np.concatenate([inputs["norm_mix_post"][L], inputs["norm_ffn_post"][L]])[None, :])
    w_in = inputs["w_in"][L]
    ha = [4 * g + j for g in range(3)]
    qc = lambda h: np.arange(64 * h, 64 * h + 64)
    kc = lambda h: 768 + qc(h)
    vc = lambda h: 1536 + qc(h)
    R0 = 2304
    cols = [qc(ha[0]), qc(ha[1]), qc(ha[2]), vc(ha[2]), kc(ha[0]), kc(ha[1]), kc(ha[2]), kc(ha[2]), vc(ha[0]), vc(ha[1])]
    rw = []
    for sec in range(3):
        rw.append(R0 + sec * 1024 + 256 * j + np.arange(256))
    rw.append(R0 + 3072 + np.arange(256))
    cols = np.concatenate(cols + rw)
    assert cols.shape[0] == NTA * 128
    m["w_in_A"] = f(w_in[:, cols])
    m["w_in_G"] = f(w_in[:, R0 + 3328:R0 + 3328 + 2048])
    mu = inputs["shift_mu"][L][cols[640:] - R0]
    m["mu_A"] = f(mu.reshape(8, 128).T)
    my = 256 * j + np.arange(256)
    m["w2a"] = f(np.concatenate([inputs["decay_w2"][L][:, my], inputs["iclr_a2"][L][:, my]], 0))
    m["g2"] = f(inputs["gate_g2"][L][:, my])
    vecs = [inputs["decay_w0"][L], inputs["iclr_a0"][L], inputs["k_k"][L], inputs["k_a"][L], inputs["r_k"][L].reshape(-1),
            inputs["gn_w"][L], inputs["gn_b"][L], inputs["gn_b"][L]]
    chv = np.stack([v[my].reshape(2, 128) for v in vecs], -1)
    m["chv"] = f(chv.transpose(1, 0, 2))
    qs = np.zeros((128, 4), np.float32); qs[:, j] = 1.0
    m["qsel"] = qs
    m["w_branch"] = f(inputs["w_branch"][L])
    m["w_out"] = f(inputs["w_out"][L])
    m["w_ff1"] = f(inputs["w_ff1"][L])
    m["w_ff2"] = f(inputs["w_ff2"][L])
    for k, v in host_consts().items():
        m["k_" + k] = f(v)
    return m


def kernel(**inputs):
    inputs = {k: np.asarray(v) for k, v in inputs.items()}
    in_maps = [core_inputs(inputs, c // 4, c % 4) for c in range(8)]
    nc = build(99, "all")
    res = run_bass_kernel_spmd(nc, in_maps, core_ids=list(range(8)))
    out = np.zeros((2, SEQ, D), np.float32)
    for c in range(8):
        out[c // 4, 2048 * (c % 4):2048 * (c % 4 + 1)] = res.results[c]["out"]
    return out
```

```python
import os
import types
import numpy as np
from contextlib import ExitStack
SUB = int(os.environ.get('SUB', '99'))
import concourse.bass as bass
import concourse.mybir as mybir
from concourse.bass_utils import run_bass_kernel_spmd

F32 = mybir.dt.float32
BF16 = mybir.dt.bfloat16
I32 = mybir.dt.int32
AF = mybir.ActivationFunctionType
ALU = mybir.AluOpType
AX = mybir.AxisListType

NDSEM = 8
SEQ = 8192
D = 1024
NBLK = 16
TB = 512
MAGIC = 12582912.0
NTA = 13


def freeze(fn):
    if fn is None or fn.__closure__ is None:
        return fn
    cells = []
    for c in fn.__closure__:
        try:
            cells.append(types.CellType(c.cell_contents))
        except ValueError:
            cells.append(c)
    return types.FunctionType(fn.__code__, fn.__globals__, fn.__name__, fn.__defaults__, tuple(cells))


class Sched:
    ENGS = ("pe", "act", "dve", "pool", "sp")

    def __init__(self, nc, same_engine_sync=True):
        self.nc = nc
        self.same = same_engine_sync
        self.ops = {e: [] for e in self.ENGS}
        self.cnt = {e: 0 for e in self.ENGS}
        self.seen = {e: {} for e in self.ENGS}
        self.state = {}
        self.dcnt = {}
        self.maxval = {}
        self.banks = {}

    def _bank(self, key):
        if isinstance(key, tuple):
            if key in self.banks:
                return self.banks[key]
            for sub in key:
                bk = self._bank(sub)
                if bk:
                    return bk
        return None

    def _bankkeys(self, reads, writes):
        out = set()
        for k in list(reads) + list(writes):
            bk = self._bank(k)
            if bk:
                out.add(("BANK", bk))
        return list(out)

    def _need(self, eng, ev, waits, cross_only=False):
        if ev is None:
            return
        key, val, peng = ev
        if peng == eng and key.startswith("e_") and (not self.same or eng == "pe" or cross_only):
            return
        if self.seen[eng].get(key, 0) >= val:
            return
        self.seen[eng][key] = val
        waits[key] = max(waits.get(key, 0), val)

    def _deps(self, eng, reads, writes):
        waits = {}
        for b in self._bankkeys(reads, writes):
            st = self.state.get(b)
            if st is not None:
                self._need(eng, st[0], waits, cross_only=True)
        for b in reads:
            st = self.state.get(b)
            if st is not None:
                self._need(eng, st[0], waits)
        for b in writes:
            st = self.state.get(b)
            if st is not None:
                self._need(eng, st[0], waits)
                for ev in st[1]:
                    self._need(eng, ev, waits)
        return waits

    def _commit(self, ev, reads, writes):
        self.maxval[ev[0]] = (max(self.maxval.get(ev[0], (0, None))[0], ev[1]), ev[2])
        for b in reads:
            st = self.state.setdefault(b, [None, []])
            st[1].append(ev)
            if len(st[1]) > 24:
                d = {}
                for e in st[1]:
                    if e[0] not in d or d[e[0]][1] < e[1]:
                        d[e[0]] = e
                st[1] = list(d.values())
        for b in writes:
            self.state[b] = [ev, []]
        for b in self._bankkeys(reads, writes):
            self.state[b] = [ev, []]

    def op(self, eng, fn, reads=(), writes=()):
        waits = self._deps(eng, reads, writes)
        self.cnt[eng] += 1
        key = "e_" + eng
        ev = (key, self.cnt[eng], eng)
        self.ops[eng].append((list(waits.items()), freeze(fn), (key, 1)))
        self._commit(ev, reads, writes)
        return ev

    def dma(self, q, out, in_, reads=(), writes=(), **kw):
        waits = self._deps(q, reads, writes)
        i = self.dcnt.get(q, 0)
        self.dcnt[q] = i + 1
        key = "d_%s_%d" % (q, i % NDSEM)
        val = 16 * (i // NDSEM + 1)
        if i >= NDSEM:
            self._need(q, (key, val - 16, q), waits)
        ev = (key, val, q)

        def fn(e, out=out, in_=in_, kw=kw):
            return e.dma_start(out=out, in_=in_, **kw)

        self.ops[q].append((list(waits.items()), fn, (key, 16)))
        self._commit(ev, reads, writes)
        return ev

    def custom(self, eng, fn, semkey, inc, reads=(), writes=()):
        waits = self._deps(eng, reads, writes)
        c = self.dcnt.get(semkey, 0) + inc
        self.dcnt[semkey] = c
        ev = (semkey, c, eng)
        self.ops[eng].append((list(waits.items()), freeze(fn), (semkey, inc)))
        self._commit(ev, reads, writes)
        return ev

    def barrier(self):
        for e in self.ENGS:
            waits = {}
            for key, (val, peng) in self.maxval.items():
                self._need(e, (key, val, peng), waits)
            if waits:
                self.ops[e].append((list(waits.items()), None, None))
        self.state = {}

    def final_wait(self, eng, bufs):
        waits = {}
        for b in bufs:
            st = self.state.get(b)
            if st is not None:
                self._need(eng, st[0], waits)
        self.ops[eng].append((list(waits.items()), None, None))

    def all_semkeys(self):
        keys = set()
        for e in self.ENGS:
            for waits, fn, inc in self.ops[e]:
                for k, _ in waits:
                    keys.add(k)
                if inc is not None:
                    keys.add(inc[0])
        return sorted(keys)

    def emit(self, block, sems):
        def mk(eng_name):
            def body(e):
                for waits, fn, inc in self.ops[eng_name]:
                    for k, v in waits:
                        e.wait_ge(sems[k], v)
                    if fn is None:
                        continue
                    ins = fn(e)
                    ins.then_inc(sems[inc[0]], inc[1])
            return body

        block.tensor(mk("pe"))
        block.scalar(mk("act"))
        block.vector(mk("dve"))
        block.gpsimd(mk("pool"))
        block.sync(mk("sp"))


class RPool:
    def __init__(self, name, tiles):
        self.name, self.tiles, self.i = name, tiles, 0

    def get(self):
        k = self.i % len(self.tiles)
        self.i += 1
        return self.tiles[k], (self.name, k)


def host_consts():
    p = np.arange(128)
    c = {}
    c["ident"] = np.eye(128, dtype=np.float32)
    r = (p % 64)[:, None]
    n = np.arange(64)[None, :]
    su = (n > r).astype(np.float32)
    ui = (n >= r).astype(np.float32)
    sl = (n < r).astype(np.float32)
    c["mask1"] = np.concatenate([su, ui], 1)
    c["masksl"] = sl
    c["identh"] = (n == r).astype(np.float32)
    c["blockones"] = ((p[:, None] // 64) == (p[None, :] // 64)).astype(np.float32)
    perm = np.zeros((128, 128), np.float32)
    for b in (0, 64):
        for i in range(8):
            perm[b + 8 + i, b + i] = -1.0
            perm[b + i, b + 8 + i] = 1.0
    c["perm"] = perm
    inv = 500000.0 ** (-(np.arange(8, dtype=np.float32)) * 2.0 / 16.0)
    inv2pi = np.zeros((128, 1), np.float32)
    for b in (0, 64):
        for i in range(16):
            inv2pi[b + i, 0] = np.float32(inv[i % 8])
    c["inv2pi"] = inv2pi
    seg = np.ones((128, 512), np.float32)
    seg[:, ::64] = 0.0
    c["segmask"] = seg
    qi = np.arange(128)[:, None]
    kj = np.arange(256)[None, :]
    lag = qi + 128 - kj
    band = (lag >= 0) & (lag <= 128)
    c["amask"] = np.where(band, 0.0, -1e30).astype(np.float32)
    c["amask0"] = np.where(band & (kj >= 128), 0.0, -1e30).astype(np.float32)
    return c


CONST_SHAPES = {"ident": [128, 128], "mask1": [128, 128], "masksl": [128, 64], "identh": [128, 64],
                "blockones": [128, 128], "perm": [128, 128], "inv2pi": [128, 1], "segmask": [128, 512],
                }
ACONST_SHAPES = {"amask": [128, 256], "amask0": [128, 256]}


def build(stage=99, mode="all"):
    nc = bass.Bass("TRN2", target_bir_lowering=False)

    def din(name, shape, dt=F32):
        return nc.dram_tensor(name, shape, dt, kind="ExternalInput").ap()

    x_b = din("x_b", [SEQ, D])
    x_q = din("x_q", [2048, D])
    pos_b = din("pos_b", [1, SEQ], I32)
    c_col = din("c_col", [128, 8])
    ada_w = din("ada_w", [D, 6 * D])
    ada_bc = din("ada_bc", [128, 48])
    gcols = din("gcols", [128, 2, 8])
    grows = din("grows", [1, 2 * D])
    w_in_A = din("w_in_A", [D, NTA * 128])
    w_in_G = din("w_in_G", [D, 2048])
    mu_A = din("mu_A", [128, 8])
    w2a = din("w2a", [128, 256])
    g2 = din("g2", [128, 256])
    qsel = din("qsel", [128, 4])
    chv = din("chv", [128, 2, 8])
    w_branch = din("w_branch", [1280, D])
    w_out = din("w_out", [D, D])
    w_ff1 = din("w_ff1", [D, 4 * D])
    w_ff2 = din("w_ff2", [4 * D, D])
    cin = {k: din("k_" + k, v) for k, v in list(CONST_SHAPES.items()) + list(ACONST_SHAPES.items())}
    out_d = nc.dram_tensor("out", [2048, D], F32, kind="ExternalOutput").ap()
    dbg = None
    if stage < 99:
        dbg = nc.dram_tensor("dbg", [4, 320, 2048], BF16, kind="ExternalOutput").ap()
    qkv_scr = nc.dram_tensor("qkv_scr", [128, 5, SEQ], BF16).ap()
    ex_in = nc.dram_tensor("ex_in", [1280, 2048], BF16, **({"kind": "ExternalOutput"} if mode == "A" else {}))
    ex_out = nc.dram_tensor("ex_out", [4 * 1280, 2048], BF16, **({"kind": "ExternalInput"} if mode == "C" else {}))
    h2_scr = nc.dram_tensor("h2_scr", [128, 8, 2048], BF16).ap()
    exg = [[nc.dram_tensor("exg%d_%d" % (qq, i), [4 * (64 if i == 0 else 128), 2048], BF16) for i in range(3)] for qq in range(4)]
    att_scr = nc.dram_tensor("att_scr", [3, SEQ, 66], F32).ap()
    x1_scr = nc.dram_tensor("x1_scr", [2048, D], F32).ap()

    S = Sched(nc)
    global LAST_SCHED
    LAST_SCHED = S
    es_top = ExitStack()
    DBG = {}

    def dd(name, ap, shape, dt, rk):
        if stage >= 99 or name in DBG:
            return
        t = nc.dram_tensor("dbg_" + name, shape, dt, kind="ExternalOutput").ap()
        DBG[name] = t
        S.dma("sp", t, ap, reads=rk, writes=[("dbgout", name)])
    with es_top:
        def mk_alloc(es):
            def sb(name, shape, dt=F32):
                return es.enter_context(nc.sbuf_tensor(name, shape, dt))

            def ps(name, shape, dt=F32):
                return es.enter_context(nc.psum_tensor(name, shape, dt))
            return sb, ps

        sbT, psT = mk_alloc(es_top)
        K = {}
        for k, shp in CONST_SHAPES.items():
            K[k] = sbT("c_" + k, shp)
            S.dma("sp", K[k][:], cin[k], writes=["c_" + k])
        identb = sbT("identb", [128, 128], BF16)
        S.op("dve", lambda e: e.tensor_copy(identb[:], K["ident"][:]), reads=["c_ident"], writes=["identb"])
        ccol = sbT("ccol", [128, 8])
        scol = sbT("scol", [128, 8])
        S.dma("sp", ccol[:], c_col, writes=["ccol"])
        S.op("act", lambda e: e.activation(scol[:], ccol[:], AF.Silu), reads=["ccol"], writes=["scol"])
        adab = sbT("adab", [128, 48])
        S.dma("sp", adab[:], ada_bc, writes=["adab"])
        gcol = sbT("gcol", [128, 2, 8])
        S.dma("sp", gcol[:], gcols, writes=["gcol"])
        modc = sbT("modc", [128, 48])

        def compute_mod(sec, sbl, psl, tagp):
            slab = sbl("adaslab" + tagp, [128, 8, 1024])
            for cch in range(8):
                S.dma("sp", slab[:, cch, :], ada_w[cch * 128:(cch + 1) * 128, sec * 1024:(sec + 1) * 1024],
                      writes=[("slab", tagp, cch)])
            pm = psl("pmod" + tagp, [128, 8])
            S.banks[("pmod", tagp)] = "pmod" + tagp
            for t in range(8):
                for cch in range(8):
                    S.op("pe", lambda e, t=t, cch=cch: e.matmul(pm[:, t:t + 1], slab[:, cch, t * 128:(t + 1) * 128],
                                                               scol[:, cch:cch + 1], start=(cch == 0), stop=(cch == 7)),
                         reads=[("slab", tagp, cch), "scol"], writes=[(("pmod", tagp), t)])
            S.op("dve", lambda e: e.tensor_tensor(modc[:, 8 * sec:8 * sec + 8], pm[:], adab[:, 8 * sec:8 * sec + 8], ALU.add),
                 reads=[(("pmod", tagp), t) for t in range(8)] + ["adab"], writes=[("modc", sec)])

        with ExitStack() as esA:
            sb, ps = mk_alloc(esA)
            with ExitStack() as es0:
                sb0, ps0 = mk_alloc(es0)
                compute_mod(0, sb0, ps0, "a")
                compute_mod(1, sb0, ps0, "b")
                S.barrier()
            s1c = sb("s1c", [128, 8])
            S.op("dve", lambda e: e.scalar_tensor_tensor(s1c[:], modc[:, 8:16], 1.0, gcol[:, 0, :], ALU.add, ALU.mult),
                 reads=[("modc", 1), "gcol"], writes=["s1c"])
            sh1 = modc[:, 0:8]

            wA = sb("wA", [128, 8, NTA * 128], BF16)
            with ExitStack() as esw:
                sbw, psw = mk_alloc(esw)
                wstage = RPool("wstage", [sbw("wstg%d" % i, [128, NTA * 128]) for i in range(2)])
                for cch in range(8):
                    t_, k_ = wstage.get()
                    S.dma("sp", t_[:], w_in_A[cch * 128:(cch + 1) * 128, :], writes=[k_])
                    S.op("pool" if cch % 2 else "act",
                         (lambda e, t_=t_, cch=cch: e.tensor_copy(wA[:, cch, :], t_[:])) if cch % 2 else
                         (lambda e, t_=t_, cch=cch: e.activation(wA[:, cch, :], t_[:], AF.Copy)),
                         reads=[k_], writes=[("wA", cch)])
                S.barrier()
            muA = sb("muA", [128, 8])
            S.dma("sp", muA[:], mu_A, writes=["muA"])
            w2a_f = sb("w2a_f", [128, 256]); g2_f = sb("g2_f", [128, 256])
            w2a_b = sb("w2a_b", [128, 256], BF16); g2_b = sb("g2_b", [128, 256], BF16)
            S.dma("sp", w2a_f[:], w2a, writes=["w2a_f"]); S.dma("sp", g2_f[:], g2, writes=["g2_f"])
            S.op("dve", lambda e: e.tensor_copy(w2a_b[:], w2a_f[:]), reads=["w2a_f"], writes=["w2a_b"])
            S.op("dve", lambda e: e.tensor_copy(g2_b[:], g2_f[:]), reads=["g2_f"], writes=["g2_b"])
            chvt = sb("chvt", [128, 2, 8])
            S.dma("sp", chvt[:], chv, writes=["chvt"])
            omka = sb("omka", [128, 2])
            S.op("dve", lambda e: e.tensor_scalar(omka[:], chvt[:, :, 3], -1.0, 1.0, ALU.mult, ALU.add), reads=["chvt"], writes=["omka"])
            bones_b = sb("bones_b", [128, 128], BF16); perm_b = sb("perm_b", [128, 128], BF16)
            S.op("dve", lambda e: e.tensor_copy(bones_b[:], K["blockones"][:]), reads=["c_blockones"], writes=["bones_b"])
            S.op("dve", lambda e: e.tensor_copy(perm_b[:], K["perm"][:]), reads=["c_perm"], writes=["perm_b"])
            mask1x4 = sb("mask1x4", [128, 4, 128]); maskslx4 = sb("maskslx4", [128, 4, 64]); identhx4 = sb("identhx4", [128, 4, 64])
            for ch in range(4):
                S.op("pool", lambda e, ch=ch: e.tensor_copy(mask1x4[:, ch, :], K["mask1"][:]), reads=["c_mask1"], writes=[("m1", ch)])
                S.op("pool", lambda e, ch=ch: e.tensor_copy(maskslx4[:, ch, :], K["masksl"][:]), reads=["c_masksl"], writes=[("msl", ch)])
                S.op("pool", lambda e, ch=ch: e.tensor_copy(identhx4[:, ch, :], K["identh"][:]), reads=["c_identh"], writes=[("idh", ch)])
            MK1 = [("m1", ch) for ch in range(4)]; MKSL = [("msl", ch) for ch in range(4)]; IDH = [("idh", ch) for ch in range(4)]

            xt_pool = RPool("xt", [sb("xt%d" % i, [128, D]) for i in range(2)])
            xn_pool = RPool("xn", [sb("xn%d" % i, [128, D], BF16) for i in range(4)])
            junk = sb("junk", [128, D], BF16)
            stat_pool = RPool("stat", [sb("stat%d" % i, [128, 4]) for i in range(8)])
            hT_pool = RPool("hT", [sb("hT%d" % i, [128, 8, TB], BF16) for i in range(1)])
            raw = [sb("raw%d" % i, [128, TB + 1]) for i in range(8)]
            for i in range(8):
                S.op("pool", lambda e, i=i: e.memset(raw[i][:], 0.0), writes=[("raw", i)])
            qst_pool = RPool("qst", [sb("qst%d" % i, [128, 5, TB], BF16) for i in range(1)])
            f32p = RPool("f", [sb("f%d" % i, [128, TB]) for i in range(22)])
            b16p = RPool("b", [sb("b%d" % i, [128, TB], BF16) for i in range(10)])
            posi = sb("posi", [128, TB], I32)
            AR = [RPool("AR%d" % p, [sb("AR%d_%d" % (p, i), [128, 8, 128], BF16) for i in range(1)]) for p in range(2)]
            KB = [RPool("KB%d" % p, [sb("KB%d_%d" % (p, i), [128, 8, 128], BF16) for i in range(1)]) for p in range(2)]
            gkp = RPool("Gk", [sb("Gk%d" % i, [128, 4, 128], BF16) for i in range(2)])
            gbp = RPool("Gb", [sb("Gb%d" % i, [128, 4, 128], BF16) for i in range(2)])
            glp = RPool("GL", [sb("GL%d" % i, [128, 4, 64], BF16) for i in range(2)])
            lmp = RPool("LM", [sb("LM%d" % i, [128, 2, 4, 64], BF16) for i in range(6)])
            xp = RPool("X", [sb("X%d" % i, [128, 4, 64], BF16) for i in range(6)])
            tmp_ = RPool("TM", [sb("TM%d" % i, [128, 4, 4, 64], BF16) for i in range(2)])
            zsp = RPool("Zs", [sb("Zs%d" % i, [128, 4, 128], BF16) for i in range(2)])
            aup = RPool("AU", [sb("AU%d" % i, [128, 4, 128], BF16) for i in range(2)])
            mctp = RPool("McT", [sb("McT%d" % i, [128, 4, 64], BF16) for i in range(2)])
            ncwp = RPool("NcW", [sb("NcW%d" % i, [128, 4, 64]) for i in range(2)])
            rhtp = RPool("RhT", [sb("RhT%d" % i, [128, 4, 64], BF16) for i in range(2)])
            Sb = [RPool("Sb%d" % p, [sb("Sb%d_%d" % (p, i), [128, 64], BF16) for i in range(3)]) for p in range(2)]
            ytm = [RPool("Ytm%d" % p, [sb("Ytm%d_%d" % (p, i), [128, 8, 64]) for i in range(1)]) for p in range(2)]
            ysq = sb("ysq", [128, 8, 64])
            ynp = RPool("yn", [sb("yn%d" % i, [128, 8, 64], BF16) for i in range(2)])
            gst = RPool("gst", [sb("gst%d" % i, [128, 8]) for i in range(8)])
            orw_pool = RPool("orw", [sb("orw%d" % i, [128, 2, TB], BF16) for i in range(2)])
            pp = RPool("pp", [ps("pp%d" % i, [128, TB]) for i in range(2)])
            ptx = RPool("ptx", [ps("ptx", [128, 4, 128], BF16)])
            ptb = RPool("ptb", [ps("ptb%d" % i, [128, 1024], BF16) for i in range(1)])
            pg = RPool("pg", [ps("pg%d" % i, [128, TB]) for i in range(3)])
            pys_t = ps("pys", [128, 8, 64])
            pys = RPool("pys", [pys_t[:, i, :] for i in range(8)])
            for i in range(2):
                S.banks[("pp", i)] = "pp%d" % i
            S.banks[("ptx", 0)] = "ptx"
            S.banks[("ptb", 0)] = "ptb"
            for i in range(3):
                S.banks[("pg", i)] = "pg%d" % i
            for i in range(8):
                S.banks[("pys", i)] = "pys"

            sstate = []
            for p in range(2):
                t_, k_ = Sb[p].get()
                S.op("pool", lambda e, t_=t_: e.memset(t_[:], 0.0), writes=[k_])
                sstate.append((t_, k_))

            exv = ex_in.ap().rearrange("(q r) t -> q r t", q=4)

            for tb in range((NBLK if stage >= 50 else 4) if mode != "C" else 0):
                if stage == 9:
                    break
                tok0 = tb * TB
                xns = []
                for tt in range(4):
                    xt, xk = xt_pool.get()
                    S.dma("sp", xt[:], x_b[tok0 + tt * 128: tok0 + (tt + 1) * 128, :], writes=[xk])
                    st, sk = stat_pool.get()
                    S.op("act", lambda e, xt=xt, st=st: e.activation(junk[:], xt[:], AF.Square, accum_out=st[:, 0:1]),
                         reads=[xk], writes=["junk", sk])
                    if SUB <= 1:
                        continue
                    S.op("dve", lambda e, st=st: e.tensor_scalar(st[:, 1:2], st[:, 0:1], 1.0 / D, 1e-6, ALU.mult, ALU.add), reads=[sk], writes=[sk])
                    S.op("act", lambda e, st=st: e.activation(st[:, 2:3], st[:, 1:2], AF.Sqrt), reads=[sk], writes=[sk])
                    S.op("dve", lambda e, st=st: e.reciprocal(st[:, 3:4], st[:, 2:3]), reads=[sk], writes=[sk])
                    if SUB <= 2:
                        continue
                    xn, nk = xn_pool.get()
                    S.op("dve", lambda e, xn=xn, xt=xt, st=st: e.tensor_scalar(xn[:], xt[:], st[:, 3:4], None, ALU.mult),
                         reads=[xk, sk], writes=[nk])
                    xns.append((xn, nk))
                if SUB <= 3:
                    continue
                hT, hk = hT_pool.get()
                for cch in range(8):
                    pt, pk = ptx.get()
                    for tt in range(4):
                        xn, nk = xns[tt]
                        S.op("pe", lambda e, pt=pt, xn=xn, tt=tt, cch=cch: e.transpose(pt[:, tt, :], xn[:, cch * 128:(cch + 1) * 128], identb[:]),
                             reads=[nk, "identb"], writes=[(pk, tt)])
                    if SUB <= 4:
                        continue
                    S.op("act", lambda e, pt=pt, hT=hT, cch=cch: e.activation(hT[:, cch, :], pt[:].rearrange("p a b -> p (a b)"), AF.Identity,
                                                                          bias=sh1[:, cch:cch + 1], scale=s1c[:, cch:cch + 1]),
                         reads=[(pk, tt) for tt in range(4)] + ["s1c", ("modc", 0)], writes=[(hk, cch)])
                HK = [(hk, cch) for cch in range(8)]
                dd("hT", hT[:, 0, :], [128, TB], BF16, HK)

                def proj(tile):
                    pt, pk = pp.get()
                    for cch in range(8):
                        S.op("pe", lambda e, pt=pt, cch=cch, tile=tile: e.matmul(pt[:], wA[:, cch, tile * 128:(tile + 1) * 128], hT[:, cch, :],
                                                                                start=(cch == 0), stop=(cch == 7)),
                             reads=[("wA", cch), (hk, cch)], writes=[pk])
                    return pt, pk

                if stage == 10:
                    continue
                S.dma("sp", posi[:], pos_b[:, tok0:tok0 + TB].partition_broadcast(128), writes=["posi"])
                posf, pfk = f32p.get()
                S.op("dve", lambda e, posf=posf: e.tensor_copy(posf[:], posi[:]), reads=["posi"], writes=[pfk])
                ysn, ysk = f32p.get()
                S.op("dve", lambda e, ysn=ysn, posf=posf: e.tensor_scalar(ysn[:], posf[:], K["inv2pi"][:, 0:1], float(1.0 / (2 * np.pi)), ALU.mult, ALU.mult),
                     reads=[pfk, "c_inv2pi"], writes=[ysk])
                ycs, yck = f32p.get()
                S.op("pool", lambda e, ycs=ycs, ysn=ysn: e.tensor_scalar(ycs[:], ysn[:], 0.25, None, ALU.add), reads=[ysk], writes=[yck])
                tabs = []
                for (yy, yk) in ((ysn, ysk), (ycs, yck)):
                    kk_, kkk = f32p.get()
                    S.op("dve", lambda e, kk_=kk_, yy=yy: e.tensor_scalar(kk_[:], yy[:], MAGIC, MAGIC, ALU.add, ALU.subtract), reads=[yk], writes=[kkk])
                    S.op("dve", lambda e, kk_=kk_, yy=yy: e.tensor_tensor(kk_[:], yy[:], kk_[:], ALU.subtract), reads=[yk, kkk], writes=[kkk])
                    tb_, tk_ = f32p.get()
                    S.op("act", lambda e, tb_=tb_, kk_=kk_: e.activation(tb_[:], kk_[:], AF.Sin, scale=float(2 * np.pi)), reads=[kkk], writes=[tk_])
                    tabs.append((tb_, tk_))
                (SS, ssk), (CC, cck) = tabs

                qst, qk = qst_pool.get()
                for tile in range(5):
                    pt, pk = proj(tile)
                    if tile == 4:
                        S.op("act", lambda e, pt=pt, tile=tile: e.activation(qst[:, tile, :], pt[:], AF.Copy), reads=[pk], writes=[(qk, tile)])
                        continue
                    qb, qbk = b16p.get()
                    S.op("act", lambda e, pt=pt, qb=qb: e.activation(qb[:], pt[:], AF.Copy), reads=[pk], writes=[qbk])
                    nrow = 128 if tile in (0, 2) else 64
                    pq, pqk = pg.get()
                    S.op("pe", lambda e, pq=pq, qb=qb: e.matmul(pq[:], perm_b[:], qb[:], start=True, stop=True), reads=["perm_b", qbk], writes=[pqk])
                    t1, t1k = f32p.get()
                    S.op("dve", lambda e, t1=t1, qb=qb, nrow=nrow: e.tensor_tensor(t1[0:nrow, :], qb[0:nrow, :], CC[0:nrow, :], ALU.mult), reads=[qbk, cck], writes=[t1k])
                    t2, t2k = f32p.get()
                    S.op("dve", lambda e, t2=t2, pq=pq, nrow=nrow: e.tensor_tensor(t2[0:nrow, :], pq[0:nrow, :], SS[0:nrow, :], ALU.mult), reads=[pqk, ssk], writes=[t2k])
                    S.op("pool", lambda e, t1=t1, t2=t2, nrow=nrow, tile=tile: e.tensor_tensor(qst[0:nrow, tile, :], t1[0:nrow, :], t2[0:nrow, :], ALU.add),
                         reads=[t1k, t2k], writes=[(qk, tile)])
                    if nrow == 64:
                        S.op("pool", lambda e, qb=qb, tile=tile: e.tensor_copy(qst[64:128, tile, :], qb[64:128, :]), reads=[qbk], writes=[(qk, tile, "b")])
                S.dma("sp", qkv_scr[:, :, tok0:tok0 + TB], qst[:], reads=[(qk, t) for t in range(5)] + [(qk, 1, "b"), (qk, 3, "b")], writes=[("qkv_scr", tb)])

                if stage == 11:
                    continue
                for i in range(8):
                    pt, pk = proj(5 + i)
                    S.op("pool", lambda e, i=i: e.tensor_copy(raw[i][:, 0:1], raw[i][:, TB:TB + 1]), reads=[("raw", i)], writes=[("raw", i)])
                    S.op("act", lambda e, i=i, pt=pt: e.activation(raw[i][:, 1:TB + 1], pt[:], AF.Copy), reads=[pk], writes=[("raw", i)])

                def shift(i, eng="dve"):
                    d_, dk = f32p.get()
                    S.op("pool", lambda e, d_=d_, i=i: e.tensor_tensor(d_[:], raw[i][:, 0:TB], raw[i][:, 1:TB + 1], ALU.subtract), reads=[("raw", i)], writes=[dk])
                    o_, ok = f32p.get()
                    S.op("dve", lambda e, d_=d_, o_=o_, i=i: e.scalar_tensor_tensor(o_[:], d_[:], muA[:, i:i + 1], raw[i][:, 1:TB + 1], ALU.mult, ALU.add),
                         reads=[dk, ("raw", i), "muA"], writes=[ok])
                    return o_, ok

                lora, lok = shift(6)
                dgs, dgk = shift(7)
                tdw, tdk = b16p.get()
                S.op("act", lambda e, tdw=tdw, lora=lora: e.activation(tdw[0:64, :], lora[0:64, :], AF.Tanh), reads=[lok], writes=[(tdk, 0)])
                S.op("act", lambda e, tdw=tdw, lora=lora: e.activation(tdw[64:128, :], lora[64:128, :], AF.Copy), reads=[lok], writes=[(tdk, 1)])
                sg, sgk = b16p.get()
                S.op("act", lambda e, sg=sg, dgs=dgs: e.activation(sg[:], dgs[:], AF.Sigmoid), reads=[dgk], writes=[sgk])
                dd("lora", lora[:], [128, TB], F32, [lok])
                dd("tdw", tdw[:], [128, TB], BF16, [(tdk, 0), (tdk, 1)])
                dd("w2a_b", w2a_b[:], [128, 256], BF16, ["w2a_b"])

                orw, ork = orw_pool.get()
                for p in range(2):
                    cv = lambda j, p=p: chvt[:, p, j:j + 1]
                    r_s, rk_ = shift(0 + p)
                    k_s, kk_ = shift(2 + p)
                    v_s, vk_ = shift(4 + p)
                    dd("raw0", raw[0][:, 0:TB], [128, TB], F32, [("raw", 0)])
                    dd("r_s", r_s[:], [128, TB], F32, [rk_])
                    dd("k_s", k_s[:], [128, TB], F32, [kk_])
                    pz, pzk = pg.get()
                    S.op("pe", lambda e, pz=pz, p=p: e.matmul(pz[:], w2a_b[0:64, p * 128:(p + 1) * 128], tdw[0:64, :], start=True, stop=True), reads=["w2a_b", (tdk, 0)], writes=[pzk])
                    lw, lwk = f32p.get()
                    S.op("act", lambda e, lw=lw, pz=pz, cv=cv: e.activation(lw[:], pz[:], AF.Sigmoid, bias=cv(0)), reads=[pzk, "chvt"], writes=[lwk])
                    S.op("pool", lambda e, lw=lw: e.tensor_scalar(lw[:], lw[:], float(-np.exp(-0.5)), None, ALU.mult), reads=[lwk], writes=[lwk])
                    pa, pak = pg.get()
                    S.op("pe", lambda e, pa=pa, p=p: e.matmul(pa[:], w2a_b[64:128, p * 128:(p + 1) * 128], tdw[64:128, :], start=True, stop=True), reads=["w2a_b", (tdk, 1)], writes=[pak])
                    icl, ick = f32p.get()
                    S.op("act", lambda e, icl=icl, pa=pa, cv=cv: e.activation(icl[:], pa[:], AF.Sigmoid, bias=cv(1)), reads=[pak, "chvt"], writes=[ick])
                    cw, cwk = f32p.get()
                    S.op("dve", lambda e, cw=cw, lw=lw: e.tensor_tensor_scan(cw[:], K["segmask"][:], lw[:], 0.0, ALU.mult, ALU.add), reads=["c_segmask", lwk], writes=[cwk])
                    cwe, cwek = f32p.get()
                    S.op("pool", lambda e, cwe=cwe, cw=cw, lw=lw: e.tensor_tensor(cwe[:], cw[:], lw[:], ALU.subtract), reads=[cwk, lwk], writes=[cwek])
                    Winc, wik = f32p.get(); Winv, wvk = f32p.get(); Wexc, wek = f32p.get()
                    S.op("act", lambda e, Winc=Winc, cw=cw: e.activation(Winc[:], cw[:], AF.Exp), reads=[cwk], writes=[wik])
                    S.op("act", lambda e, Winv=Winv, cw=cw: e.activation(Winv[:], cw[:], AF.Exp, scale=-1.0), reads=[cwk], writes=[wvk])
                    S.op("act", lambda e, Wexc=Wexc, cwe=cwe: e.activation(Wexc[:], cwe[:], AF.Exp), reads=[cwek], writes=[wek])
                    dd("lw", lw[:], [128, TB], F32, [lwk])
                    dd("icl", icl[:], [128, TB], F32, [ick])
                    dd("cw", cw[:], [128, TB], F32, [cwk])
                    dd("Winc", Winc[:], [128, TB], F32, [wik])
                    dd("Winv", Winv[:], [128, TB], F32, [wvk])
                    kk, kkk = f32p.get()
                    S.op("pool", lambda e, kk=kk, k_s=k_s, cv=cv: e.tensor_scalar(kk[:], k_s[:], cv(2), None, ALU.mult), reads=[kk_, "chvt"], writes=[kkk])
                    kk2, kk2k = b16p.get()
                    S.op("pool", lambda e, kk2=kk2, kk=kk: e.tensor_tensor(kk2[:], kk[:], kk[:], ALU.mult), reads=[kkk], writes=[kk2k])
                    pn, pnk = pg.get()
                    S.op("pe", lambda e, pn=pn, kk2=kk2: e.matmul(pn[:], bones_b[:], kk2[:], start=True, stop=True), reads=["bones_b", kk2k], writes=[pnk])
                    rn, rnk = f32p.get()
                    S.op("act", lambda e, rn=rn, pn=pn: e.activation(rn[:], pn[:], AF.Sqrt), reads=[pnk], writes=[rnk])
                    S.op("dve", lambda e, rn=rn: e.tensor_scalar(rn[:], rn[:], 1e-12, None, ALU.max), reads=[rnk], writes=[rnk])
                    S.op("dve", lambda e, rn=rn: e.reciprocal(rn[:], rn[:]), reads=[rnk], writes=[rnk])
                    S.op("dve", lambda e, kk=kk, rn=rn: e.tensor_tensor(kk[:], kk[:], rn[:], ALU.mult), reads=[kkk, rnk], writes=[kkk])
                    km, kmk = f32p.get()
                    S.op("dve", lambda e, km=km, icl=icl, cv=cv, p=p: e.tensor_scalar(km[:], icl[:], cv(3), omka[:, p:p + 1], ALU.mult, ALU.add), reads=[ick, "chvt", "omka"], writes=[kmk])
                    S.op("pool", lambda e, km=km, k_s=k_s: e.tensor_tensor(km[:], km[:], k_s[:], ALU.mult), reads=[kmk, kk_], writes=[kmk])
                    ar, ark = AR[p].get(); kb, kbk = KB[p].get()
                    v3 = lambda t: t[:].rearrange("p (c t) -> p c t", t=64)
                    S.op("dve", lambda e, ar=ar, r_s=r_s, Winc=Winc, v3=v3: e.tensor_tensor(ar[:, :, 64:128], v3(r_s), v3(Winc), ALU.mult), reads=[rk_, wik], writes=[(ark, "r")])
                    S.op("dve", lambda e, ar=ar, kk=kk, Wexc=Wexc, v3=v3: e.scalar_tensor_tensor(ar[:, :, 0:64], v3(kk), -1.0, v3(Wexc), ALU.mult, ALU.mult), reads=[kkk, wek], writes=[(ark, "a")])
                    S.op("pool", lambda e, kb=kb, km=km, Winv=Winv, v3=v3: e.tensor_tensor(kb[:, :, 0:64], v3(km), v3(Winv), ALU.mult), reads=[kmk, wvk], writes=[(kbk, "k")])
                    bt, btk = f32p.get()
                    S.op("pool", lambda e, bt=bt, kk=kk, icl=icl: e.tensor_tensor(bt[:], kk[:], icl[:], ALU.mult), reads=[kkk, ick], writes=[btk])
                    S.op("pool", lambda e, kb=kb, bt=bt, Winv=Winv, v3=v3: e.tensor_tensor(kb[:, :, 64:128], v3(bt), v3(Winv), ALU.mult), reads=[btk, wvk], writes=[(kbk, "b")])
                    ARK = [(ark, "r"), (ark, "a")]; KBK = [(kbk, "k"), (kbk, "b")]
                    dd("kk", kk[:], [128, TB], F32, [kkk])
                    dd("km", km[:], [128, TB], F32, [kmk])
                    dd("AR", ar[:], [128, 8, 128], BF16, ARK)
                    dd("KB", kb[:], [128, 8, 128], BF16, KBK)
                    vb, vbk = b16p.get()
                    S.op("act", lambda e, vb=vb, v_s=v_s: e.activation(vb[:], v_s[:], AF.Copy), reads=[vk_], writes=[vbk])
                    rkb, rkbk = b16p.get()
                    S.op("dve", lambda e, rkb=rkb, r_s=r_s, km=km, cv=cv: e.scalar_tensor_tensor(rkb[:], r_s[:], cv(4), km[:], ALU.mult, ALU.mult), reads=[rk_, kmk, "chvt"], writes=[rkbk])
                    pb, pbk = pg.get()
                    S.op("pe", lambda e, pb=pb, rkb=rkb: e.matmul(pb[:], bones_b[:], rkb[:], start=True, stop=True), reads=["bones_b", rkbk], writes=[pbk])
                    bon, bonk = f32p.get()
                    S.op("dve", lambda e, bon=bon, pb=pb, v_s=v_s: e.tensor_tensor(bon[:], pb[:], v_s[:], ALU.mult), reads=[pbk, vk_], writes=[bonk])

                    if stage == 12:
                        continue
                    yt, ytk = ytm[p].get()
                    def cg_pipe(cg):
                        c0 = cg * 4
                        def gstage(lsel, rsel_ar, width, maskt, maskk, pool_, lhs_from_kb=True):
                            pgt, pgk = pg.get()
                            pv = pgt[:].rearrange("p (c w) -> p c w", w=128)[:, :, 0:width] if width == 128 else pgt[:, 0:256].rearrange("p (c w) -> p c w", w=64)
                            for ch in range(4):
                                for hb in (0, 64):
                                    if lhs_from_kb:
                                        S.op("pe", lambda e, pv=pv, ch=ch, hb=hb, lsel=lsel: e.matmul(pv[hb:hb + 64, ch, :], kb[hb:hb + 64, c0 + ch, lsel], ar[hb:hb + 64, c0 + ch, :], start=True, stop=True),
                                             reads=ARK + KBK, writes=[(pgk, ch, hb)])
                                    else:
                                        S.op("pe", lambda e, pv=pv, ch=ch, hb=hb: e.matmul(pv[hb:hb + 64, ch, :], ar[hb:hb + 64, c0 + ch, 0:64], kb[hb:hb + 64, c0 + ch, 64:128], start=True, stop=True),
                                             reads=ARK + KBK, writes=[(pgk, ch, hb)])
                            gt, gk_ = pool_.get()
                            S.op("dve", lambda e, gt=gt, pv=pv, maskt=maskt: e.tensor_tensor(gt[:], pv, maskt[:], ALU.mult),
                                 reads=[(pgk, ch, hb) for ch in range(4) for hb in (0, 64)] + maskk, writes=[gk_])
                            return gt, gk_
                        Gk, gkk = gstage(slice(0, 64), None, 128, mask1x4, MK1, gkp)
                        Gb, gbk = gstage(slice(64, 128), None, 128, mask1x4, MK1, gbp)
                        GL, glk = gstage(None, None, 64, maskslx4, MKSL, glp, lhs_from_kb=False)
                        dd("Gk", Gk[:], [128, 4, 128], BF16, [gkk])
                        dd("Gb", Gb[:], [128, 4, 128], BF16, [gbk])
                        dd("GL", GL[:], [128, 4, 64], BF16, [glk])
                        yield
                        X, xk_ = xp.get()
                        S.op("pool", lambda e, X=X, Gb=Gb: e.tensor_tensor(X[:], Gb[:, :, 0:64], identhx4[:], ALU.add), reads=[gbk] + IDH, writes=[xk_])
                        Lk, Lkk = freeze(lambda ch, hb: GL[hb:hb + 64, ch, :]), glk
                        Mk, Mkk = freeze(lambda ch, hb: Gb[hb:hb + 64, ch, 0:64]), gbk
                        for lev in range(1, 6):
                            yield
                            plt, plk = pg.get()
                            plv = plt[:].rearrange("p (a c w) -> p a c w", a=2, w=64)
                            for ch in range(4):
                                for hb in (0, 64):
                                    S.op("pe", lambda e, plv=plv, ch=ch, hb=hb, Lk=Lk, Mk=Mk: e.matmul(plv[hb:hb + 64, 0, ch, :], Mk(ch, hb), Lk(ch, hb), start=True, stop=True),
                                         reads=[Lkk, Mkk], writes=[(plk, 0, ch, hb)])
                                    if lev < 5:
                                        S.op("pe", lambda e, plv=plv, ch=ch, hb=hb, Lk=Lk, Mk=Mk: e.matmul(plv[hb:hb + 64, 1, ch, :], Lk(ch, hb), Mk(ch, hb), start=True, stop=True),
                                             reads=[Lkk, Mkk], writes=[(plk, 1, ch, hb)])
                            lm, lmk = lmp.get()
                            na = 2 if lev < 5 else 1
                            S.op("act", lambda e, lm=lm, plv=plv, na=na: e.activation(lm[:, 0:na, :, :], plv[:, 0:na, :, :], AF.Copy),
                                 reads=[(plk, a, ch, hb) for a in range(na) for ch in range(4) for hb in (0, 64)], writes=[lmk])
                            Lk, Lkk = (lambda ch, hb, lm=lm: lm[hb:hb + 64, 0, ch, :]), lmk
                            Mk, Mkk = (lambda ch, hb, lm=lm: lm[hb:hb + 64, 1, ch, :]), lmk
                            pxt, pxk = pg.get()
                            pxv = pxt[:, 0:256].rearrange("p (c w) -> p c w", w=64)
                            for ch in range(4):
                                for hb in (0, 64):
                                    S.op("pe", lambda e, pxv=pxv, ch=ch, hb=hb, Lk=Lk, X=X: e.matmul(pxv[hb:hb + 64, ch, :], Lk(ch, hb), X[hb:hb + 64, ch, :], start=True, stop=True),
                                         reads=[Lkk, xk_], writes=[(pxk, ch, hb)])
                            Xn, xnk = xp.get()
                            S.op("dve", lambda e, Xn=Xn, pxv=pxv, X=X: e.tensor_tensor(Xn[:], pxv, X[:], ALU.add),
                                 reads=[(pxk, ch, hb) for ch in range(4) for hb in (0, 64)] + [xk_], writes=[xnk])
                            X, xk_ = Xn, xnk
                        dd("X", X[:], [128, 4, 64], BF16, [xk_])
                        yield
                        if stage == 13:
                            return
                        ptt, ptk = ptb.get()
                        ptv = ptt[:].rearrange("p (c k w) -> p c k w", k=4, w=64)
                        for ch in range(4):
                            c = c0 + ch
                            srcs = [(freeze(lambda hb, c=c: vb[hb:hb + 64, c * 64:(c + 1) * 64]), [vbk]),
                                    (freeze(lambda hb, c=c: ar[hb:hb + 64, c, 0:64]), ARK),
                                    (freeze(lambda hb, c=c: kb[hb:hb + 64, c, 0:64]), KBK),
                                    (freeze(lambda hb, c=c: kb[hb:hb + 64, c, 64:128]), KBK)]
                            for kind, (sf, sk_) in enumerate(srcs):
                                for hb in (0, 64):
                                    S.op("pe", lambda e, ptv=ptv, ch=ch, kind=kind, hb=hb, sf=sf: e.transpose(ptv[hb:hb + 64, ch, kind, :], sf(hb), identb[hb:hb + 64, hb:hb + 64]),
                                         reads=sk_ + ["identb"], writes=[(ptk, ch, kind, hb)])
                        TM, tmk = tmp_.get()
                        S.op("act", lambda e, TM=TM, ptv=ptv: e.activation(TM[:], ptv, AF.Copy),
                             reads=[(ptk, ch, kind, hb) for ch in range(4) for kind in range(4) for hb in (0, 64)], writes=[tmk])
                        yield
                        pzt, pzk2 = pg.get()
                        pzv = pzt[:, 0:256].rearrange("p (c w) -> p c w", w=64)
                        for ch in range(4):
                            for hb in (0, 64):
                                S.op("pe", lambda e, pzv=pzv, ch=ch, hb=hb, Gk=Gk, TM=TM: e.matmul(pzv[hb:hb + 64, ch, :], Gk[hb:hb + 64, ch, 0:64], TM[hb:hb + 64, ch, 0, :], start=True, stop=True),
                                     reads=[gkk, tmk], writes=[(pzk2, ch, hb)])
                        Zs, zsk = zsp.get()
                        S.op("pool", lambda e, Zs=Zs, TM=TM: e.tensor_copy(Zs[:, :, 0:64], TM[:, :, 1, :]), reads=[tmk], writes=[(zsk, 0)])
                        S.op("act", lambda e, Zs=Zs, pzv=pzv: e.activation(Zs[:, :, 64:128], pzv, AF.Copy),
                             reads=[(pzk2, ch, hb) for ch in range(4) for hb in (0, 64)], writes=[(zsk, 1)])
                        yield
                        pat, pauk = pg.get()
                        pav = pat[:].rearrange("p (c w) -> p c w", w=128)
                        for ch in range(4):
                            for hb in (0, 64):
                                S.op("pe", lambda e, pav=pav, ch=ch, hb=hb, X=X, Zs=Zs: e.matmul(pav[hb:hb + 64, ch, :], X[hb:hb + 64, ch, :], Zs[hb:hb + 64, ch, :], start=True, stop=True),
                                     reads=[xk_, (zsk, 0), (zsk, 1)], writes=[(pauk, ch, hb)])
                        AU, auk = aup.get()
                        S.op("dve", lambda e, AU=AU, pav=pav: e.tensor_copy(AU[:], pav), reads=[(pauk, ch, hb) for ch in range(4) for hb in (0, 64)], writes=[auk])
                        dd("TM", TM[:], [128, 4, 4, 64], BF16, [tmk])
                        dd("AU", AU[:], [128, 4, 128], BF16, [auk])
                        yield
                        pmt, pmk = pg.get()
                        pmv = pmt[:].rearrange("p (c w) -> p c w", w=128)
                        for ch in range(4):
                            for hb in (0, 64):
                                S.op("pe", lambda e, pmv=pmv, ch=ch, hb=hb, AU=AU, TM=TM: e.matmul(pmv[hb:hb + 64, ch, 0:64], AU[hb:hb + 64, ch, 0:64], TM[hb:hb + 64, ch, 3, :], start=True, stop=True),
                                     reads=[auk, tmk], writes=[(pmk, ch, hb, 0)])
                                S.op("pe", lambda e, pmv=pmv, ch=ch, hb=hb, AU=AU, TM=TM: e.matmul(pmv[hb:hb + 64, ch, 64:128], TM[hb:hb + 64, ch, 3, :], AU[hb:hb + 64, ch, 64:128], start=True, stop=False),
                                     reads=[auk, tmk], writes=[(pmk, ch, hb, 1)])
                                S.op("pe", lambda e, pmv=pmv, ch=ch, hb=hb, TM=TM: e.matmul(pmv[hb:hb + 64, ch, 64:128], TM[hb:hb + 64, ch, 2, :], TM[hb:hb + 64, ch, 0, :], start=False, stop=True),
                                     reads=[tmk], writes=[(pmk, ch, hb, 1)])
                        McT, mck = mctp.get()
                        S.op("dve", lambda e, McT=McT, pmv=pmv: e.tensor_tensor(McT[:], pmv[:, :, 0:64], identhx4[:], ALU.add),
                             reads=[(pmk, ch, hb, 0) for ch in range(4) for hb in (0, 64)] + IDH, writes=[mck])
                        NcW, nck = ncwp.get()
                        for ch in range(4):
                            wc = Winc[:, (c0 + ch) * 64 + 63:(c0 + ch) * 64 + 64]
                            S.op("dve", lambda e, NcW=NcW, pmv=pmv, ch=ch, wc=wc: e.tensor_scalar(NcW[:, ch, :], pmv[:, ch, 64:128], wc, None, ALU.mult),
                                 reads=[(pmk, ch, hb, 1) for hb in (0, 64)] + [wik], writes=[(nck, ch)])
                        yield
                        prt, prk = pg.get()
                        prv = prt[:, 0:256].rearrange("p (c w) -> p c w", w=64)
                        for ch in range(4):
                            for hb in (0, 64):
                                S.op("pe", lambda e, prv=prv, ch=ch, hb=hb, AU=AU, Gb=Gb: e.matmul(prv[hb:hb + 64, ch, :], AU[hb:hb + 64, ch, 0:64], Gb[hb:hb + 64, ch, 64:128], start=True, stop=True),
                                     reads=[auk, gbk], writes=[(prk, ch, hb)])
                        RhT, rhk = rhtp.get()
                        S.op("dve", lambda e, RhT=RhT, prv=prv: e.tensor_tensor(RhT[:], prv, ar[:, c0:c0 + 4, 64:128], ALU.add),
                             reads=[(prk, ch, hb) for ch in range(4) for hb in (0, 64)] + ARK, writes=[rhk])
                        dd("McT", McT[:], [128, 4, 64], BF16, [mck])
                        dd("NcW", NcW[:], [128, 4, 64], F32, [(nck, ch) for ch in range(4)])
                        dd("RhT", RhT[:], [128, 4, 64], BF16, [rhk])
                        yield
                        for ch in range(4):
                            c = c0 + ch
                            s0, s0k = sstate[p]
                            py, pyk = pys.get()
                            for hb in (0, 64):
                                S.op("pe", lambda e, py=py, hb=hb, ch=ch, Gb=Gb, AU=AU: e.matmul(py[hb:hb + 64, :], Gb[hb:hb + 64, ch, 64:128], AU[hb:hb + 64, ch, 64:128], start=True, stop=False),
                                     reads=[gbk, auk], writes=[(pyk, hb)])
                                S.op("pe", lambda e, py=py, hb=hb, ch=ch, Gk=Gk, TM=TM: e.matmul(py[hb:hb + 64, :], Gk[hb:hb + 64, ch, 64:128], TM[hb:hb + 64, ch, 0, :], start=False, stop=False),
                                     reads=[gkk, tmk], writes=[(pyk, hb)])
                                S.op("pe", lambda e, py=py, hb=hb, ch=ch, RhT=RhT, s0=s0: e.matmul(py[hb:hb + 64, :], RhT[hb:hb + 64, ch, :], s0[hb:hb + 64, :], start=False, stop=True),
                                     reads=[rhk, s0k], writes=[(pyk, hb)])
                            S.op("act", lambda e, py=py, yt=yt, c=c: e.activation(yt[:, c, :], py, AF.Copy), reads=[(pyk, 0), (pyk, 64)], writes=[(ytk, c)])
                            psn, psk = pys.get()
                            for hb in (0, 64):
                                S.op("pe", lambda e, psn=psn, hb=hb, ch=ch, McT=McT, s0=s0: e.matmul(psn[hb:hb + 64, :], McT[hb:hb + 64, ch, :], s0[hb:hb + 64, :], start=True, stop=True),
                                     reads=[mck, s0k], writes=[(psk, hb)])
                            s1_, s1k = Sb[p].get()
                            wc = Winc[:, c * 64 + 63:c * 64 + 64]
                            S.op("dve", lambda e, s1_=s1_, psn=psn, wc=wc, NcW=NcW, ch=ch: e.scalar_tensor_tensor(s1_[:], psn, wc, NcW[:, ch, :], ALU.mult, ALU.add),
                                 reads=[(psk, 0), (psk, 64), wik, (nck, ch)], writes=[s1k])
                            sstate[p] = (s1_, s1k)
                    pipes = [cg_pipe(0), cg_pipe(1)]
                    while pipes:
                        for g_ in list(pipes):
                            try:
                                next(g_)
                            except StopIteration:
                                pipes.remove(g_)
                    if stage == 13:
                        continue
                    YK = [(ytk, c) for c in range(8)]
                    dd("yt", yt[:], [128, 8, 64], F32, YK)
                    dd("Sb", sstate[p][0][:], [128, 64], BF16, [sstate[p][1]])
                    g0, g0k = gst.get(); g1, g1k = gst.get(); g2_, g2k = gst.get(); g3, g3k = gst.get()
                    S.op("dve", lambda e, g0=g0, yt=yt: e.tensor_reduce(g0[:], yt[:], AX.X, ALU.add), reads=YK, writes=[g0k])
                    S.op("pool", lambda e, yt=yt: e.tensor_tensor(ysq[:], yt[:], yt[:], ALU.mult), reads=YK, writes=["ysq"])
                    S.op("dve", lambda e, g1=g1: e.tensor_reduce(g1[:], ysq[:], AX.X, ALU.add), reads=["ysq"], writes=[g1k])
                    S.op("dve", lambda e, g0=g0: e.tensor_scalar(g0[:], g0[:], 1.0 / 64, None, ALU.mult), reads=[g0k], writes=[g0k])
                    S.op("dve", lambda e, g2_=g2_, g0=g0: e.tensor_tensor(g2_[:], g0[:], g0[:], ALU.mult), reads=[g0k], writes=[g2k])
                    S.op("dve", lambda e, g1=g1, g2_=g2_: e.scalar_tensor_tensor(g1[:], g1[:], 1.0 / 64, g2_[:], ALU.mult, ALU.subtract), reads=[g1k, g2k], writes=[g1k])
                    S.op("dve", lambda e, g1=g1: e.tensor_scalar(g1[:], g1[:], 64e-5, None, ALU.add), reads=[g1k], writes=[g1k])
                    S.op("act", lambda e, g3=g3, g1=g1: e.activation(g3[:], g1[:], AF.Sqrt), reads=[g1k], writes=[g3k])
                    S.op("dve", lambda e, g3=g3: e.reciprocal(g3[:], g3[:]), reads=[g3k], writes=[g3k])
                    yn, ynk = ynp.get()
                    for c in range(8):
                        S.op("dve" if c % 2 else "pool", lambda e, yn=yn, yt=yt, c=c, g0=g0, g3=g3: e.tensor_scalar(yn[:, c, :], yt[:, c, :], g0[:, c:c + 1], g3[:, c:c + 1], ALU.subtract, ALU.mult),
                             reads=[(ytk, c), g0k, g3k], writes=[(ynk, c)])
                    pt2, pt2k = ptb.get()
                    p2v = pt2[:, 0:512].rearrange("p (c w) -> p c w", w=64)
                    for c in range(8):
                        for hb in (0, 64):
                            S.op("pe", lambda e, p2v=p2v, c=c, hb=hb, yn=yn: e.transpose(p2v[hb:hb + 64, c, :], yn[hb:hb + 64, c, :], identb[hb:hb + 64, hb:hb + 64]),
                                 reads=[(ynk, c), "identb"], writes=[(pt2k, c, hb)])
                    o1, o1k = f32p.get()
                    S.op("act", lambda e, o1=o1, pt2=pt2, cv=cv: e.activation(o1[:], pt2[:, 0:512], AF.Identity, bias=cv(6), scale=cv(5)),
                         reads=[(pt2k, c, hb) for c in range(8) for hb in (0, 64)] + ["chvt"], writes=[o1k])
                    S.op("pool", lambda e, o1=o1, bon=bon: e.tensor_tensor(o1[:], o1[:], bon[:], ALU.add), reads=[o1k, bonk], writes=[o1k])
                    dd("yn", yn[:], [128, 8, 64], BF16, [(ynk, c) for c in range(8)])
                    dd("o1", o1[:], [128, TB], F32, [o1k])
                    dd("bon", bon[:], [128, TB], F32, [bonk])
                    pgg, pggk = pg.get()
                    S.op("pe", lambda e, pgg=pgg, p=p: e.matmul(pgg[:], g2_b[:, p * 128:(p + 1) * 128], sg[:], start=True, stop=True), reads=["g2_b", sgk], writes=[pggk])
                    S.op("dve", lambda e, p=p, pgg=pgg, o1=o1: e.tensor_tensor(orw[:, p, :], pgg[:], o1[:], ALU.mult), reads=[pggk, o1k], writes=[(ork, p)])
                if stage in (12, 13):
                    continue
                qd = tb // 4
                tq = (tb % 4) * TB
                S.dma("sp", exv[qd, 64:320, tq:tq + TB].rearrange("(p c) t -> c p t", p=2), orw[:], reads=[(ork, 0), (ork, 1)], writes=[("ex_in", tb)])
            S.barrier()


        if stage >= 30 and mode != "C":
          with ExitStack() as esB:
            sb, ps = mk_alloc(esB)
            exv = ex_in.ap().rearrange("(q r) t -> q r t", q=4)
            qkv = sb("qkv", [128, 5, SEQ], BF16)
            for t5 in range(5):
                for hf in range(4):
                    S.dma("sp", qkv[:, t5, hf * 2048:(hf + 1) * 2048], qkv_scr[:, t5, hf * 2048:(hf + 1) * 2048], writes=[("qkv", t5, hf)])
            QKV = [("qkv", t5, hf) for t5 in range(5) for hf in range(4)]
            am = sb("am", [128, 256]); am0 = sb("am0", [128, 256])
            S.dma("sp", am[:], cin["amask"], writes=["am"]); S.dma("sp", am0[:], cin["amask0"], writes=["am0"])
            ps_s = RPool("ps_s", [ps("ps_s%d" % i, [128, 256]) for i in range(2)])
            ps_pt = RPool("ps_pt", [ps("ps_pt", [128, 2, 128], BF16)])
            ps_v = RPool("ps_v", [ps("ps_v", [128, 64], BF16)])
            ps_o = RPool("ps_o", [ps("ps_o%d" % i, [128, 64]) for i in range(2)])
            ps_m = RPool("ps_m", [ps("ps_m", [64, 4, 128], BF16)])
            for i in range(2):
                S.banks[("ps_s", i)] = "ps_s%d" % i
                S.banks[("ps_o", i)] = "ps_o%d" % i
            S.banks[("ps_pt", 0)] = "ps_pt"; S.banks[("ps_v", 0)] = "ps_v"; S.banks[("ps_m", 0)] = "ps_m"
            smp = RPool("sm", [sb("sm%d" % i, [128, 256]) for i in range(3)])
            pbp = RPool("pb", [sb("pb%d" % i, [128, 256], BF16) for i in range(3)])
            ptp = RPool("ptp", [sb("ptp%d" % i, [128, 2, 128], BF16) for i in range(3)])
            vtp = RPool("vt", [sb("vt%d" % i, [128, 64], BF16) for i in range(4)])
            stp = RPool("st66", [sb("st66_%d" % i, [128, 66]) for i in range(4)])
            sts = RPool("sts", [sb("sts%d" % i, [128, 2]) for i in range(4)])
            heads = [(0, 0, 2, 0, 4, 0), (0, 64, 2, 64, 4, 64), (1, 0, 3, 0, 1, 64)]
            for g, dil in enumerate((1, 4, 16)):
                qt_, qb_, kt_, kb_, vt_, vb_ = heads[g]
                nb = SEQ // (128 * dil)
                for r in range(dil):
                  vst = [None]

                  def blk_pipe(n):
                        q0 = 128 * n * dil + r
                        k0 = 128 * (n - 1) * dil + r if n > 0 else q0
                        qs = qkv[qb_:qb_ + 64, qt_, q0:q0 + 127 * dil + 1:dil]
                        if n > 0:
                            ks = qkv[kb_:kb_ + 64, kt_, k0:k0 + 255 * dil + 1:dil]
                        pS, pSk = ps_s.get()
                        if n > 0:
                            S.op("pe", lambda e: e.matmul(pS[:], qs, ks, start=True, stop=True), reads=QKV, writes=[pSk])
                        else:
                            kc = qkv[kb_:kb_ + 64, kt_, q0:q0 + 127 * dil + 1:dil]
                            S.op("pe", lambda e: e.matmul(pS[:, 0:128], qs, kc, start=True, stop=True), reads=QKV, writes=[(pSk, 0)])
                            S.op("pe", lambda e: e.matmul(pS[:, 128:256], qs, kc, start=True, stop=True), reads=QKV, writes=[(pSk, 1)])
                        mk_ = am if n > 0 else am0
                        sm, smk = smp.get()
                        S.op("dve", lambda e: e.tensor_tensor(sm[:], pS[:], mk_[:], ALU.add), reads=[pSk, (pSk, 0), (pSk, 1), "am", "am0"], writes=[smk])
                        st, stk = stp.get()
                        ss_, ssk_ = sts.get()
                        S.op("dve", lambda e: e.tensor_reduce(st[:, 64:65], sm[:], AX.X, ALU.max), reads=[smk], writes=[(stk, "m")])
                        S.op("dve", lambda e: e.tensor_scalar(ss_[:, 0:1], st[:, 64:65], -0.125, None, ALU.mult), reads=[(stk, "m")], writes=[ssk_])
                        pb, pbk = pbp.get()
                        S.op("act", lambda e: e.activation(pb[:], sm[:], AF.Exp, bias=ss_[:, 0:1], scale=0.125, accum_out=st[:, 65:66]), reads=[smk, ssk_], writes=[pbk, (stk, "l")])
                        yield
                        ppt, pptk = ps_pt.get()
                        for hf in range(2):
                            S.op("pe", lambda e, hf=hf: e.transpose(ppt[:, hf, :], pb[:, hf * 128:(hf + 1) * 128], identb[:]), reads=[pbk, "identb"], writes=[(pptk, hf)])
                        pts, ptsk = ptp.get()
                        S.op("act", lambda e: e.activation(pts[:], ppt[:], AF.Copy), reads=[(pptk, 0), (pptk, 1)], writes=[ptsk])
                        pv_, pvk = ps_v.get()
                        vsl = qkv[vb_:vb_ + 64, vt_, q0:q0 + 127 * dil + 1:dil]
                        S.op("pe", lambda e: e.transpose(pv_[:], vsl, identb[vb_:vb_ + 64, vb_:vb_ + 64]), reads=QKV + ["identb"], writes=[pvk])
                        vcur, vck = vtp.get()
                        S.op("dve", lambda e: e.tensor_copy(vcur[:], pv_[:]), reads=[pvk], writes=[vck])
                        if vst[0] is None:
                            vst[0] = (vcur, vck)
                        vp, vpk = vst[0]
                        po, pok = ps_o.get()
                        S.op("pe", lambda e: e.matmul(po[:], pts[:, 0, :], vp[:], start=True, stop=False), reads=[ptsk, vpk], writes=[pok])
                        S.op("pe", lambda e: e.matmul(po[:], pts[:, 1, :], vcur[:], start=False, stop=True), reads=[ptsk, vck], writes=[pok])
                        S.op("dve", lambda e: e.tensor_copy(st[:, 0:64], po[:]), reads=[pok], writes=[(stk, "o")])
                        S.dma("sp", att_scr[g, q0:q0 + 127 * dil + 1:dil, :], st[:], reads=[(stk, "o"), (stk, "m"), (stk, "l")], writes=[("att_scr", g, r, n)])
                        vst[0] = (vcur, vck)

                  act_ = []
                  for n in range(nb):
                      g_ = blk_pipe(n)
                      next(g_)
                      act_.append(g_)
                      if len(act_) > 1:
                          for _ in act_.pop(0):
                              pass
                  for g_ in act_:
                      for _ in g_:
                          pass
            S.barrier()
            mgp = RPool("mg", [sb("mg%d" % i, [128, 3, 66]) for i in range(3)])
            mws = RPool("mw", [sb("mw%d" % i, [128, 8]) for i in range(4)])
            mo = RPool("mo", [sb("mo%d" % i, [128, 64]) for i in range(3)])
            mob = RPool("mob", [sb("mob%d" % i, [128, 64], BF16) for i in range(3)])
            oat = RPool("oat", [sb("oat%d" % i, [64, 4, 128], BF16) for i in range(2)])
            for tb in range(NBLK):
                pm_, pmk_ = ps_m.get()
                for tt in range(4):
                    T0 = tb * TB + tt * 128
                    mg, mgk = mgp.get()
                    S.dma("sp", mg[:], att_scr[:, T0:T0 + 128, :].rearrange("g t c -> t g c"), writes=[mgk])
                    w, wk = mws.get()
                    S.op("dve", lambda e: e.tensor_reduce(w[:, 3:4], mg[:, :, 64], AX.X, ALU.max), reads=[mgk], writes=[wk])
                    S.op("dve", lambda e: e.tensor_scalar(w[:, 3:4], w[:, 3:4], -0.125, None, ALU.mult), reads=[wk], writes=[wk])
                    S.op("act", lambda e: e.activation(w[:, 0:3], mg[:, :, 64], AF.Exp, bias=w[:, 3:4], scale=0.125), reads=[mgk, wk], writes=[wk])
                    S.op("dve", lambda e: e.tensor_tensor(w[:, 4:7], w[:, 0:3], mg[:, :, 65], ALU.mult), reads=[wk, mgk], writes=[wk])
                    S.op("dve", lambda e: e.tensor_reduce(w[:, 7:8], w[:, 4:7], AX.X, ALU.add), reads=[wk], writes=[wk])
                    S.op("dve", lambda e: e.reciprocal(w[:, 7:8], w[:, 7:8]), reads=[wk], writes=[wk])
                    o_, ok_ = mo.get()
                    S.op("dve", lambda e: e.tensor_scalar(o_[:], mg[:, 0, 0:64], w[:, 0:1], None, ALU.mult), reads=[mgk, wk], writes=[ok_])
                    S.op("dve", lambda e: e.scalar_tensor_tensor(o_[:], mg[:, 1, 0:64], w[:, 1:2], o_[:], ALU.mult, ALU.add), reads=[mgk, wk, ok_], writes=[ok_])
                    S.op("dve", lambda e: e.scalar_tensor_tensor(o_[:], mg[:, 2, 0:64], w[:, 2:3], o_[:], ALU.mult, ALU.add), reads=[mgk, wk, ok_], writes=[ok_])
                    ob, obk = mob.get()
                    S.op("dve", lambda e: e.tensor_scalar(ob[:], o_[:], w[:, 7:8], None, ALU.mult), reads=[ok_, wk], writes=[obk])
                    S.op("pe", lambda e, tt=tt: e.transpose(pm_[:, tt, :], ob[:], identb[:]), reads=[obk, "identb"], writes=[(pmk_, tt)])
                oa, oak = oat.get()
                S.op("act", lambda e: e.activation(oa[:], pm_[:], AF.Copy), reads=[(pmk_, tt) for tt in range(4)], writes=[oak])
                qd = tb // 4
                tq = (tb % 4) * TB
                S.dma("sp", exv[qd, 0:64, tq:tq + TB], oa[:].rearrange("p a b -> p (a b)"), reads=[oak], writes=[("ex_in_a", tb)])
            S.barrier()

        if mode == "A":
            keys = S.all_semkeys()
            sems = {k: es_top.enter_context(nc.semaphore(k.replace("_", ""))) for k in keys}
            with nc.Block() as block:
                S.emit(block, sems)
            return nc
        if stage >= 40 and mode == "all":
            for qq in range(4):
                for i, (ra, rb) in enumerate(((0, 64), (64, 192), (192, 320))):
                    S.custom("pool", lambda e: e.collective_compute("AllGather", ALU.bypass, replica_groups=[[0, 1, 2, 3], [4, 5, 6, 7]],
                                                                    ins=[ex_in.ap()[320 * qq + ra:320 * qq + rb, :]], outs=[exg[qq][i].ap()]),
                             "cc", 1, reads=[], writes=[("exg", qq, i)])
            S.barrier()

        if stage >= 50 and mode != "A":
          with ExitStack() as esC:
            sbC, psC = mk_alloc(esC)
            for (sa, sb_) in ((2, 3), (4, 5)):
                with ExitStack() as es0:
                    sb0, ps0 = mk_alloc(es0)
                    compute_mod(sa, sb0, ps0, "m%d" % sa)
                    compute_mod(sb_, sb0, ps0, "m%d" % sb_)
                    S.barrier()
            s1c2 = sbC("s1c2", [128, 8]); s2c = sbC("s2c", [128, 8])
            S.op("dve", lambda e: e.scalar_tensor_tensor(s1c2[:], modc[:, 8:16], 1.0, gcol[:, 0, :], ALU.add, ALU.mult), writes=["s1c2"])
            S.op("dve", lambda e: e.scalar_tensor_tensor(s2c[:], modc[:, 32:40], 1.0, gcol[:, 1, :], ALU.add, ALU.mult), writes=["s2c"])
            sh1 = modc[:, 0:8]; sh2 = modc[:, 24:32]
            G = [sbC("G1", [128, D]), sbC("G2", [128, D])]
            with ExitStack() as es0:
                sb0, ps0 = mk_alloc(es0)
                ones = sb0("ones", [128, 128])
                S.op("pool", lambda e: e.memset(ones[:], 1.0), writes=["ones"])
                grow = sb0("grow", [128, 2 * D])
                S.dma("sp", grow[:], grows.partition_broadcast(128), writes=["grow"])
                crp = RPool("cr", [sb0("cr%d" % i, [128, 128]) for i in range(2)])
                pbc = ps0("pbc", [128, D])
                S.banks[("pbc",)] = "pbc"
                for gi, sec in enumerate((2, 5)):
                    for c in range(8):
                        cr, crk = crp.get()
                        S.op("dve", lambda e: e.tensor_scalar(cr[:], ones[:], modc[:, sec * 8 + c:sec * 8 + c + 1], None, ALU.mult), reads=["ones"], writes=[crk])
                        S.op("pe", lambda e: e.matmul(pbc[:, c * 128:(c + 1) * 128], cr[:], K["ident"][:], start=True, stop=True), reads=[crk], writes=[(("pbc",), c)])
                    S.op("dve", lambda e: e.tensor_tensor(G[gi][:], pbc[:], grow[:, gi * D:(gi + 1) * D], ALU.mult),
                         reads=[(("pbc",), c) for c in range(8)] + ["grow"], writes=[("G", gi)])
                S.barrier()

            def load_cast(dst_fn, src_fn, n, width, stg, tag):
                for i in range(n):
                    t_, k_ = stg.get()
                    S.dma("sp", t_[:, 0:width], src_fn(i), writes=[k_])
                    if i % 2:
                        S.op("pool", lambda e: e.tensor_copy(dst_fn(i), t_[:, 0:width]), reads=[k_], writes=[(tag, i)])
                    else:
                        S.op("act", lambda e: e.activation(dst_fn(i), t_[:, 0:width], AF.Copy), reads=[k_], writes=[(tag, i)])

            with ExitStack() as es1:
                sb, ps = mk_alloc(es1)
                stg = RPool("stg", [sb("stg%d" % i, [128, 2048]) for i in range(2)])
                wG = sb("wG", [128, 8, 2048], BF16); wbr = sb("wbr", [128, 10, D], BF16); wo = sb("wo", [128, 8, D], BF16)
                load_cast(lambda i: wG[:, i, :], lambda i: w_in_G[i * 128:(i + 1) * 128, :], 8, 2048, stg, "wG")
                load_cast(lambda i: wbr[:, i, :], lambda i: w_branch[i * 128:(i + 1) * 128, :], 10, D, stg, "wbr")
                load_cast(lambda i: wo[:, i, :], lambda i: w_out[i * 128:(i + 1) * 128, :], 8, D, stg, "wo")
                S.barrier()
                xt_pool = RPool("cxt", [sb("cxt%d" % i, [128, D]) for i in range(2)])
                xn_pool = RPool("cxn", [sb("cxn%d" % i, [128, D], BF16) for i in range(4)])
                junk = sb("cjunk", [128, D], BF16)
                stat_pool = RPool("cstat", [sb("cstat%d" % i, [128, 4]) for i in range(8)])
                hT = sb("chT", [128, 8, TB], BF16)
                sg = sb("csg", [128, 16, TB], BF16)
                oall = sb("oall", [128, 10, TB], BF16)
                candp = RPool("cand", [sb("cand%d" % i, [128, 10, TB], BF16) for i in range(2)])
                qs_t = sb("qs_t", [128, 4])
                S.dma("sp", qs_t[:], qsel, writes=["qs_t"])
                mT = sb("mT", [128, 8, TB], BF16)
                tmpp = RPool("ctmp", [sb("ctmp%d" % i, [128, TB]) for i in range(3)])
                big = RPool("cbig", [sb("cbig%d" % i, [128, D]) for i in range(2)])
                x1p = RPool("cx1", [sb("cx1_%d" % i, [128, D]) for i in range(2)])
                xn2p = RPool("cxn2", [sb("cxn2_%d" % i, [128, D], BF16) for i in range(2)])
                h2sp = RPool("ch2s", [sb("ch2s%d" % i, [128, 8, 128], BF16) for i in range(2)])
                pacc = RPool("pacc", [ps("pacc%d" % i, [128, TB]) for i in range(3)])
                pmix = RPool("pmix", [ps("pmix", [128, D])])
                ptx = RPool("cptx", [ps("cptx", [128, 4, 128], BF16)])
                pt8 = RPool("cpt8", [ps("cpt8", [128, 8, 128], BF16)])
                for i in range(3):
                    S.banks[("pacc", i)] = "pacc%d" % i
                S.banks[("pmix", 0)] = "pmix"; S.banks[("cptx", 0)] = "cptx"; S.banks[("cpt8", 0)] = "cpt8"
                exo = ex_out.ap().rearrange("(r q c) t -> r q c t", r=4, q=4)

                def rstd_of(st, sk):
                    S.op("dve", lambda e: e.tensor_scalar(st[:, 1:2], st[:, 0:1], 1.0 / D, 1e-6, ALU.mult, ALU.add), reads=[sk], writes=[sk])
                    S.op("act", lambda e: e.activation(st[:, 2:3], st[:, 1:2], AF.Sqrt), reads=[sk], writes=[sk])
                    S.op("dve", lambda e: e.reciprocal(st[:, 3:4], st[:, 2:3]), reads=[sk], writes=[sk])

                for tb in range(4):
                    tok0 = tb * TB
                    xns = []
                    for tt in range(4):
                        xt, xk = xt_pool.get()
                        S.dma("sp", xt[:], x_q[tok0 + tt * 128: tok0 + (tt + 1) * 128, :], writes=[xk])
                        st, sk = stat_pool.get()
                        S.op("act", lambda e: e.activation(junk[:], xt[:], AF.Square, accum_out=st[:, 0:1]), reads=[xk], writes=["cjunk", sk])
                        rstd_of(st, sk)
                        xn, nk = xn_pool.get()
                        S.op("dve", lambda e: e.tensor_scalar(xn[:], xt[:], st[:, 3:4], None, ALU.mult), reads=[xk, sk], writes=[nk])
                        xns.append((xn, nk))
                    for cch in range(8):
                        pt, pk = ptx.get()
                        for tt in range(4):
                            xn, nk = xns[tt]
                            S.op("pe", lambda e: e.transpose(pt[:, tt, :], xn[:, cch * 128:(cch + 1) * 128], identb[:]), reads=[nk, "identb"], writes=[(pk, tt)])
                        S.op("act", lambda e: e.activation(hT[:, cch, :], pt[:].rearrange("p a b -> p (a b)"), AF.Identity, bias=sh1[:, cch:cch + 1], scale=s1c2[:, cch:cch + 1]),
                             reads=[(pk, tt) for tt in range(4)] + ["s1c2"], writes=[("chT", cch)])
                    HK = [("chT", cch) for cch in range(8)]
                    for ct in range(16):
                        pa, pak = pacc.get()
                        for cch in range(8):
                            S.op("pe", lambda e: e.matmul(pa[:], wG[:, cch, ct * 128:(ct + 1) * 128], hT[:, cch, :], start=(cch == 0), stop=(cch == 7)), reads=[("chT", cch)], writes=[pak])
                        S.op("act", lambda e: e.activation(sg[:, ct, :], pa[:], AF.Sigmoid), reads=[pak], writes=[("csg", ct)])
                    for qq in range(4):
                        cd, cdk = candp.get()
                        for r in range(4):
                            if mode == "C":
                                S.dma("sp", cd[64 * (r % 2):64 * (r % 2) + 64, r // 2, :], exo[r, qq, 0:64, tok0:tok0 + TB], writes=[(cdk, "a", r)])
                                S.dma("sp", cd[:, 2 + 2 * r:4 + 2 * r, :], exo[r, qq, 64:320, tok0:tok0 + TB].rearrange("(p c) t -> c p t", p=2), writes=[(cdk, "r", r)])
                            else:
                                S.dma("sp", cd[64 * (r % 2):64 * (r % 2) + 64, r // 2, :], exg[qq][0].ap()[64 * r:64 * r + 64, tok0:tok0 + TB], writes=[(cdk, "a", r)])
                                for p in range(2):
                                    S.dma("sp", cd[:, 2 + 2 * r + p, :], exg[qq][1 + p].ap()[128 * r:128 * r + 128, tok0:tok0 + TB], writes=[(cdk, "r", r, p)])
                        CK = [(cdk, "a", r) for r in range(4)] + [(cdk, "r", r) for r in range(4)] + [(cdk, "r", r, p) for r in range(4) for p in range(2)]
                        if qq == 0:
                            S.op("dve", lambda e: e.tensor_scalar(oall[:], cd[:], qs_t[:, 0:1], None, ALU.mult), reads=CK + ["qs_t"], writes=["oall"])
                        else:
                            S.op("dve", lambda e: e.scalar_tensor_tensor(oall[:].rearrange("p a t -> p (a t)"), cd[:].rearrange("p a t -> p (a t)"), qs_t[:, qq:qq + 1],
                                                                         oall[:].rearrange("p a t -> p (a t)"), ALU.mult, ALU.add), reads=CK + ["qs_t", "oall"], writes=["oall"])
                    for ct in range(8):
                        pA, pAk = pacc.get()
                        for a in range(2):
                            S.op("pe", lambda e: e.matmul(pA[:], wbr[:, a, ct * 128:(ct + 1) * 128], oall[:, a, :], start=(a == 0), stop=(a == 1)),
                                 reads=["oall"], writes=[pAk])
                        t1, t1k = tmpp.get()
                        S.op("dve", lambda e: e.tensor_tensor(t1[:], pA[:], sg[:, ct, :], ALU.mult), reads=[pAk, ("csg", ct)], writes=[t1k])
                        pR, pRk = pacc.get()
                        for k in range(8):
                            S.op("pe", lambda e: e.matmul(pR[:], wbr[:, 2 + k, ct * 128:(ct + 1) * 128], oall[:, 2 + k, :], start=(k == 0), stop=(k == 7)), reads=["oall"], writes=[pRk])
                        t2, t2k = tmpp.get()
                        S.op("dve", lambda e: e.tensor_tensor(t2[:], pR[:], sg[:, 8 + ct, :], ALU.mult), reads=[pRk, ("csg", 8 + ct)], writes=[t2k])
                        S.op("pool", lambda e: e.tensor_tensor(mT[:, ct, :], t1[:], t2[:], ALU.add), reads=[t1k, t2k], writes=[("mT", ct)])
                    for tt in range(4):
                        T0 = tok0 + tt * 128
                        pm, pmk = pmix.get()
                        for half in range(2):
                            for k in range(8):
                                S.op("pe", lambda e: e.matmul(pm[:, half * 512:(half + 1) * 512], mT[:, k, tt * 128:(tt + 1) * 128], wo[:, k, half * 512:(half + 1) * 512],
                                                              start=(k == 0), stop=(k == 7)), reads=[("mT", k)], writes=[(pmk, half)])
                        st, sk = stat_pool.get()
                        S.op("act", lambda e: e.activation(junk[:], pm[:], AF.Square, accum_out=st[:, 0:1]), reads=[(pmk, 0), (pmk, 1)], writes=["cjunk", sk])
                        rstd_of(st, sk)
                        xt, xk = xt_pool.get()
                        S.dma("sp", xt[:], x_q[T0:T0 + 128, :], writes=[xk])
                        bg, bgk = big.get()
                        S.op("dve", lambda e: e.scalar_tensor_tensor(bg[:], pm[:], st[:, 3:4], G[0][:], ALU.mult, ALU.mult), reads=[(pmk, 0), (pmk, 1), sk, ("G", 0)], writes=[bgk])
                        x1, x1k = x1p.get()
                        S.op("pool", lambda e: e.tensor_tensor(x1[:], bg[:], xt[:], ALU.add), reads=[bgk, xk], writes=[x1k])
                        S.dma("sp", x1_scr[T0:T0 + 128, :], x1[:], reads=[x1k], writes=[("x1_scr", T0)])
                        st2, sk2 = stat_pool.get()
                        S.op("act", lambda e: e.activation(junk[:], x1[:], AF.Square, accum_out=st2[:, 0:1]), reads=[x1k], writes=["cjunk", sk2])
                        rstd_of(st2, sk2)
                        xn2, xn2k = xn2p.get()
                        S.op("dve", lambda e: e.tensor_scalar(xn2[:], x1[:], st2[:, 3:4], None, ALU.mult), reads=[x1k, sk2], writes=[xn2k])
                        p8, p8k = pt8.get()
                        for cch in range(8):
                            S.op("pe", lambda e: e.transpose(p8[:, cch, :], xn2[:, cch * 128:(cch + 1) * 128], identb[:]), reads=[xn2k, "identb"], writes=[(p8k, cch)])
                        h2s, h2k = h2sp.get()
                        for cch in range(8):
                            S.op("act", lambda e: e.activation(h2s[:, cch, :], p8[:, cch, :], AF.Identity, bias=sh2[:, cch:cch + 1], scale=s2c[:, cch:cch + 1]),
                                 reads=[(p8k, cch), "s2c"], writes=[(h2k, cch)])
                        S.dma("sp", h2_scr[:, :, T0:T0 + 128], h2s[:], reads=[(h2k, cch) for cch in range(8)], writes=[("h2_scr", T0)])
                S.barrier()

            with ExitStack() as es2:
                sb, ps = mk_alloc(es2)
                w1 = sb("w1", [128, 8, 4 * D], BF16); w2 = sb("w2", [128, 32, D], BF16)
                with ExitStack() as esw:
                    sbw, psw = mk_alloc(esw)
                    stg = RPool("stg2", [sbw("stgb%d" % i, [128, 2048]) for i in range(3)])
                    load_cast(lambda i: w1[:, i // 2, (i % 2) * 2048:(i % 2 + 1) * 2048], lambda i: w_ff1[(i // 2) * 128:(i // 2 + 1) * 128, (i % 2) * 2048:(i % 2 + 1) * 2048], 16, 2048, stg, "w1")
                    load_cast(lambda i: w2[:, 2 * i:2 * i + 2, :].rearrange("p a n -> p (a n)"), lambda i: w_ff2[i * 256:(i + 1) * 256, :].rearrange("(a p) n -> p a n", p=128), 16, 2048, stg, "w2")
                    S.barrier()
                BL = 256
                h2p = RPool("h2b", [sb("h2b%d" % i, [128, 8, BL], BF16) for i in range(2)])
                sqp = RPool("sq", [sb("sq%d" % i, [128, 32, BL], BF16) for i in range(1)])
                rp = RPool("rl", [sb("rl%d" % i, [128, BL]) for i in range(3)])
                x1p = RPool("dx1", [sb("dx1_%d" % i, [128, D]) for i in range(2)])
                big = RPool("dbig", [sb("dbig%d" % i, [128, D]) for i in range(2)])
                outp = RPool("dout", [sb("dout%d" % i, [128, D]) for i in range(2)])
                junk = sb("djunk", [128, D], BF16)
                stat_pool = RPool("dstat", [sb("dstat%d" % i, [128, 4]) for i in range(4)])
                pf = RPool("pf", [ps("pf%d" % i, [128, BL]) for i in range(3)])
                pmix = RPool("pmx", [ps("pmx%d" % i, [128, D]) for i in range(2)])
                for i in range(3):
                    S.banks[("pf", i)] = "pf%d" % i
                for i in range(2):
                    S.banks[("pmx", i)] = "pmx%d" % i
                OUTK = []
                for blk in range(2048 // BL):
                    b0 = blk * BL
                    h2b, h2bk = h2p.get()
                    S.dma("sp", h2b[:], h2_scr[:, :, b0:b0 + BL], writes=[h2bk])
                    sq, sqk = sqp.get()
                    for ft in range(32):
                        pp_, ppk = pf.get()
                        for k in range(8):
                            S.op("pe", lambda e: e.matmul(pp_[:], w1[:, k, ft * 128:(ft + 1) * 128], h2b[:, k, :], start=(k == 0), stop=(k == 7)), reads=[h2bk], writes=[ppk])
                        rl, rlk = rp.get()
                        S.op("act", lambda e: e.activation(rl[:], pp_[:], AF.Relu), reads=[ppk], writes=[rlk])
                        S.op("pool" if ft % 2 else "dve", lambda e: e.tensor_tensor(sq[:, ft, :], rl[:], rl[:], ALU.mult), reads=[rlk], writes=[(sqk, ft)])
                    for tt in range(BL // 128):
                        T0 = b0 + tt * 128
                        pm, pmk = pmix.get()
                        for half in range(2):
                            for k in range(32):
                                S.op("pe", lambda e: e.matmul(pm[:, half * 512:(half + 1) * 512], sq[:, k, tt * 128:(tt + 1) * 128], w2[:, k, half * 512:(half + 1) * 512],
                                                              start=(k == 0), stop=(k == 31)), reads=[(sqk, k)], writes=[(pmk, half)])
                        st, sk = stat_pool.get()
                        S.op("act", lambda e: e.activation(junk[:], pm[:], AF.Square, accum_out=st[:, 0:1]), reads=[(pmk, 0), (pmk, 1)], writes=["djunk", sk])
                        S.op("dve", lambda e: e.tensor_scalar(st[:, 1:2], st[:, 0:1], 1.0 / D, 1e-6, ALU.mult, ALU.add), reads=[sk], writes=[sk])
                        S.op("act", lambda e: e.activation(st[:, 2:3], st[:, 1:2], AF.Sqrt), reads=[sk], writes=[sk])
                        S.op("dve", lambda e: e.reciprocal(st[:, 3:4], st[:, 2:3]), reads=[sk], writes=[sk])
                        x1, x1k = x1p.get()
                        S.dma("sp", x1[:], x1_scr[T0:T0 + 128, :], writes=[x1k])
                        bg, bgk = big.get()
                        S.op("dve", lambda e: e.scalar_tensor_tensor(bg[:], pm[:], st[:, 3:4], G[1][:], ALU.mult, ALU.mult), reads=[(pmk, 0), (pmk, 1), sk], writes=[bgk])
                        ot, otk = outp.get()
                        S.op("pool", lambda e: e.tensor_tensor(ot[:], bg[:], x1[:], ALU.add), reads=[bgk, x1k], writes=[otk])
                        S.dma("sp", out_d[T0:T0 + 128, :], ot[:], reads=[otk], writes=[("out", T0)])
                        OUTK.append(("out", T0))
                S.final_wait("sp", OUTK)
            keys = S.all_semkeys()
            sems = {k: es_top.enter_context(nc.semaphore(k.replace("_", ""))) for k in keys}
            with nc.Block() as block:
                S.emit(block, sems)
            return nc
        if stage < 99:
            with ExitStack() as esd:
                sbd, psd = mk_alloc(esd)
                db = sbd("dbgbuf", [128, 2048], BF16)
                for qd in range(4):
                    for r0 in (0, 128, 256):
                        n = min(128, 320 - r0)
                        S.dma("sp", db[0:n, :], (ex_out if stage >= 40 else ex_in).ap().rearrange("(q r) t -> q r t", q=4)[qd, r0:r0 + n, :], writes=["db"])
                        S.dma("sp", dbg[qd, r0:r0 + n, :], db[0:n, :], reads=["db"], writes=[("dbg", qd, r0)])
                S.final_wait("sp", [("dbg", qd, r0) for qd in range(4) for r0 in (0, 128, 256)] + [("dbgout", n) for n in DBG])
            keys = S.all_semkeys()
            sems = {k: es_top.enter_context(nc.semaphore(k.replace("_", ""))) for k in keys}
            with nc.Block() as block:
                S.emit(block, sems)
            return nc
    return nc


def core_inputs(inputs, b, j):
    f = lambda a: np.ascontiguousarray(a)
    x = inputs["x"]; L = 0
    m = {}
    m["x_b"] = f(x[b])
    m["x_q"] = f(x[b, 2048 * j:2048 * (j + 1)])
    m["pos_b"] = f(inputs["positions"][b][None, :].astype(np.int32))
    m["c_col"] = f(inputs["c"][b].reshape(8, 128).T)
    m["ada_w"] = f(inputs["ada_w"][L])
    m["ada_bc"] = f(inputs["ada_b"][L].reshape(48, 128).T)
    col = lambda v: v.reshape(8, 128).T
    m["gcols"] = f(np.stack([col(inputs["norm_mix_pre"][L]), col(inputs["norm_ffn_pre"][L])], 1))
    m["grows"] = f(np.concatenate([inputs["norm_mix_post"][L], inputs["norm_ffn_post"][L]])[None, :])
    w_in = inputs["w_in"][L]
    ha = [4 * g + j for g in range(3)]
    qc = lambda h: np.arange(64 * h, 64 * h + 64)
    kc = lambda h: 768 + qc(h)
    vc = lambda h: 1536 + qc(h)
    R0 = 2304
    cols = [qc(ha[0]), qc(ha[1]), qc(ha[2]), vc(ha[2]), kc(ha[0]), kc(ha[1]), kc(ha[2]), kc(ha[2]), vc(ha[0]), vc(ha[1])]
    rw = []
    for sec in range(3):
        rw.append(R0 + sec * 1024 + 256 * j + np.arange(256))
    rw.append(R0 + 3072 + np.arange(256))
    cols = np.concatenate(cols + rw)
    assert cols.shape[0] == NTA * 128
    m["w_in_A"] = f(w_in[:, cols])
    m["w_in_G"] = f(w_in[:, R0 + 3328:R0 + 3328 + 2048])
    mu = inputs["shift_mu"][L][cols[640:] - R0]
    m["mu_A"] = f(mu.reshape(8, 128).T)
    my = 256 * j + np.arange(256)
    m["w2a"] = f(np.concatenate([inputs["decay_w2"][L][:, my], inputs["iclr_a2"][L][:, my]], 0))
    m["g2"] = f(inputs["gate_g2"][L][:, my])
    vecs = [inputs["decay_w0"][L], inputs["iclr_a0"][L], inputs["k_k"][L], inputs["k_a"][L], inputs["r_k"][L].reshape(-1),
            inputs["gn_w"][L], inputs["gn_b"][L], inputs["gn_b"][L]]
    chv = np.stack([v[my].reshape(2, 128) for v in vecs], -1)
    m["chv"] = f(chv.transpose(1, 0, 2))
    qs = np.zeros((128, 4), np.float32); qs[:, j] = 1.0
    m["qsel"] = qs
    m["w_branch"] = f(inputs["w_branch"][L])
    m["w_out"] = f(inputs["w_out"][L])
    m["w_ff1"] = f(inputs["w_ff1"][L])
    m["w_ff2"] = f(inputs["w_ff2"][L])
    for k, v in host_consts().items():
        m["k_" + k] = f(v)
    return m


def kernel(**inputs):
    inputs = {k: np.asarray(v) for k, v in inputs.items()}
    in_maps = [core_inputs(inputs, c // 4, c % 4) for c in range(8)]
    nc = build(99, "all")
    res = run_bass_kernel_spmd(nc, in_maps, core_ids=list(range(8)))
    out = np.zeros((2, SEQ, D), np.float32)
    for c in range(8):
        out[c // 4, 2048 * (c % 4):2048 * (c % 4 + 1)] = res.results[c]["out"]
    return out
```

```python
import os
import types
import numpy as np
from contextlib import ExitStack
SUB = int(os.environ.get('SUB', '99'))
import concourse.bass as bass
import concourse.mybir as mybir
from concourse.bass_utils import run_bass_kernel_spmd

F32 = mybir.dt.float32
BF16 = mybir.dt.bfloat16
I32 = mybir.dt.int32
AF = mybir.ActivationFunctionType
ALU = mybir.AluOpType
AX = mybir.AxisListType

NDSEM = 8
SEQ = 8192
D = 1024
NBLK = 16
TB = 512
MAGIC = 12582912.0
NTA = 13


def freeze(fn):
    if fn is None or fn.__closure__ is None:
        return fn
    cells = []
    for c in fn.__closure__:
        try:
            cells.append(types.CellType(c.cell_contents))
        except ValueError:
            cells.append(c)
    return types.FunctionType(fn.__code__, fn.__globals__, fn.__name__, fn.__defaults__, tuple(cells))


class Sched:
    ENGS = ("pe", "act", "dve", "pool", "sp")

    def __init__(self, nc, same_engine_sync=True):
        self.nc = nc
        self.same = same_engine_sync
        self.ops = {e: [] for e in self.ENGS}
        self.cnt = {e: 0 for e in self.ENGS}
        self.seen = {e: {} for e in self.ENGS}
        self.state = {}
        self.dcnt = {}
        self.maxval = {}
        self.banks = {}

    def _bank(self, key):
        if isinstance(key, tuple):
            if key in self.banks:
                return self.banks[key]
            for sub in key:
                bk = self._bank(sub)
                if bk:
                    return bk
        return None

    def _bankkeys(self, reads, writes):
        out = set()
        for k in list(reads) + list(writes):
            bk = self._bank(k)
            if bk:
                out.add(("BANK", bk))
        return list(out)

    def _need(self, eng, ev, waits, cross_only=False):
        if ev is None:
            return
        key, val, peng = ev
        if peng == eng and key.startswith("e_") and (not self.same or eng == "pe" or cross_only):
            return
        if self.seen[eng].get(key, 0) >= val:
            return
        self.seen[eng][key] = val
        waits[key] = max(waits.get(key, 0), val)

    def _deps(self, eng, reads, writes):
        waits = {}
        for b in self._bankkeys(reads, writes):
            st = self.state.get(b)
            if st is not None:
                self._need(eng, st[0], waits, cross_only=True)
        for b in reads:
            st = self.state.get(b)
            if st is not None:
                self._need(eng, st[0], waits)
        for b in writes:
            st = self.state.get(b)
            if st is not None:
                self._need(eng, st[0], waits)
                for ev in st[1]:
                    self._need(eng, ev, waits)
        return waits

    def _commit(self, ev, reads, writes):
        self.maxval[ev[0]] = (max(self.maxval.get(ev[0], (0, None))[0], ev[1]), ev[2])
        for b in reads:
            st = self.state.setdefault(b, [None, []])
            st[1].append(ev)
            if len(st[1]) > 24:
                d = {}
                for e in st[1]:
                    if e[0] not in d or d[e[0]][1] < e[1]:
                        d[e[0]] = e
                st[1] = list(d.values())
        for b in writes:
            self.state[b] = [ev, []]
        for b in self._bankkeys(reads, writes):
            self.state[b] = [ev, []]

    def op(self, eng, fn, reads=(), writes=()):
        waits = self._deps(eng, reads, writes)
        self.cnt[eng] += 1
        key = "e_" + eng
        ev = (key, self.cnt[eng], eng)
        self.ops[eng].append((list(waits.items()), freeze(fn), (key, 1)))
        self._commit(ev, reads, writes)
        return ev

    def dma(self, q, out, in_, reads=(), writes=(), **kw):
        waits = self._deps(q, reads, writes)
        i = self.dcnt.get(q, 0)
        self.dcnt[q] = i + 1
        key = "d_%s_%d" % (q, i % NDSEM)
        val = 16 * (i // NDSEM + 1)
        if i >= NDSEM:
            self._need(q, (key, val - 16, q), waits)
        ev = (key, val, q)

        def fn(e, out=out, in_=in_, kw=kw):
            return e.dma_start(out=out, in_=in_, **kw)

        self.ops[q].append((list(waits.items()), fn, (key, 16)))
        self._commit(ev, reads, writes)
        return ev

    def custom(self, eng, fn, semkey, inc, reads=(), writes=()):
        waits = self._deps(eng, reads, writes)
        c = self.dcnt.get(semkey, 0) + inc
        self.dcnt[semkey] = c
        ev = (semkey, c, eng)
        self.ops[eng].append((list(waits.items()), freeze(fn), (semkey, inc)))
        self._commit(ev, reads, writes)
        return ev

    def barrier(self):
        for e in self.ENGS:
            waits = {}
            for key, (val, peng) in self.maxval.items():
                self._need(e, (key, val, peng), waits)
            if waits:
                self.ops[e].append((list(waits.items()), None, None))
        self.state = {}

    def final_wait(self, eng, bufs):
        waits = {}
        for b in bufs:
            st = self.state.get(b)
            if st is not None:
                self._need(eng, st[0], waits)
        self.ops[eng].append((list(waits.items()), None, None))

    def all_semkeys(self):
        keys = set()
        for e in self.ENGS:
            for waits, fn, inc in self.ops[e]:
                for k, _ in waits:
                    keys.add(k)
                if inc is not None:
                    keys.add(inc[0])
        return sorted(keys)

    def emit(self, block, sems):
        def mk(eng_name):
            def body(e):
                for waits, fn, inc in self.ops[eng_name]:
                    for k, v in waits:
                        e.wait_ge(sems[k], v)
                    if fn is None:
                        continue
                    ins = fn(e)
                    ins.then_inc(sems[inc[0]], inc[1])
            return body

        block.tensor(mk("pe"))
        block.scalar(mk("act"))
        block.vector(mk("dve"))
        block.gpsimd(mk("pool"))
        block.sync(mk("sp"))


class RPool:
    def __init__(self, name, tiles):
        self.name, self.tiles, self.i = name, tiles, 0

    def get(self):
        k = self.i % len(self.tiles)
        self.i += 1
        return self.tiles[k], (self.name, k)


def host_consts():
    p = np.arange(128)
    c = {}
    c["ident"] = np.eye(128, dtype=np.float32)
    r = (p % 64)[:, None]
    n = np.arange(64)[None, :]
    su = (n > r).astype(np.float32)
    ui = (n >= r).astype(np.float32)
    sl = (n < r).astype(np.float32)
    c["mask1"] = np.concatenate([su, ui], 1)
    c["masksl"] = sl
    c["identh"] = (n == r).astype(np.float32)
    c["blockones"] = ((p[:, None] // 64) == (p[None, :] // 64)).astype(np.float32)
    perm = np.zeros((128, 128), np.float32)
    for b in (0, 64):
        for i in range(8):
            perm[b + 8 + i, b + i] = -1.0
            perm[b + i, b + 8 + i] = 1.0
    c["perm"] = perm
    inv = 500000.0 ** (-(np.arange(8, dtype=np.float32)) * 2.0 / 16.0)
    inv2pi = np.zeros((128, 1), np.float32)
    for b in (0, 64):
        for i in range(16):
            inv2pi[b + i, 0] = np.float32(inv[i % 8])
    c["inv2pi"] = inv2pi
    seg = np.ones((128, 512), np.float32)
    seg[:, ::64] = 0.0
    c["segmask"] = seg
    qi = np.arange(128)[:, None]
    kj = np.arange(256)[None, :]
    lag = qi + 128 - kj
    band = (lag >= 0) & (lag <= 128)
    c["amask"] = np.where(band, 0.0, -1e30).astype(np.float32)
    c["amask0"] = np.where(band & (kj >= 128), 0.0, -1e30).astype(np.float32)
    return c


CONST_SHAPES = {"ident": [128, 128], "mask1": [128, 128], "masksl": [128, 64], "identh": [128, 64],
                "blockones": [128, 128], "perm": [128, 128], "inv2pi": [128, 1], "segmask": [128, 512],
                }
ACONST_SHAPES = {"amask": [128, 256], "amask0": [128, 256]}


def build(stage=99, mode="all"):
    nc = bass.Bass("TRN2", target_bir_lowering=False)

    def din(name, shape, dt=F32):
        return nc.dram_tensor(name, shape, dt, kind="ExternalInput").ap()

    x_b = din("x_b", [SEQ, D])
    x_q = din("x_q", [2048, D])
    pos_b = din("pos_b", [1, SEQ], I32)
    c_col = din("c_col", [128, 8])
    ada_w = din("ada_w", [D, 6 * D])
    ada_bc = din("ada_bc", [128, 48])
    gcols = din("gcols", [128, 2, 8])
    grows = din("grows", [1, 2 * D])
    w_in_A = din("w_in_A", [D, NTA * 128])
    w_in_G = din("w_in_G", [D, 2048])
    mu_A = din("mu_A", [128, 8])
    w2a = din("w2a", [128, 256])
    g2 = din("g2", [128, 256])
    qsel = din("qsel", [128, 4])
    chv = din("chv", [128, 2, 8])
    w_branch = din("w_branch", [1280, D])
    w_out = din("w_out", [D, D])
    w_ff1 = din("w_ff1", [D, 4 * D])
    w_ff2 = din("w_ff2", [4 * D, D])
    cin = {k: din("k_" + k, v) for k, v in list(CONST_SHAPES.items()) + list(ACONST_SHAPES.items())}
    out_d = nc.dram_tensor("out", [2048, D], F32, kind="ExternalOutput").ap()
    dbg = None
    if stage < 99:
        dbg = nc.dram_tensor("dbg", [4, 320, 2048], BF16, kind="ExternalOutput").ap()
    qkv_scr = nc.dram_tensor("qkv_scr", [128, 5, SEQ], BF16).ap()
    ex_in = nc.dram_tensor("ex_in", [1280, 2048], BF16, **({"kind": "ExternalOutput"} if mode == "A" else {}))
    ex_out = nc.dram_tensor("ex_out", [4 * 1280, 2048], BF16, **({"kind": "ExternalInput"} if mode == "C" else {}))
    h2_scr = nc.dram_tensor("h2_scr", [128, 8, 2048], BF16).ap()
    exg = [[nc.dram_tensor("exg%d_%d" % (qq, i), [4 * (64 if i == 0 else 128), 2048], BF16) for i in range(3)] for qq in range(4)]
    att_scr = nc.dram_tensor("att_scr", [3, SEQ, 66], F32).ap()
    x1_scr = nc.dram_tensor("x1_scr", [2048, D], F32).ap()

    S = Sched(nc)
    global LAST_SCHED
    LAST_SCHED = S
    es_top = ExitStack()
    DBG = {}

    def dd(name, ap, shape, dt, rk):
        if stage >= 99 or name in DBG:
            return
        t = nc.dram_tensor("dbg_" + name, shape, dt, kind="ExternalOutput").ap()
        DBG[name] = t
        S.dma("sp", t, ap, reads=rk, writes=[("dbgout", name)])
    with es_top:
        def mk_alloc(es):
            def sb(name, shape, dt=F32):
                return es.enter_context(nc.sbuf_tensor(name, shape, dt))

            def ps(name, shape, dt=F32):
                return es.enter_context(nc.psum_tensor(name, shape, dt))
            return sb, ps

        sbT, psT = mk_alloc(es_top)
        K = {}
        for k, shp in CONST_SHAPES.items():
            K[k] = sbT("c_" + k, shp)
            S.dma("sp", K[k][:], cin[k], writes=["c_" + k])
        identb = sbT("identb", [128, 128], BF16)
        S.op("dve", lambda e: e.tensor_copy(identb[:], K["ident"][:]), reads=["c_ident"], writes=["identb"])
        ccol = sbT("ccol", [128, 8])
        scol = sbT("scol", [128, 8])
        S.dma("sp", ccol[:], c_col, writes=["ccol"])
        S.op("act", lambda e: e.activation(scol[:], ccol[:], AF.Silu), reads=["ccol"], writes=["scol"])
        adab = sbT("adab", [128, 48])
        S.dma("sp", adab[:], ada_bc, writes=["adab"])
        gcol = sbT("gcol", [128, 2, 8])
        S.dma("sp", gcol[:], gcols, writes=["gcol"])
        modc = sbT("modc", [128, 48])

        def compute_mod(sec, sbl, psl, tagp):
            slab = sbl("adaslab" + tagp, [128, 8, 1024])
            for cch in range(8):
                S.dma("sp", slab[:, cch, :], ada_w[cch * 128:(cch + 1) * 128, sec * 1024:(sec + 1) * 1024],
                      writes=[("slab", tagp, cch)])
            pm = psl("pmod" + tagp, [128, 8])
            S.banks[("pmod", tagp)] = "pmod" + tagp
            for t in range(8):
                for cch in range(8):
                    S.op("pe", lambda e, t=t, cch=cch: e.matmul(pm[:, t:t + 1], slab[:, cch, t * 128:(t + 1) * 128],
                                                               scol[:, cch:cch + 1], start=(cch == 0), stop=(cch == 7)),
                         reads=[("slab", tagp, cch), "scol"], writes=[(("pmod", tagp), t)])
            S.op("dve", lambda e: e.tensor_tensor(modc[:, 8 * sec:8 * sec + 8], pm[:], adab[:, 8 * sec:8 * sec + 8], ALU.add),
                 reads=[(("pmod", tagp), t) for t in range(8)] + ["adab"], writes=[("modc", sec)])

        with ExitStack() as esA:
            sb, ps = mk_alloc(esA)
            with ExitStack() as es0:
                sb0, ps0 = mk_alloc(es0)
                compute_mod(0, sb0, ps0, "a")
                compute_mod(1, sb0, ps0, "b")
                S.barrier()
            s1c = sb("s1c", [128, 8])
            S.op("dve", lambda e: e.scalar_tensor_tensor(s1c[:], modc[:, 8:16], 1.0, gcol[:, 0, :], ALU.add, ALU.mult),
                 reads=[("modc", 1), "gcol"], writes=["s1c"])
            sh1 = modc[:, 0:8]

            wA = sb("wA", [128, 8, NTA * 128], BF16)
            with ExitStack() as esw:
                sbw, psw = mk_alloc(esw)
                wstage = RPool("wstage", [sbw("wstg%d" % i, [128, NTA * 128]) for i in range(2)])
                for cch in range(8):
                    t_, k_ = wstage.get()
                    S.dma("sp", t_[:], w_in_A[cch * 128:(cch + 1) * 128, :], writes=[k_])
                    S.op("pool" if cch % 2 else "act",
                         (lambda e, t_=t_, cch=cch: e.tensor_copy(wA[:, cch, :], t_[:])) if cch % 2 else
                         (lambda e, t_=t_, cch=cch: e.activation(wA[:, cch, :], t_[:], AF.Copy)),
                         reads=[k_], writes=[("wA", cch)])
                S.barrier()
            muA = sb("muA", [128, 8])
            S.dma("sp", muA[:], mu_A, writes=["muA"])
            w2a_f = sb("w2a_f", [128, 256]); g2_f = sb("g2_f", [128, 256])
            w2a_b = sb("w2a_b", [128, 256], BF16); g2_b = sb("g2_b", [128, 256], BF16)
            S.dma("sp", w2a_f[:], w2a, writes=["w2a_f"]); S.dma("sp", g2_f[:], g2, writes=["g2_f"])
            S.op("dve", lambda e: e.tensor_copy(w2a_b[:], w2a_f[:]), reads=["w2a_f"], writes=["w2a_b"])
            S.op("dve", lambda e: e.tensor_copy(g2_b[:], g2_f[:]), reads=["g2_f"], writes=["g2_b"])
            chvt = sb("chvt", [128, 2, 8])
            S.dma("sp", chvt[:], chv, writes=["chvt"])
            omka = sb("omka", [128, 2])
            S.op("dve", lambda e: e.tensor_scalar(omka[:], chvt[:, :, 3], -1.0, 1.0, ALU.mult, ALU.add), reads=["chvt"], writes=["omka"])
            bones_b = sb("bones_b", [128, 128], BF16); perm_b = sb("perm_b", [128, 128], BF16)
            S.op("dve", lambda e: e.tensor_copy(bones_b[:], K["blockones"][:]), reads=["c_blockones"], writes=["bones_b"])
            S.op("dve", lambda e: e.tensor_copy(perm_b[:], K["perm"][:]), reads=["c_perm"], writes=["perm_b"])
            mask1x4 = sb("mask1x4", [128, 4, 128]); maskslx4 = sb("maskslx4", [128, 4, 64]); identhx4 = sb("identhx4", [128, 4, 64])
            for ch in range(4):
                S.op("pool", lambda e, ch=ch: e.tensor_copy(mask1x4[:, ch, :], K["mask1"][:]), reads=["c_mask1"], writes=[("m1", ch)])
                S.op("pool", lambda e, ch=ch: e.tensor_copy(maskslx4[:, ch, :], K["masksl"][:]), reads=["c_masksl"], writes=[("msl", ch)])
                S.op("pool", lambda e, ch=ch: e.tensor_copy(identhx4[:, ch, :], K["identh"][:]), reads=["c_identh"], writes=[("idh", ch)])
            MK1 = [("m1", ch) for ch in range(4)]; MKSL = [("msl", ch) for ch in range(4)]; IDH = [("idh", ch) for ch in range(4)]

            xt_pool = RPool("xt", [sb("xt%d" % i, [128, D]) for i in range(2)])
            xn_pool = RPool("xn", [sb("xn%d" % i, [128, D], BF16) for i in range(4)])
            junk = sb("junk", [128, D], BF16)
            stat_pool = RPool("stat", [sb("stat%d" % i, [128, 4]) for i in range(8)])
            hT_pool = RPool("hT", [sb("hT%d" % i, [128, 8, TB], BF16) for i in range(1)])
            raw = [sb("raw%d" % i, [128, TB + 1]) for i in range(8)]
            for i in range(8):
                S.op("pool", lambda e, i=i: e.memset(raw[i][:], 0.0), writes=[("raw", i)])
            qst_pool = RPool("qst", [sb("qst%d" % i, [128, 5, TB], BF16) for i in range(1)])
            f32p = RPool("f", [sb("f%d" % i, [128, TB]) for i in range(22)])
            b16p = RPool("b", [sb("b%d" % i, [128, TB], BF16) for i in range(10)])
            posi = sb("posi", [128, TB], I32)
            AR = [RPool("AR%d" % p, [sb("AR%d_%d" % (p, i), [128, 8, 128], BF16) for i in range(1)]) for p in range(2)]
            KB = [RPool("KB%d" % p, [sb("KB%d_%d" % (p, i), [128, 8, 128], BF16) for i in range(1)]) for p in range(2)]
            gkp = RPool("Gk", [sb("Gk%d" % i, [128, 4, 128], BF16) for i in range(2)])
            gbp = RPool("Gb", [sb("Gb%d" % i, [128, 4, 128], BF16) for i in range(2)])
            glp = RPool("GL", [sb("GL%d" % i, [128, 4, 64], BF16) for i in range(2)])
            lmp = RPool("LM", [sb("LM%d" % i, [128, 2, 4, 64], BF16) for i in range(6)])
            xp = RPool("X", [sb("X%d" % i, [128, 4, 64], BF16) for i in range(6)])
            tmp_ = RPool("TM", [sb("TM%d" % i, [128, 4, 4, 64], BF16) for i in range(2)])
            zsp = RPool("Zs", [sb("Zs%d" % i, [128, 4, 128], BF16) for i in range(2)])
            aup = RPool("AU", [sb("AU%d" % i, [128, 4, 128], BF16) for i in range(2)])
            mctp = RPool("McT", [sb("McT%d" % i, [128, 4, 64], BF16) for i in range(2)])
            ncwp = RPool("NcW", [sb("NcW%d" % i, [128, 4, 64]) for i in range(2)])
            rhtp = RPool("RhT", [sb("RhT%d" % i, [128, 4, 64], BF16) for i in range(2)])
            Sb = [RPool("Sb%d" % p, [sb("Sb%d_%d" % (p, i), [128, 64], BF16) for i in range(3)]) for p in range(2)]
            ytm = [RPool("Ytm%d" % p, [sb("Ytm%d_%d" % (p, i), [128, 8, 64]) for i in range(1)]) for p in range(2)]
            ysq = sb("ysq", [128, 8, 64])
            ynp = RPool("yn", [sb("yn%d" % i, [128, 8, 64], BF16) for i in range(2)])
            gst = RPool("gst", [sb("gst%d" % i, [128, 8]) for i in range(8)])
            orw_pool = RPool("orw", [sb("orw%d" % i, [128, 2, TB], BF16) for i in range(2)])
            pp = RPool("pp", [ps("pp%d" % i, [128, TB]) for i in range(2)])
            ptx = RPool("ptx", [ps("ptx", [128, 4, 128], BF16)])
            ptb = RPool("ptb", [ps("ptb%d" % i, [128, 1024], BF16) for i in range(1)])
            pg = RPool("pg", [ps("pg%d" % i, [128, TB]) for i in range(3)])
            pys_t = ps("pys", [128, 8, 64])
            pys = RPool("pys", [pys_t[:, i, :] for i in range(8)])
            for i in range(2):
                S.banks[("pp", i)] = "pp%d" % i
            S.banks[("ptx", 0)] = "ptx"
            S.banks[("ptb", 0)] = "ptb"
            for i in range(3):
                S.banks[("pg", i)] = "pg%d" % i
            for i in range(8):
                S.banks[("pys", i)] = "pys"

            sstate = []
            for p in range(2):
                t_, k_ = Sb[p].get()
                S.op("pool", lambda e, t_=t_: e.memset(t_[:], 0.0), writes=[k_])
                sstate.append((t_, k_))

            exv = ex_in.ap().rearrange("(q r) t -> q r t", q=4)

            for tb in range((NBLK if stage >= 50 else 4) if mode != "C" else 0):
                if stage == 9:
                    break
                tok0 = tb * TB
                xns = []
                for tt in range(4):
                    xt, xk = xt_pool.get()
                    S.dma("sp", xt[:], x_b[tok0 + tt * 128: tok0 + (tt + 1) * 128, :], writes=[xk])
                    st, sk = stat_pool.get()
                    S.op("act", lambda e, xt=xt, st=st: e.activation(junk[:], xt[:], AF.Square, accum_out=st[:, 0:1]),
                         reads=[xk], writes=["junk", sk])
                    if SUB <= 1:
                        continue
                    S.op("dve", lambda e, st=st: e.tensor_scalar(st[:, 1:2], st[:, 0:1], 1.0 / D, 1e-6, ALU.mult, ALU.add), reads=[sk], writes=[sk])
                    S.op("act", lambda e, st=st: e.activation(st[:, 2:3], st[:, 1:2], AF.Sqrt), reads=[sk], writes=[sk])
                    S.op("dve", lambda e, st=st: e.reciprocal(st[:, 3:4], st[:, 2:3]), reads=[sk], writes=[sk])
                    if SUB <= 2:
                        continue
                    xn, nk = xn_pool.get()
                    S.op("dve", lambda e, xn=xn, xt=xt, st=st: e.tensor_scalar(xn[:], xt[:], st[:, 3:4], None, ALU.mult),
                         reads=[xk, sk], writes=[nk])
                    xns.append((xn, nk))
                if SUB <= 3:
                    continue
                hT, hk = hT_pool.get()
                for cch in range(8):
                    pt, pk = ptx.get()
                    for tt in range(4):
                        xn, nk = xns[tt]
                        S.op("pe", lambda e, pt=pt, xn=xn, tt=tt, cch=cch: e.transpose(pt[:, tt, :], xn[:, cch * 128:(cch + 1) * 128], identb[:]),
                             reads=[nk, "identb"], writes=[(pk, tt)])
                    if SUB <= 4:
                        continue
                    S.op("act", lambda e, pt=pt, hT=hT, cch=cch: e.activation(hT[:, cch, :], pt[:].rearrange("p a b -> p (a b)"), AF.Identity,
                                                                          bias=sh1[:, cch:cch + 1], scale=s1c[:, cch:cch + 1]),
                         reads=[(pk, tt) for tt in range(4)] + ["s1c", ("modc", 0)], writes=[(hk, cch)])
                HK = [(hk, cch) for cch in range(8)]
                dd("hT", hT[:, 0, :], [128, TB], BF16, HK)

                def proj(tile):
                    pt, pk = pp.get()
                    for cch in range(8):
                        S.op("pe", lambda e, pt=pt, cch=cch, tile=tile: e.matmul(pt[:], wA[:, cch, tile * 128:(tile + 1) * 128], hT[:, cch, :],
                                                                                start=(cch == 0), stop=(cch == 7)),
                             reads=[("wA", cch), (hk, cch)], writes=[pk])
                    return pt, pk

                if stage == 10:
                    continue
                S.dma("sp", posi[:], pos_b[:, tok0:tok0 + TB].partition_broadcast(128), writes=["posi"])
                posf, pfk = f32p.get()
                S.op("dve", lambda e, posf=posf: e.tensor_copy(posf[:], posi[:]), reads=["posi"], writes=[pfk])
                ysn, ysk = f32p.get()
                S.op("dve", lambda e, ysn=ysn, posf=posf: e.tensor_scalar(ysn[:], posf[:], K["inv2pi"][:, 0:1], float(1.0 / (2 * np.pi)), ALU.mult, ALU.mult),
                     reads=[pfk, "c_inv2pi"], writes=[ysk])
                ycs, yck = f32p.get()
                S.op("pool", lambda e, ycs=ycs, ysn=ysn: e.tensor_scalar(ycs[:], ysn[:], 0.25, None, ALU.add), reads=[ysk], writes=[yck])
                tabs = []
                for (yy, yk) in ((ysn, ysk), (ycs, yck)):
                    kk_, kkk = f32p.get()
                    S.op("dve", lambda e, kk_=kk_, yy=yy: e.tensor_scalar(kk_[:], yy[:], MAGIC, MAGIC, ALU.add, ALU.subtract), reads=[yk], writes=[kkk])
                    S.op("dve", lambda e, kk_=kk_, yy=yy: e.tensor_tensor(kk_[:], yy[:], kk_[:], ALU.subtract), reads=[yk, kkk], writes=[kkk])
                    tb_, tk_ = f32p.get()
                    S.op("act", lambda e, tb_=tb_, kk_=kk_: e.activation(tb_[:], kk_[:], AF.Sin, scale=float(2 * np.pi)), reads=[kkk], writes=[tk_])
                    tabs.append((tb_, tk_))
                (SS, ssk), (CC, cck) = tabs

                qst, qk = qst_pool.get()
                for tile in range(5):
                    pt, pk = proj(tile)
                    if tile == 4:
                        S.op("act", lambda e, pt=pt, tile=tile: e.activation(qst[:, tile, :], pt[:], AF.Copy), reads=[pk], writes=[(qk, tile)])
                        continue
                    qb, qbk = b16p.get()
                    S.op("act", lambda e, pt=pt, qb=qb: e.activation(qb[:], pt[:], AF.Copy), reads=[pk], writes=[qbk])
                    nrow = 128 if tile in (0, 2) else 64
                    pq, pqk = pg.get()
                    S.op("pe", lambda e, pq=pq, qb=qb: e.matmul(pq[:], perm_b[:], qb[:], start=True, stop=True), reads=["perm_b", qbk], writes=[pqk])
                    t1, t1k = f32p.get()
                    S.op("dve", lambda e, t1=t1, qb=qb, nrow=nrow: e.tensor_tensor(t1[0:nrow, :], qb[0:nrow, :], CC[0:nrow, :], ALU.mult), reads=[qbk, cck], writes=[t1k])
                    t2, t2k = f32p.get()
                    S.op("dve", lambda e, t2=t2, pq=pq, nrow=nrow: e.tensor_tensor(t2[0:nrow, :], pq[0:nrow, :], SS[0:nrow, :], ALU.mult), reads=[pqk, ssk], writes=[t2k])
                    S.op("pool", lambda e, t1=t1, t2=t2, nrow=nrow, tile=tile: e.tensor_tensor(qst[0:nrow, tile, :], t1[0:nrow, :], t2[0:nrow, :], ALU.add),
                         reads=[t1k, t2k], writes=[(qk, tile)])
                    if nrow == 64:
                        S.op("pool", lambda e, qb=qb, tile=tile: e.tensor_copy(qst[64:128, tile, :], qb[64:128, :]), reads=[qbk], writes=[(qk, tile, "b")])
                S.dma("pool", qkv_scr[:, :, tok0:tok0 + TB], qst[:], reads=[(qk, t) for t in range(5)] + [(qk, 1, "b"), (qk, 3, "b")], writes=[("qkv_scr", tb)])

                if stage == 11:
                    continue
                for i in range(8):
                    pt, pk = proj(5 + i)
                    S.op("pool", lambda e, i=i: e.tensor_copy(raw[i][:, 0:1], raw[i][:, TB:TB + 1]), reads=[("raw", i)], writes=[("raw", i)])
                    S.op("act", lambda e, i=i, pt=pt: e.activation(raw[i][:, 1:TB + 1], pt[:], AF.Copy), reads=[pk], writes=[("raw", i)])

                def shift(i, eng="dve"):
                    d_, dk = f32p.get()
                    S.op("pool", lambda e, d_=d_, i=i: e.tensor_tensor(d_[:], raw[i][:, 0:TB], raw[i][:, 1:TB + 1], ALU.subtract), reads=[("raw", i)], writes=[dk])
                    o_, ok = f32p.get()
                    S.op("dve", lambda e, d_=d_, o_=o_, i=i: e.scalar_tensor_tensor(o_[:], d_[:], muA[:, i:i + 1], raw[i][:, 1:TB + 1], ALU.mult, ALU.add),
                         reads=[dk, ("raw", i), "muA"], writes=[ok])
                    return o_, ok

                lora, lok = shift(6)
                dgs, dgk = shift(7)
                tdw, tdk = b16p.get()
                S.op("act", lambda e, tdw=tdw, lora=lora: e.activation(tdw[0:64, :], lora[0:64, :], AF.Tanh), reads=[lok], writes=[(tdk, 0)])
                S.op("act", lambda e, tdw=tdw, lora=lora: e.activation(tdw[64:128, :], lora[64:128, :], AF.Copy), reads=[lok], writes=[(tdk, 1)])
                sg, sgk = b16p.get()
                S.op("act", lambda e, sg=sg, dgs=dgs: e.activation(sg[:], dgs[:], AF.Sigmoid), reads=[dgk], writes=[sgk])
                dd("lora", lora[:], [128, TB], F32, [lok])
                dd("tdw", tdw[:], [128, TB], BF16, [(tdk, 0), (tdk, 1)])
                dd("w2a_b", w2a_b[:], [128, 256], BF16, ["w2a_b"])

                orw, ork = orw_pool.get()
                for p in range(2):
                    cv = lambda j, p=p: chvt[:, p, j:j + 1]
                    r_s, rk_ = shift(0 + p)
                    k_s, kk_ = shift(2 + p)
                    v_s, vk_ = shift(4 + p)
                    dd("raw0", raw[0][:, 0:TB], [128, TB], F32, [("raw", 0)])
                    dd("r_s", r_s[:], [128, TB], F32, [rk_])
                    dd("k_s", k_s[:], [128, TB], F32, [kk_])
                    pz, pzk = pg.get()
                    S.op("pe", lambda e, pz=pz, p=p: e.matmul(pz[:], w2a_b[0:64, p * 128:(p + 1) * 128], tdw[0:64, :], start=True, stop=True), reads=["w2a_b", (tdk, 0)], writes=[pzk])
                    lw, lwk = f32p.get()
                    S.op("act", lambda e, lw=lw, pz=pz, cv=cv: e.activation(lw[:], pz[:], AF.Sigmoid, bias=cv(0)), reads=[pzk, "chvt"], writes=[lwk])
                    S.op("pool", lambda e, lw=lw: e.tensor_scalar(lw[:], lw[:], float(-np.exp(-0.5)), None, ALU.mult), reads=[lwk], writes=[lwk])
                    pa, pak = pg.get()
                    S.op("pe", lambda e, pa=pa, p=p: e.matmul(pa[:], w2a_b[64:128, p * 128:(p + 1) * 128], tdw[64:128, :], start=True, stop=True), reads=["w2a_b", (tdk, 1)], writes=[pak])
                    icl, ick = f32p.get()
                    S.op("act", lambda e, icl=icl, pa=pa, cv=cv: e.activation(icl[:], pa[:], AF.Sigmoid, bias=cv(1)), reads=[pak, "chvt"], writes=[ick])
                    cw, cwk = f32p.get()
                    S.op("dve", lambda e, cw=cw, lw=lw: e.tensor_tensor_scan(cw[:], K["segmask"][:], lw[:], 0.0, ALU.mult, ALU.add), reads=["c_segmask", lwk], writes=[cwk])
                    cwe, cwek = f32p.get()
                    S.op("pool", lambda e, cwe=cwe, cw=cw, lw=lw: e.tensor_tensor(cwe[:], cw[:], lw[:], ALU.subtract), reads=[cwk, lwk], writes=[cwek])
                    Winc, wik = f32p.get(); Winv, wvk = f32p.get(); Wexc, wek = f32p.get()
                    S.op("act", lambda e, Winc=Winc, cw=cw: e.activation(Winc[:], cw[:], AF.Exp), reads=[cwk], writes=[wik])
                    S.op("act", lambda e, Winv=Winv, cw=cw: e.activation(Winv[:], cw[:], AF.Exp, scale=-1.0), reads=[cwk], writes=[wvk])
                    S.op("act", lambda e, Wexc=Wexc, cwe=cwe: e.activation(Wexc[:], cwe[:], AF.Exp), reads=[cwek], writes=[wek])
                    dd("lw", lw[:], [128, TB], F32, [lwk])
                    dd("icl", icl[:], [128, TB], F32, [ick])
                    dd("cw", cw[:], [128, TB], F32, [cwk])
                    dd("Winc", Winc[:], [128, TB], F32, [wik])
                    dd("Winv", Winv[:], [128, TB], F32, [wvk])
                    kk, kkk = f32p.get()
                    S.op("pool", lambda e, kk=kk, k_s=k_s, cv=cv: e.tensor_scalar(kk[:], k_s[:], cv(2), None, ALU.mult), reads=[kk_, "chvt"], writes=[kkk])
                    kk2, kk2k = b16p.get()
                    S.op("pool", lambda e, kk2=kk2, kk=kk: e.tensor_tensor(kk2[:], kk[:], kk[:], ALU.mult), reads=[kkk], writes=[kk2k])
                    pn, pnk = pg.get()
                    S.op("pe", lambda e, pn=pn, kk2=kk2: e.matmul(pn[:], bones_b[:], kk2[:], start=True, stop=True), reads=["bones_b", kk2k], writes=[pnk])
                    rn, rnk = f32p.get()
                    S.op("act", lambda e, rn=rn, pn=pn: e.activation(rn[:], pn[:], AF.Sqrt), reads=[pnk], writes=[rnk])
                    S.op("dve", lambda e, rn=rn: e.tensor_scalar(rn[:], rn[:], 1e-12, None, ALU.max), reads=[rnk], writes=[rnk])
                    S.op("dve", lambda e, rn=rn: e.reciprocal(rn[:], rn[:]), reads=[rnk], writes=[rnk])
                    S.op("dve", lambda e, kk=kk, rn=rn: e.tensor_tensor(kk[:], kk[:], rn[:], ALU.mult), reads=[kkk, rnk], writes=[kkk])
                    km, kmk = f32p.get()
                    S.op("dve", lambda e, km=km, icl=icl, cv=cv, p=p: e.tensor_scalar(km[:], icl[:], cv(3), omka[:, p:p + 1], ALU.mult, ALU.add), reads=[ick, "chvt", "omka"], writes=[kmk])
                    S.op("pool", lambda e, km=km, k_s=k_s: e.tensor_tensor(km[:], km[:], k_s[:], ALU.mult), reads=[kmk, kk_], writes=[kmk])
                    ar, ark = AR[p].get(); kb, kbk = KB[p].get()
                    v3 = lambda t: t[:].rearrange("p (c t) -> p c t", t=64)
                    S.op("dve", lambda e, ar=ar, r_s=r_s, Winc=Winc, v3=v3: e.tensor_tensor(ar[:, :, 64:128], v3(r_s), v3(Winc), ALU.mult), reads=[rk_, wik], writes=[(ark, "r")])
                    S.op("dve", lambda e, ar=ar, kk=kk, Wexc=Wexc, v3=v3: e.scalar_tensor_tensor(ar[:, :, 0:64], v3(kk), -1.0, v3(Wexc), ALU.mult, ALU.mult), reads=[kkk, wek], writes=[(ark, "a")])
                    S.op("pool", lambda e, kb=kb, km=km, Winv=Winv, v3=v3: e.tensor_tensor(kb[:, :, 0:64], v3(km), v3(Winv), ALU.mult), reads=[kmk, wvk], writes=[(kbk, "k")])
                    bt, btk = f32p.get()
                    S.op("pool", lambda e, bt=bt, kk=kk, icl=icl: e.tensor_tensor(bt[:], kk[:], icl[:], ALU.mult), reads=[kkk, ick], writes=[btk])
                    S.op("pool", lambda e, kb=kb, bt=bt, Winv=Winv, v3=v3: e.tensor_tensor(kb[:, :, 64:128], v3(bt), v3(Winv), ALU.mult), reads=[btk, wvk], writes=[(kbk, "b")])
                    ARK = [(ark, "r"), (ark, "a")]; KBK = [(kbk, "k"), (kbk, "b")]
                    dd("kk", kk[:], [128, TB], F32, [kkk])
                    dd("km", km[:], [128, TB], F32, [kmk])
                    dd("AR", ar[:], [128, 8, 128], BF16, ARK)
                    dd("KB", kb[:], [128, 8, 128], BF16, KBK)
                    vb, vbk = b16p.get()
                    S.op("act", lambda e, vb=vb, v_s=v_s: e.activation(vb[:], v_s[:], AF.Copy), reads=[vk_], writes=[vbk])
                    rkb, rkbk = b16p.get()
                    S.op("dve", lambda e, rkb=rkb, r_s=r_s, km=km, cv=cv: e.scalar_tensor_tensor(rkb[:], r_s[:], cv(4), km[:], ALU.mult, ALU.mult), reads=[rk_, kmk, "chvt"], writes=[rkbk])
                    pb, pbk = pg.get()
                    S.op("pe", lambda e, pb=pb, rkb=rkb: e.matmul(pb[:], bones_b[:], rkb[:], start=True, stop=True), reads=["bones_b", rkbk], writes=[pbk])
                    bon, bonk = f32p.get()
                    S.op("dve", lambda e, bon=bon, pb=pb, v_s=v_s: e.tensor_tensor(bon[:], pb[:], v_s[:], ALU.mult), reads=[pbk, vk_], writes=[bonk])

                    if stage == 12:
                        continue
                    yt, ytk = ytm[p].get()
                    def cg_pipe(cg):
                        c0 = cg * 4
                        def gstage(lsel, rsel_ar, width, maskt, maskk, pool_, lhs_from_kb=True):
                            pgt, pgk = pg.get()
                            pv = pgt[:].rearrange("p (c w) -> p c w", w=128)[:, :, 0:width] if width == 128 else pgt[:, 0:256].rearrange("p (c w) -> p c w", w=64)
                            for ch in range(4):
                                for hb in (0, 64):
                                    if lhs_from_kb:
                                        S.op("pe", lambda e, pv=pv, ch=ch, hb=hb, lsel=lsel: e.matmul(pv[hb:hb + 64, ch, :], kb[hb:hb + 64, c0 + ch, lsel], ar[hb:hb + 64, c0 + ch, :], start=True, stop=True),
                                             reads=ARK + KBK, writes=[(pgk, ch, hb)])
                                    else:
                                        S.op("pe", lambda e, pv=pv, ch=ch, hb=hb: e.matmul(pv[hb:hb + 64, ch, :], ar[hb:hb + 64, c0 + ch, 0:64], kb[hb:hb + 64, c0 + ch, 64:128], start=True, stop=True),
                                             reads=ARK + KBK, writes=[(pgk, ch, hb)])
                            gt, gk_ = pool_.get()
                            S.op("dve", lambda e, gt=gt, pv=pv, maskt=maskt: e.tensor_tensor(gt[:], pv, maskt[:], ALU.mult),
                                 reads=[(pgk, ch, hb) for ch in range(4) for hb in (0, 64)] + maskk, writes=[gk_])
                            return gt, gk_
                        Gk, gkk = gstage(slice(0, 64), None, 128, mask1x4, MK1, gkp)
                        Gb, gbk = gstage(slice(64, 128), None, 128, mask1x4, MK1, gbp)
                        GL, glk = gstage(None, None, 64, maskslx4, MKSL, glp, lhs_from_kb=False)
                        dd("Gk", Gk[:], [128, 4, 128], BF16, [gkk])
                        dd("Gb", Gb[:], [128, 4, 128], BF16, [gbk])
                        dd("GL", GL[:], [128, 4, 64], BF16, [glk])
                        yield
                        X, xk_ = xp.get()
                        S.op("pool", lambda e, X=X, Gb=Gb: e.tensor_tensor(X[:], Gb[:, :, 0:64], identhx4[:], ALU.add), reads=[gbk] + IDH, writes=[xk_])
                        Lk, Lkk = freeze(lambda ch, hb: GL[hb:hb + 64, ch, :]), glk
                        Mk, Mkk = freeze(lambda ch, hb: Gb[hb:hb + 64, ch, 0:64]), gbk
                        for lev in range(1, 6):
                            yield
                            plt, plk = pg.get()
                            plv = plt[:].rearrange("p (a c w) -> p a c w", a=2, w=64)
                            for ch in range(4):
                                for hb in (0, 64):
                                    S.op("pe", lambda e, plv=plv, ch=ch, hb=hb, Lk=Lk, Mk=Mk: e.matmul(plv[hb:hb + 64, 0, ch, :], Mk(ch, hb), Lk(ch, hb), start=True, stop=True),
                                         reads=[Lkk, Mkk], writes=[(plk, 0, ch, hb)])
                                    if lev < 5:
                                        S.op("pe", lambda e, plv=plv, ch=ch, hb=hb, Lk=Lk, Mk=Mk: e.matmul(plv[hb:hb + 64, 1, ch, :], Lk(ch, hb), Mk(ch, hb), start=True, stop=True),
                                             reads=[Lkk, Mkk], writes=[(plk, 1, ch, hb)])
                            lm, lmk = lmp.get()
                            na = 2 if lev < 5 else 1
                            S.op("act", lambda e, lm=lm, plv=plv, na=na: e.activation(lm[:, 0:na, :, :], plv[:, 0:na, :, :], AF.Copy),
                                 reads=[(plk, a, ch, hb) for a in range(na) for ch in range(4) for hb in (0, 64)], writes=[lmk])
                            Lk, Lkk = (lambda ch, hb, lm=lm: lm[hb:hb + 64, 0, ch, :]), lmk
                            Mk, Mkk = (lambda ch, hb, lm=lm: lm[hb:hb + 64, 1, ch, :]), lmk
                            pxt, pxk = pg.get()
                            pxv = pxt[:, 0:256].rearrange("p (c w) -> p c w", w=64)
                            for ch in range(4):
                                for hb in (0, 64):
                                    S.op("pe", lambda e, pxv=pxv, ch=ch, hb=hb, Lk=Lk, X=X: e.matmul(pxv[hb:hb + 64, ch, :], Lk(ch, hb), X[hb:hb + 64, ch, :], start=True, stop=True),
                                         reads=[Lkk, xk_], writes=[(pxk, ch, hb)])
                            Xn, xnk = xp.get()
                            S.op("dve", lambda e, Xn=Xn, pxv=pxv, X=X: e.tensor_tensor(Xn[:], pxv, X[:], ALU.add),
                                 reads=[(pxk, ch, hb) for ch in range(4) for hb in (0, 64)] + [xk_], writes=[xnk])
                            X, xk_ = Xn, xnk
                        dd("X", X[:], [128, 4, 64], BF16, [xk_])
                        yield
                        if stage == 13:
                            return
                        ptt, ptk = ptb.get()
                        ptv = ptt[:].rearrange("p (c k w) -> p c k w", k=4, w=64)
                        for ch in range(4):
                            c = c0 + ch
                            srcs = [(freeze(lambda hb, c=c: vb[hb:hb + 64, c * 64:(c + 1) * 64]), [vbk]),
                                    (freeze(lambda hb, c=c: ar[hb:hb + 64, c, 0:64]), ARK),
                                    (freeze(lambda hb, c=c: kb[hb:hb + 64, c, 0:64]), KBK),
                                    (freeze(lambda hb, c=c: kb[hb:hb + 64, c, 64:128]), KBK)]
                            for kind, (sf, sk_) in enumerate(srcs):
                                for hb in (0, 64):
                                    S.op("pe", lambda e, ptv=ptv, ch=ch, kind=kind, hb=hb, sf=sf: e.transpose(ptv[hb:hb + 64, ch, kind, :], sf(hb), identb[hb:hb + 64, hb:hb + 64]),
                                         reads=sk_ + ["identb"], writes=[(ptk, ch, kind, hb)])
                        TM, tmk = tmp_.get()
                        S.op("act", lambda e, TM=TM, ptv=ptv: e.activation(TM[:], ptv, AF.Copy),
                             reads=[(ptk, ch, kind, hb) for ch in range(4) for kind in range(4) for hb in (0, 64)], writes=[tmk])
                        yield
                        pzt, pzk2 = pg.get()
                        pzv = pzt[:, 0:256].rearrange("p (c w) -> p c w", w=64)
                        for ch in range(4):
                            for hb in (0, 64):
                                S.op("pe", lambda e, pzv=pzv, ch=ch, hb=hb, Gk=Gk, TM=TM: e.matmul(pzv[hb:hb + 64, ch, :], Gk[hb:hb + 64, ch, 0:64], TM[hb:hb + 64, ch, 0, :], start=True, stop=True),
                                     reads=[gkk, tmk], writes=[(pzk2, ch, hb)])
                        Zs, zsk = zsp.get()
                        S.op("pool", lambda e, Zs=Zs, TM=TM: e.tensor_copy(Zs[:, :, 0:64], TM[:, :, 1, :]), reads=[tmk], writes=[(zsk, 0)])
                        S.op("act", lambda e, Zs=Zs, pzv=pzv: e.activation(Zs[:, :, 64:128], pzv, AF.Copy),
                             reads=[(pzk2, ch, hb) for ch in range(4) for hb in (0, 64)], writes=[(zsk, 1)])
                        yield
                        pat, pauk = pg.get()
                        pav = pat[:].rearrange("p (c w) -> p c w", w=128)
                        for ch in range(4):
                            for hb in (0, 64):
                                S.op("pe", lambda e, pav=pav, ch=ch, hb=hb, X=X, Zs=Zs: e.matmul(pav[hb:hb + 64, ch, :], X[hb:hb + 64, ch, :], Zs[hb:hb + 64, ch, :], start=True, stop=True),
                                     reads=[xk_, (zsk, 0), (zsk, 1)], writes=[(pauk, ch, hb)])
                        AU, auk = aup.get()
                        S.op("dve", lambda e, AU=AU, pav=pav: e.tensor_copy(AU[:], pav), reads=[(pauk, ch, hb) for ch in range(4) for hb in (0, 64)], writes=[auk])
                        dd("TM", TM[:], [128, 4, 4, 64], BF16, [tmk])
                        dd("AU", AU[:], [128, 4, 128], BF16, [auk])
                        yield
                        pmt, pmk = pg.get()
                        pmv = pmt[:].rearrange("p (c w) -> p c w", w=128)
                        for ch in range(4):
                            for hb in (0, 64):
                                S.op("pe", lambda e, pmv=pmv, ch=ch, hb=hb, AU=AU, TM=TM: e.matmul(pmv[hb:hb + 64, ch, 0:64], AU[hb:hb + 64, ch, 0:64], TM[hb:hb + 64, ch, 3, :], start=True, stop=True),
                                     reads=[auk, tmk], writes=[(pmk, ch, hb, 0)])
                                S.op("pe", lambda e, pmv=pmv, ch=ch, hb=hb, AU=AU, TM=TM: e.matmul(pmv[hb:hb + 64, ch, 64:128], TM[hb:hb + 64, ch, 3, :], AU[hb:hb + 64, ch, 64:128], start=True, stop=False),
                                     reads=[auk, tmk], writes=[(pmk, ch, hb, 1)])
                                S.op("pe", lambda e, pmv=pmv, ch=ch, hb=hb, TM=TM: e.matmul(pmv[hb:hb + 64, ch, 64:128], TM[hb:hb + 64, ch, 2, :], TM[hb:hb + 64, ch, 0, :], start=False, stop=True),
                                     reads=[tmk], writes=[(pmk, ch, hb, 1)])
                        McT, mck = mctp.get()
                        S.op("dve", lambda e, McT=McT, pmv=pmv: e.tensor_tensor(McT[:], pmv[:, :, 0:64], identhx4[:], ALU.add),
                             reads=[(pmk, ch, hb, 0) for ch in range(4) for hb in (0, 64)] + IDH, writes=[mck])
                        NcW, nck = ncwp.get()
                        for ch in range(4):
                            wc = Winc[:, (c0 + ch) * 64 + 63:(c0 + ch) * 64 + 64]
                            S.op("dve", lambda e, NcW=NcW, pmv=pmv, ch=ch, wc=wc: e.tensor_scalar(NcW[:, ch, :], pmv[:, ch, 64:128], wc, None, ALU.mult),
                                 reads=[(pmk, ch, hb, 1) for hb in (0, 64)] + [wik], writes=[(nck, ch)])
                        yield
                        prt, prk = pg.get()
                        prv = prt[:, 0:256].rearrange("p (c w) -> p c w", w=64)
                        for ch in range(4):
                            for hb in (0, 64):
                                S.op("pe", lambda e, prv=prv, ch=ch, hb=hb, AU=AU, Gb=Gb: e.matmul(prv[hb:hb + 64, ch, :], AU[hb:hb + 64, ch, 0:64], Gb[hb:hb + 64, ch, 64:128], start=True, stop=True),
                                     reads=[auk, gbk], writes=[(prk, ch, hb)])
                        RhT, rhk = rhtp.get()
                        S.op("dve", lambda e, RhT=RhT, prv=prv: e.tensor_tensor(RhT[:], prv, ar[:, c0:c0 + 4, 64:128], ALU.add),
                             reads=[(prk, ch, hb) for ch in range(4) for hb in (0, 64)] + ARK, writes=[rhk])
                        dd("McT", McT[:], [128, 4, 64], BF16, [mck])
                        dd("NcW", NcW[:], [128, 4, 64], F32, [(nck, ch) for ch in range(4)])
                        dd("RhT", RhT[:], [128, 4, 64], BF16, [rhk])
                        yield
                        for ch in range(4):
                            c = c0 + ch
                            s0, s0k = sstate[p]
                            py, pyk = pys.get()
                            for hb in (0, 64):
                                S.op("pe", lambda e, py=py, hb=hb, ch=ch, Gb=Gb, AU=AU: e.matmul(py[hb:hb + 64, :], Gb[hb:hb + 64, ch, 64:128], AU[hb:hb + 64, ch, 64:128], start=True, stop=False),
                                     reads=[gbk, auk], writes=[(pyk, hb)])
                                S.op("pe", lambda e, py=py, hb=hb, ch=ch, Gk=Gk, TM=TM: e.matmul(py[hb:hb + 64, :], Gk[hb:hb + 64, ch, 64:128], TM[hb:hb + 64, ch, 0, :], start=False, stop=False),
                                     reads=[gkk, tmk], writes=[(pyk, hb)])
                                S.op("pe", lambda e, py=py, hb=hb, ch=ch, RhT=RhT, s0=s0: e.matmul(py[hb:hb + 64, :], RhT[hb:hb + 64, ch, :], s0[hb:hb + 64, :], start=False, stop=True),
                                     reads=[rhk, s0k], writes=[(pyk, hb)])
                            S.op("act", lambda e, py=py, yt=yt, c=c: e.activation(yt[:, c, :], py, AF.Copy), reads=[(pyk, 0), (pyk, 64)], writes=[(ytk, c)])
                            psn, psk = pys.get()
                            for hb in (0, 64):
                                S.op("pe", lambda e, psn=psn, hb=hb, ch=ch, McT=McT, s0=s0: e.matmul(psn[hb:hb + 64, :], McT[hb:hb + 64, ch, :], s0[hb:hb + 64, :], start=True, stop=True),
                                     reads=[mck, s0k], writes=[(psk, hb)])
                            s1_, s1k = Sb[p].get()
                            wc = Winc[:, c * 64 + 63:c * 64 + 64]
                            S.op("dve", lambda e, s1_=s1_, psn=psn, wc=wc, NcW=NcW, ch=ch: e.scalar_tensor_tensor(s1_[:], psn, wc, NcW[:, ch, :], ALU.mult, ALU.add),
                                 reads=[(psk, 0), (psk, 64), wik, (nck, ch)], writes=[s1k])
                            sstate[p] = (s1_, s1k)
                    pipes = [cg_pipe(0), cg_pipe(1)]
                    while pipes:
                        for g_ in list(pipes):
                            try:
                                next(g_)
                            except StopIteration:
                                pipes.remove(g_)
                    if stage == 13:
                        continue
                    YK = [(ytk, c) for c in range(8)]
                    dd("yt", yt[:], [128, 8, 64], F32, YK)
                    dd("Sb", sstate[p][0][:], [128, 64], BF16, [sstate[p][1]])
                    g0, g0k = gst.get(); g1, g1k = gst.get(); g2_, g2k = gst.get(); g3, g3k = gst.get()
                    S.op("dve", lambda e, g0=g0, yt=yt: e.tensor_reduce(g0[:], yt[:], AX.X, ALU.add), reads=YK, writes=[g0k])
                    S.op("pool", lambda e, yt=yt: e.tensor_tensor(ysq[:], yt[:], yt[:], ALU.mult), reads=YK, writes=["ysq"])
                    S.op("dve", lambda e, g1=g1: e.tensor_reduce(g1[:], ysq[:], AX.X, ALU.add), reads=["ysq"], writes=[g1k])
                    S.op("dve", lambda e, g0=g0: e.tensor_scalar(g0[:], g0[:], 1.0 / 64, None, ALU.mult), reads=[g0k], writes=[g0k])
                    S.op("dve", lambda e, g2_=g2_, g0=g0: e.tensor_tensor(g2_[:], g0[:], g0[:], ALU.mult), reads=[g0k], writes=[g2k])
                    S.op("dve", lambda e, g1=g1, g2_=g2_: e.scalar_tensor_tensor(g1[:], g1[:], 1.0 / 64, g2_[:], ALU.mult, ALU.subtract), reads=[g1k, g2k], writes=[g1k])
                    S.op("dve", lambda e, g1=g1: e.tensor_scalar(g1[:], g1[:], 64e-5, None, ALU.add), reads=[g1k], writes=[g1k])
                    S.op("act", lambda e, g3=g3, g1=g1: e.activation(g3[:], g1[:], AF.Sqrt), reads=[g1k], writes=[g3k])
                    S.op("dve", lambda e, g3=g3: e.reciprocal(g3[:], g3[:]), reads=[g3k], writes=[g3k])
                    yn, ynk = ynp.get()
                    for c in range(8):
                        S.op("dve" if c % 2 else "pool", lambda e, yn=yn, yt=yt, c=c, g0=g0, g3=g3: e.tensor_scalar(yn[:, c, :], yt[:, c, :], g0[:, c:c + 1], g3[:, c:c + 1], ALU.subtract, ALU.mult),
                             reads=[(ytk, c), g0k, g3k], writes=[(ynk, c)])
                    pt2, pt2k = ptb.get()
                    p2v = pt2[:, 0:512].rearrange("p (c w) -> p c w", w=64)
                    for c in range(8):
                        for hb in (0, 64):
                            S.op("pe", lambda e, p2v=p2v, c=c, hb=hb, yn=yn: e.transpose(p2v[hb:hb + 64, c, :], yn[hb:hb + 64, c, :], identb[hb:hb + 64, hb:hb + 64]),
                                 reads=[(ynk, c), "identb"], writes=[(pt2k, c, hb)])
                    o1, o1k = f32p.get()
                    S.op("act", lambda e, o1=o1, pt2=pt2, cv=cv: e.activation(o1[:], pt2[:, 0:512], AF.Identity, bias=cv(6), scale=cv(5)),
                         reads=[(pt2k, c, hb) for c in range(8) for hb in (0, 64)] + ["chvt"], writes=[o1k])
                    S.op("pool", lambda e, o1=o1, bon=bon: e.tensor_tensor(o1[:], o1[:], bon[:], ALU.add), reads=[o1k, bonk], writes=[o1k])
                    dd("yn", yn[:], [128, 8, 64], BF16, [(ynk, c) for c in range(8)])
                    dd("o1", o1[:], [128, TB], F32, [o1k])
                    dd("bon", bon[:], [128, TB], F32, [bonk])
                    pgg, pggk = pg.get()
                    S.op("pe", lambda e, pgg=pgg, p=p: e.matmul(pgg[:], g2_b[:, p * 128:(p + 1) * 128], sg[:], start=True, stop=True), reads=["g2_b", sgk], writes=[pggk])
                    S.op("dve", lambda e, p=p, pgg=pgg, o1=o1: e.tensor_tensor(orw[:, p, :], pgg[:], o1[:], ALU.mult), reads=[pggk, o1k], writes=[(ork, p)])
                if stage in (12, 13):
                    continue
                qd = tb // 4
                tq = (tb % 4) * TB
                S.dma("pool", exv[qd, 64:320, tq:tq + TB].rearrange("(p c) t -> c p t", p=2), orw[:], reads=[(ork, 0), (ork, 1)], writes=[("ex_in", tb)])
            S.barrier()


        if stage >= 30 and mode != "C":
          with ExitStack() as esB:
            sb, ps = mk_alloc(esB)
            exv = ex_in.ap().rearrange("(q r) t -> q r t", q=4)
            qkv = sb("qkv", [128, 5, SEQ], BF16)
            for t5 in range(5):
                for hf in range(4):
                    S.dma("sp", qkv[:, t5, hf * 2048:(hf + 1) * 2048], qkv_scr[:, t5, hf * 2048:(hf + 1) * 2048], writes=[("qkv", t5, hf)])
            QKV = [("qkv", t5, hf) for t5 in range(5) for hf in range(4)]
            am = sb("am", [128, 256]); am0 = sb("am0", [128, 256])
            S.dma("sp", am[:], cin["amask"], writes=["am"]); S.dma("sp", am0[:], cin["amask0"], writes=["am0"])
            ps_s = RPool("ps_s", [ps("ps_s%d" % i, [128, 256]) for i in range(2)])
            ps_pt = RPool("ps_pt", [ps("ps_pt", [128, 2, 128], BF16)])
            ps_v = RPool("ps_v", [ps("ps_v", [128, 64], BF16)])
            ps_o = RPool("ps_o", [ps("ps_o%d" % i, [128, 64]) for i in range(2)])
            ps_m = RPool("ps_m", [ps("ps_m", [64, 4, 128], BF16)])
            for i in range(2):
                S.banks[("ps_s", i)] = "ps_s%d" % i
                S.banks[("ps_o", i)] = "ps_o%d" % i
            S.banks[("ps_pt", 0)] = "ps_pt"; S.banks[("ps_v", 0)] = "ps_v"; S.banks[("ps_m", 0)] = "ps_m"
            smp = RPool("sm", [sb("sm%d" % i, [128, 256]) for i in range(3)])
            pbp = RPool("pb", [sb("pb%d" % i, [128, 256], BF16) for i in range(3)])
            ptp = RPool("ptp", [sb("ptp%d" % i, [128, 2, 128], BF16) for i in range(3)])
            vtp = RPool("vt", [sb("vt%d" % i, [128, 64], BF16) for i in range(4)])
            stp = RPool("st66", [sb("st66_%d" % i, [128, 66]) for i in range(4)])
            sts = RPool("sts", [sb("sts%d" % i, [128, 2]) for i in range(4)])
            heads = [(0, 0, 2, 0, 4, 0), (0, 64, 2, 64, 4, 64), (1, 0, 3, 0, 1, 64)]
            for g, dil in enumerate((1, 4, 16)):
                qt_, qb_, kt_, kb_, vt_, vb_ = heads[g]
                nb = SEQ // (128 * dil)
                for r in range(dil):
                  vst = [None]

                  def blk_pipe(n):
                        q0 = 128 * n * dil + r
                        k0 = 128 * (n - 1) * dil + r if n > 0 else q0
                        qs = qkv[qb_:qb_ + 64, qt_, q0:q0 + 127 * dil + 1:dil]
                        if n > 0:
                            ks = qkv[kb_:kb_ + 64, kt_, k0:k0 + 255 * dil + 1:dil]
                        pS, pSk = ps_s.get()
                        if n > 0:
                            S.op("pe", lambda e: e.matmul(pS[:], qs, ks, start=True, stop=True), reads=QKV, writes=[pSk])
                        else:
                            kc = qkv[kb_:kb_ + 64, kt_, q0:q0 + 127 * dil + 1:dil]
                            S.op("pe", lambda e: e.matmul(pS[:, 0:128], qs, kc, start=True, stop=True), reads=QKV, writes=[(pSk, 0)])
                            S.op("pe", lambda e: e.matmul(pS[:, 128:256], qs, kc, start=True, stop=True), reads=QKV, writes=[(pSk, 1)])
                        mk_ = am if n > 0 else am0
                        sm, smk = smp.get()
                        S.op("dve", lambda e: e.tensor_tensor(sm[:], pS[:], mk_[:], ALU.add), reads=[pSk, (pSk, 0), (pSk, 1), "am", "am0"], writes=[smk])
                        st, stk = stp.get()
                        ss_, ssk_ = sts.get()
                        S.op("dve", lambda e: e.tensor_reduce(st[:, 64:65], sm[:], AX.X, ALU.max), reads=[smk], writes=[(stk, "m")])
                        S.op("dve", lambda e: e.tensor_scalar(ss_[:, 0:1], st[:, 64:65], -0.125, None, ALU.mult), reads=[(stk, "m")], writes=[ssk_])
                        pb, pbk = pbp.get()
                        S.op("act", lambda e: e.activation(pb[:], sm[:], AF.Exp, bias=ss_[:, 0:1], scale=0.125, accum_out=st[:, 65:66]), reads=[smk, ssk_], writes=[pbk, (stk, "l")])
                        yield
                        ppt, pptk = ps_pt.get()
                        for hf in range(2):
                            S.op("pe", lambda e, hf=hf: e.transpose(ppt[:, hf, :], pb[:, hf * 128:(hf + 1) * 128], identb[:]), reads=[pbk, "identb"], writes=[(pptk, hf)])
                        pts, ptsk = ptp.get()
                        S.op("act", lambda e: e.activation(pts[:], ppt[:], AF.Copy), reads=[(pptk, 0), (pptk, 1)], writes=[ptsk])
                        pv_, pvk = ps_v.get()
                        vsl = qkv[vb_:vb_ + 64, vt_, q0:q0 + 127 * dil + 1:dil]
                        S.op("pe", lambda e: e.transpose(pv_[:], vsl, identb[vb_:vb_ + 64, vb_:vb_ + 64]), reads=QKV + ["identb"], writes=[pvk])
                        vcur, vck = vtp.get()
                        S.op("dve", lambda e: e.tensor_copy(vcur[:], pv_[:]), reads=[pvk], writes=[vck])
                        if vst[0] is None:
                            vst[0] = (vcur, vck)
                        vp, vpk = vst[0]
                        po, pok = ps_o.get()
                        S.op("pe", lambda e: e.matmul(po[:], pts[:, 0, :], vp[:], start=True, stop=False), reads=[ptsk, vpk], writes=[pok])
                        S.op("pe", lambda e: e.matmul(po[:], pts[:, 1, :], vcur[:], start=False, stop=True), reads=[ptsk, vck], writes=[pok])
                        S.op("dve", lambda e: e.tensor_copy(st[:, 0:64], po[:]), reads=[pok], writes=[(stk, "o")])
                        S.dma("sp", att_scr[g, q0:q0 + 127 * dil + 1:dil, :], st[:], reads=[(stk, "o"), (stk, "m"), (stk, "l")], writes=[("att_scr", g, r, n)])
                        vst[0] = (vcur, vck)

                  act_ = []
                  for n in range(nb):
                      g_ = blk_pipe(n)
                      next(g_)
                      act_.append(g_)
                      if len(act_) > 1:
                          for _ in act_.pop(0):
                              pass
                  for g_ in act_:
                      for _ in g_:
                          pass
            S.barrier()
            mgp = RPool("mg", [sb("mg%d" % i, [128, 3, 66]) for i in range(3)])
            mws = RPool("mw", [sb("mw%d" % i, [128, 8]) for i in range(4)])
            mo = RPool("mo", [sb("mo%d" % i, [128, 64]) for i in range(3)])
            mob = RPool("mob", [sb("mob%d" % i, [128, 64], BF16) for i in range(3)])
            oat = RPool("oat", [sb("oat%d" % i, [64, 4, 128], BF16) for i in range(2)])
            for tb in range(NBLK):
                pm_, pmk_ = ps_m.get()
                for tt in range(4):
                    T0 = tb * TB + tt * 128
                    mg, mgk = mgp.get()
                    S.dma("sp", mg[:], att_scr[:, T0:T0 + 128, :].rearrange("g t c -> t g c"), writes=[mgk])
                    w, wk = mws.get()
                    S.op("dve", lambda e: e.tensor_reduce(w[:, 3:4], mg[:, :, 64], AX.X, ALU.max), reads=[mgk], writes=[wk])
                    S.op("dve", lambda e: e.tensor_scalar(w[:, 3:4], w[:, 3:4], -0.125, None, ALU.mult), reads=[wk], writes=[wk])
                    S.op("act", lambda e: e.activation(w[:, 0:3], mg[:, :, 64], AF.Exp, bias=w[:, 3:4], scale=0.125), reads=[mgk, wk], writes=[wk])
                    S.op("dve", lambda e: e.tensor_tensor(w[:, 4:7], w[:, 0:3], mg[:, :, 65], ALU.mult), reads=[wk, mgk], writes=[wk])
                    S.op("dve", lambda e: e.tensor_reduce(w[:, 7:8], w[:, 4:7], AX.X, ALU.add), reads=[wk], writes=[wk])
                    S.op("dve", lambda e: e.reciprocal(w[:, 7:8], w[:, 7:8]), reads=[wk], writes=[wk])
                    o_, ok_ = mo.get()
                    S.op("dve", lambda e: e.tensor_scalar(o_[:], mg[:, 0, 0:64], w[:, 0:1], None, ALU.mult), reads=[mgk, wk], writes=[ok_])
                    S.op("dve", lambda e: e.scalar_tensor_tensor(o_[:], mg[:, 1, 0:64], w[:, 1:2], o_[:], ALU.mult, ALU.add), reads=[mgk, wk, ok_], writes=[ok_])
                    S.op("dve", lambda e: e.scalar_tensor_tensor(o_[:], mg[:, 2, 0:64], w[:, 2:3], o_[:], ALU.mult, ALU.add), reads=[mgk, wk, ok_], writes=[ok_])
                    ob, obk = mob.get()
                    S.op("dve", lambda e: e.tensor_scalar(ob[:], o_[:], w[:, 7:8], None, ALU.mult), reads=[ok_, wk], writes=[obk])
                    S.op("pe", lambda e, tt=tt: e.transpose(pm_[:, tt, :], ob[:], identb[:]), reads=[obk, "identb"], writes=[(pmk_, tt)])
                oa, oak = oat.get()
                S.op("act", lambda e: e.activation(oa[:], pm_[:], AF.Copy), reads=[(pmk_, tt) for tt in range(4)], writes=[oak])
                qd = tb // 4
                tq = (tb % 4) * TB
                S.dma("sp", exv[qd, 0:64, tq:tq + TB], oa[:].rearrange("p a b -> p (a b)"), reads=[oak], writes=[("ex_in_a", tb)])
            S.barrier()

        if mode == "A":
            keys = S.all_semkeys()
            sems = {k: es_top.enter_context(nc.semaphore(k.replace("_", ""))) for k in keys}
            with nc.Block() as block:
                S.emit(block, sems)
            return nc
        if stage >= 40 and mode == "all":
            for qq in range(4):
                for i, (ra, rb) in enumerate(((0, 64), (64, 192), (192, 320))):
                    S.custom("pool", lambda e: e.collective_compute("AllGather", ALU.bypass, replica_groups=[[0, 1, 2, 3], [4, 5, 6, 7]],
                                                                    ins=[ex_in.ap()[320 * qq + ra:320 * qq + rb, :]], outs=[exg[qq][i].ap()]),
                             "cc", 1, reads=[], writes=[("exg", qq, i)])
            S.barrier()

        if stage >= 50 and mode != "A":
          with ExitStack() as esC:
            sbC, psC = mk_alloc(esC)
            for (sa, sb_) in ((2, 3), (4, 5)):
                with ExitStack() as es0:
                    sb0, ps0 = mk_alloc(es0)
                    compute_mod(sa, sb0, ps0, "m%d" % sa)
                    compute_mod(sb_, sb0, ps0, "m%d" % sb_)
                    S.barrier()
            s1c2 = sbC("s1c2", [128, 8]); s2c = sbC("s2c", [128, 8])
            S.op("dve", lambda e: e.scalar_tensor_tensor(s1c2[:], modc[:, 8:16], 1.0, gcol[:, 0, :], ALU.add, ALU.mult), writes=["s1c2"])
            S.op("dve", lambda e: e.scalar_tensor_tensor(s2c[:], modc[:, 32:40], 1.0, gcol[:, 1, :], ALU.add, ALU.mult), writes=["s2c"])
            sh1 = modc[:, 0:8]; sh2 = modc[:, 24:32]
            G = [sbC("G1", [128, D]), sbC("G2", [128, D])]
            with ExitStack() as es0:
                sb0, ps0 = mk_alloc(es0)
                ones = sb0("ones", [128, 128])
                S.op("pool", lambda e: e.memset(ones[:], 1.0), writes=["ones"])
                grow = sb0("grow", [128, 2 * D])
                S.dma("sp", grow[:], grows.partition_broadcast(128), writes=["grow"])
                crp = RPool("cr", [sb0("cr%d" % i, [128, 128]) for i in range(2)])
                pbc = ps0("pbc", [128, D])
                S.banks[("pbc",)] = "pbc"
                for gi, sec in enumerate((2, 5)):
                    for c in range(8):
                        cr, crk = crp.get()
                        S.op("dve", lambda e: e.tensor_scalar(cr[:], ones[:], modc[:, sec * 8 + c:sec * 8 + c + 1], None, ALU.mult), reads=["ones"], writes=[crk])
                        S.op("pe", lambda e: e.matmul(pbc[:, c * 128:(c + 1) * 128], cr[:], K["ident"][:], start=True, stop=True), reads=[crk], writes=[(("pbc",), c)])
                    S.op("dve", lambda e: e.tensor_tensor(G[gi][:], pbc[:], grow[:, gi * D:(gi + 1) * D], ALU.mult),
                         reads=[(("pbc",), c) for c in range(8)] + ["grow"], writes=[("G", gi)])
                S.barrier()

            def load_cast(dst_fn, src_fn, n, width, stg, tag):
                for i in range(n):
                    t_, k_ = stg.get()
                    S.dma("sp", t_[:, 0:width], src_fn(i), writes=[k_])
                    if i % 2:
                        S.op("pool", lambda e: e.tensor_copy(dst_fn(i), t_[:, 0:width]), reads=[k_], writes=[(tag, i)])
                    else:
                        S.op("act", lambda e: e.activation(dst_fn(i), t_[:, 0:width], AF.Copy), reads=[k_], writes=[(tag, i)])

            with ExitStack() as es1:
                sb, ps = mk_alloc(es1)
                stg = RPool("stg", [sb("stg%d" % i, [128, 2048]) for i in range(2)])
                wG = sb("wG", [128, 8, 2048], BF16); wbr = sb("wbr", [128, 10, D], BF16); wo = sb("wo", [128, 8, D], BF16)
                load_cast(lambda i: wG[:, i, :], lambda i: w_in_G[i * 128:(i + 1) * 128, :], 8, 2048, stg, "wG")
                load_cast(lambda i: wbr[:, i, :], lambda i: w_branch[i * 128:(i + 1) * 128, :], 10, D, stg, "wbr")
                load_cast(lambda i: wo[:, i, :], lambda i: w_out[i * 128:(i + 1) * 128, :], 8, D, stg, "wo")
                S.barrier()
                xt_pool = RPool("cxt", [sb("cxt%d" % i, [128, D]) for i in range(2)])
                xn_pool = RPool("cxn", [sb("cxn%d" % i, [128, D], BF16) for i in range(4)])
                junk = sb("cjunk", [128, D], BF16)
                stat_pool = RPool("cstat", [sb("cstat%d" % i, [128, 4]) for i in range(8)])
                hT = sb("chT", [128, 8, TB], BF16)
                sg = sb("csg", [128, 16, TB], BF16)
                oall = sb("oall", [128, 10, TB], BF16)
                candp = RPool("cand", [sb("cand%d" % i, [128, 10, TB], BF16) for i in range(2)])
                qs_t = sb("qs_t", [128, 4])
                S.dma("sp", qs_t[:], qsel, writes=["qs_t"])
                mT = sb("mT", [128, 8, TB], BF16)
                tmpp = RPool("ctmp", [sb("ctmp%d" % i, [128, TB]) for i in range(3)])
                big = RPool("cbig", [sb("cbig%d" % i, [128, D]) for i in range(2)])
                x1p = RPool("cx1", [sb("cx1_%d" % i, [128, D]) for i in range(2)])
                xn2p = RPool("cxn2", [sb("cxn2_%d" % i, [128, D], BF16) for i in range(2)])
                h2sp = RPool("ch2s", [sb("ch2s%d" % i, [128, 8, 128], BF16) for i in range(2)])
                pacc = RPool("pacc", [ps("pacc%d" % i, [128, TB]) for i in range(3)])
                pmix = RPool("pmix", [ps("pmix", [128, D])])
                ptx = RPool("cptx", [ps("cptx", [128, 4, 128], BF16)])
                pt8 = RPool("cpt8", [ps("cpt8", [128, 8, 128], BF16)])
                for i in range(3):
                    S.banks[("pacc", i)] = "pacc%d" % i
                S.banks[("pmix", 0)] = "pmix"; S.banks[("cptx", 0)] = "cptx"; S.banks[("cpt8", 0)] = "cpt8"
                exo = ex_out.ap().rearrange("(r q c) t -> r q c t", r=4, q=4)

                def rstd_of(st, sk):
                    S.op("dve", lambda e: e.tensor_scalar(st[:, 1:2], st[:, 0:1], 1.0 / D, 1e-6, ALU.mult, ALU.add), reads=[sk], writes=[sk])
                    S.op("act", lambda e: e.activation(st[:, 2:3], st[:, 1:2], AF.Sqrt), reads=[sk], writes=[sk])
                    S.op("dve", lambda e: e.reciprocal(st[:, 3:4], st[:, 2:3]), reads=[sk], writes=[sk])

                for tb in range(4):
                    tok0 = tb * TB
                    xns = []
                    for tt in range(4):
                        xt, xk = xt_pool.get()
                        S.dma("sp", xt[:], x_q[tok0 + tt * 128: tok0 + (tt + 1) * 128, :], writes=[xk])
                        st, sk = stat_pool.get()
                        S.op("act", lambda e: e.activation(junk[:], xt[:], AF.Square, accum_out=st[:, 0:1]), reads=[xk], writes=["cjunk", sk])
                        rstd_of(st, sk)
                        xn, nk = xn_pool.get()
                        S.op("dve", lambda e: e.tensor_scalar(xn[:], xt[:], st[:, 3:4], None, ALU.mult), reads=[xk, sk], writes=[nk])
                        xns.append((xn, nk))
                    for cch in range(8):
                        pt, pk = ptx.get()
                        for tt in range(4):
                            xn, nk = xns[tt]
                            S.op("pe", lambda e: e.transpose(pt[:, tt, :], xn[:, cch * 128:(cch + 1) * 128], identb[:]), reads=[nk, "identb"], writes=[(pk, tt)])
                        S.op("act", lambda e: e.activation(hT[:, cch, :], pt[:].rearrange("p a b -> p (a b)"), AF.Identity, bias=sh1[:, cch:cch + 1], scale=s1c2[:, cch:cch + 1]),
                             reads=[(pk, tt) for tt in range(4)] + ["s1c2"], writes=[("chT", cch)])
                    HK = [("chT", cch) for cch in range(8)]
                    for ct in range(16):
                        pa, pak = pacc.get()
                        for cch in range(8):
                            S.op("pe", lambda e: e.matmul(pa[:], wG[:, cch, ct * 128:(ct + 1) * 128], hT[:, cch, :], start=(cch == 0), stop=(cch == 7)), reads=[("chT", cch)], writes=[pak])
                        S.op("act", lambda e: e.activation(sg[:, ct, :], pa[:], AF.Sigmoid), reads=[pak], writes=[("csg", ct)])
                    for qq in range(4):
                        cd, cdk = candp.get()
                        for r in range(4):
                            if mode == "C":
                                S.dma("sp", cd[64 * (r % 2):64 * (r % 2) + 64, r // 2, :], exo[r, qq, 0:64, tok0:tok0 + TB], writes=[(cdk, "a", r)])
                                S.dma("sp", cd[:, 2 + 2 * r:4 + 2 * r, :], exo[r, qq, 64:320, tok0:tok0 + TB].rearrange("(p c) t -> c p t", p=2), writes=[(cdk, "r", r)])
                            else:
                                S.dma("sp", cd[64 * (r % 2):64 * (r % 2) + 64, r // 2, :], exg[qq][0].ap()[64 * r:64 * r + 64, tok0:tok0 + TB], writes=[(cdk, "a", r)])
                                for p in range(2):
                                    S.dma("sp", cd[:, 2 + 2 * r + p, :], exg[qq][1 + p].ap()[128 * r:128 * r + 128, tok0:tok0 + TB], writes=[(cdk, "r", r, p)])
                        CK = [(cdk, "a", r) for r in range(4)] + [(cdk, "r", r) for r in range(4)] + [(cdk, "r", r, p) for r in range(4) for p in range(2)]
                        if qq == 0:
                            S.op("dve", lambda e: e.tensor_scalar(oall[:], cd[:], qs_t[:, 0:1], None, ALU.mult), reads=CK + ["qs_t"], writes=["oall"])
                        else:
                            S.op("dve", lambda e: e.scalar_tensor_tensor(oall[:].rearrange("p a t -> p (a t)"), cd[:].rearrange("p a t -> p (a t)"), qs_t[:, qq:qq + 1],
                                                                         oall[:].rearrange("p a t -> p (a t)"), ALU.mult, ALU.add), reads=CK + ["qs_t", "oall"], writes=["oall"])
                    for ct in range(8):
                        pA, pAk = pacc.get()
                        for a in range(2):
                            S.op("pe", lambda e: e.matmul(pA[:], wbr[:, a, ct * 128:(ct + 1) * 128], oall[:, a, :], start=(a == 0), stop=(a == 1)),
                                 reads=["oall"], writes=[pAk])
                        t1, t1k = tmpp.get()
                        S.op("dve", lambda e: e.tensor_tensor(t1[:], pA[:], sg[:, ct, :], ALU.mult), reads=[pAk, ("csg", ct)], writes=[t1k])
                        pR, pRk = pacc.get()
                        for k in range(8):
                            S.op("pe", lambda e: e.matmul(pR[:], wbr[:, 2 + k, ct * 128:(ct + 1) * 128], oall[:, 2 + k, :], start=(k == 0), stop=(k == 7)), reads=["oall"], writes=[pRk])
                        t2, t2k = tmpp.get()
                        S.op("dve", lambda e: e.tensor_tensor(t2[:], pR[:], sg[:, 8 + ct, :], ALU.mult), reads=[pRk, ("csg", 8 + ct)], writes=[t2k])
                        S.op("pool", lambda e: e.tensor_tensor(mT[:, ct, :], t1[:], t2[:], ALU.add), reads=[t1k, t2k], writes=[("mT", ct)])
                    for tt in range(4):
                        T0 = tok0 + tt * 128
                        pm, pmk = pmix.get()
                        for half in range(2):
                            for k in range(8):
                                S.op("pe", lambda e: e.matmul(pm[:, half * 512:(half + 1) * 512], mT[:, k, tt * 128:(tt + 1) * 128], wo[:, k, half * 512:(half + 1) * 512],
                                                              start=(k == 0), stop=(k == 7)), reads=[("mT", k)], writes=[(pmk, half)])
                        st, sk = stat_pool.get()
                        S.op("act", lambda e: e.activation(junk[:], pm[:], AF.Square, accum_out=st[:, 0:1]), reads=[(pmk, 0), (pmk, 1)], writes=["cjunk", sk])
                        rstd_of(st, sk)
                        xt, xk = xt_pool.get()
                        S.dma("sp", xt[:], x_q[T0:T0 + 128, :], writes=[xk])
                        bg, bgk = big.get()
                        S.op("dve", lambda e: e.scalar_tensor_tensor(bg[:], pm[:], st[:, 3:4], G[0][:], ALU.mult, ALU.mult), reads=[(pmk, 0), (pmk, 1), sk, ("G", 0)], writes=[bgk])
                        x1, x1k = x1p.get()
                        S.op("pool", lambda e: e.tensor_tensor(x1[:], bg[:], xt[:], ALU.add), reads=[bgk, xk], writes=[x1k])
                        S.dma("pool", x1_scr[T0:T0 + 128, :], x1[:], reads=[x1k], writes=[("x1_scr", T0)])
                        st2, sk2 = stat_pool.get()
                        S.op("act", lambda e: e.activation(junk[:], x1[:], AF.Square, accum_out=st2[:, 0:1]), reads=[x1k], writes=["cjunk", sk2])
                        rstd_of(st2, sk2)
                        xn2, xn2k = xn2p.get()
                        S.op("dve", lambda e: e.tensor_scalar(xn2[:], x1[:], st2[:, 3:4], None, ALU.mult), reads=[x1k, sk2], writes=[xn2k])
                        p8, p8k = pt8.get()
                        for cch in range(8):
                            S.op("pe", lambda e: e.transpose(p8[:, cch, :], xn2[:, cch * 128:(cch + 1) * 128], identb[:]), reads=[xn2k, "identb"], writes=[(p8k, cch)])
                        h2s, h2k = h2sp.get()
                        for cch in range(8):
                            S.op("act", lambda e: e.activation(h2s[:, cch, :], p8[:, cch, :], AF.Identity, bias=sh2[:, cch:cch + 1], scale=s2c[:, cch:cch + 1]),
                                 reads=[(p8k, cch), "s2c"], writes=[(h2k, cch)])
                        S.dma("pool", h2_scr[:, :, T0:T0 + 128], h2s[:], reads=[(h2k, cch) for cch in range(8)], writes=[("h2_scr", T0)])
                S.barrier()

            with ExitStack() as es2:
                sb, ps = mk_alloc(es2)
                w1 = sb("w1", [128, 8, 4 * D], BF16); w2 = sb("w2", [128, 32, D], BF16)
                with ExitStack() as esw:
                    sbw, psw = mk_alloc(esw)
                    stg = RPool("stg2", [sbw("stgb%d" % i, [128, 2048]) for i in range(3)])
                    load_cast(lambda i: w1[:, i // 2, (i % 2) * 2048:(i % 2 + 1) * 2048], lambda i: w_ff1[(i // 2) * 128:(i // 2 + 1) * 128, (i % 2) * 2048:(i % 2 + 1) * 2048], 16, 2048, stg, "w1")
                    load_cast(lambda i: w2[:, 2 * i:2 * i + 2, :].rearrange("p a n -> p (a n)"), lambda i: w_ff2[i * 256:(i + 1) * 256, :].rearrange("(a p) n -> p a n", p=128), 16, 2048, stg, "w2")
                    S.barrier()
                BL = 256
                h2p = RPool("h2b", [sb("h2b%d" % i, [128, 8, BL], BF16) for i in range(2)])
                sqp = RPool("sq", [sb("sq%d" % i, [128, 32, BL], BF16) for i in range(1)])
                rp = RPool("rl", [sb("rl%d" % i, [128, BL]) for i in range(3)])
                x1p = RPool("dx1", [sb("dx1_%d" % i, [128, D]) for i in range(2)])
                big = RPool("dbig", [sb("dbig%d" % i, [128, D]) for i in range(2)])
                outp = RPool("dout", [sb("dout%d" % i, [128, D]) for i in range(2)])
                junk = sb("djunk", [128, D], BF16)
                stat_pool = RPool("dstat", [sb("dstat%d" % i, [128, 4]) for i in range(4)])
                pf = RPool("pf", [ps("pf%d" % i, [128, BL]) for i in range(3)])
                pmix = RPool("pmx", [ps("pmx%d" % i, [128, D]) for i in range(2)])
                for i in range(3):
                    S.banks[("pf", i)] = "pf%d" % i
                for i in range(2):
                    S.banks[("pmx", i)] = "pmx%d" % i
                OUTK = []
                for blk in range(2048 // BL):
                    b0 = blk * BL
                    h2b, h2bk = h2p.get()
                    S.dma("sp", h2b[:], h2_scr[:, :, b0:b0 + BL], writes=[h2bk])
                    sq, sqk = sqp.get()
                    for ft in range(32):
                        pp_, ppk = pf.get()
                        for k in range(8):
                            S.op("pe", lambda e: e.matmul(pp_[:], w1[:, k, ft * 128:(ft + 1) * 128], h2b[:, k, :], start=(k == 0), stop=(k == 7)), reads=[h2bk], writes=[ppk])
                        rl, rlk = rp.get()
                        S.op("act", lambda e: e.activation(rl[:], pp_[:], AF.Relu), reads=[ppk], writes=[rlk])
                        S.op("pool" if ft % 2 else "dve", lambda e: e.tensor_tensor(sq[:, ft, :], rl[:], rl[:], ALU.mult), reads=[rlk], writes=[(sqk, ft)])
                    for tt in range(BL // 128):
                        T0 = b0 + tt * 128
                        pm, pmk = pmix.get()
                        for half in range(2):
                            for k in range(32):
                                S.op("pe", lambda e: e.matmul(pm[:, half * 512:(half + 1) * 512], sq[:, k, tt * 128:(tt + 1) * 128], w2[:, k, half * 512:(half + 1) * 512],
                                                              start=(k == 0), stop=(k == 31)), reads=[(sqk, k)], writes=[(pmk, half)])
                        st, sk = stat_pool.get()
                        S.op("act", lambda e: e.activation(junk[:], pm[:], AF.Square, accum_out=st[:, 0:1]), reads=[(pmk, 0), (pmk, 1)], writes=["djunk", sk])
                        S.op("dve", lambda e: e.tensor_scalar(st[:, 1:2], st[:, 0:1], 1.0 / D, 1e-6, ALU.mult, ALU.add), reads=[sk], writes=[sk])
                        S.op("act", lambda e: e.activation(st[:, 2:3], st[:, 1:2], AF.Sqrt), reads=[sk], writes=[sk])
                        S.op("dve", lambda e: e.reciprocal(st[:, 3:4], st[:, 2:3]), reads=[sk], writes=[sk])
                        x1, x1k = x1p.get()
                        S.dma("sp", x1[:], x1_scr[T0:T0 + 128, :], writes=[x1k])
                        bg, bgk = big.get()
                        S.op("dve", lambda e: e.scalar_tensor_tensor(bg[:], pm[:], st[:, 3:4], G[1][:], ALU.mult, ALU.mult), reads=[(pmk, 0), (pmk, 1), sk], writes=[bgk])
                        ot, otk = outp.get()
                        S.op("pool", lambda e: e.tensor_tensor(ot[:], bg[:], x1[:], ALU.add), reads=[bgk, x1k], writes=[otk])
                        S.dma("pool", out_d[T0:T0 + 128, :], ot[:], reads=[otk], writes=[("out", T0)])
                        OUTK.append(("out", T0))
                S.final_wait("sp", OUTK)
            keys = S.all_semkeys()
            sems = {k: es_top.enter_context(nc.semaphore(k.replace("_", ""))) for k in keys}
            with nc.Block() as block:
                S.emit(block, sems)
            return nc
        if stage < 99:
            with ExitStack() as esd:
                sbd, psd = mk_alloc(esd)
                db = sbd("dbgbuf", [128, 2048], BF16)
                for qd in range(4):
                    for r0 in (0, 128, 256):
                        n = min(128, 320 - r0)
                        S.dma("sp", db[0:n, :], (ex_out if stage >= 40 else ex_in).ap().rearrange("(q r) t -> q r t", q=4)[qd, r0:r0 + n, :], writes=["db"])
                        S.dma("sp", dbg[qd, r0:r0 + n, :], db[0:n, :], reads=["db"], writes=[("dbg", qd, r0)])
                S.final_wait("sp", [("dbg", qd, r0) for qd in range(4) for r0 in (0, 128, 256)] + [("dbgout", n) for n in DBG])
            keys = S.all_semkeys()
            sems = {k: es_top.enter_context(nc.semaphore(k.replace("_", ""))) for k in keys}
            with nc.Block() as block:
                S.emit(block, sems)
            return nc
    return nc


def core_inputs(inputs, b, j):
    f = lambda a: np.ascontiguousarray(a)
    x = inputs["x"]; L = 0
    m = {}
    m["x_b"] = f(x[b])
    m["x_q"] = f(x[b, 2048 * j:2048 * (j + 1)])
    m["pos_b"] = f(inputs["positions"][b][None, :].astype(np.int32))
    m["c_col"] = f(inputs["c"][b].reshape(8, 128).T)
    m["ada_w"] = f(inputs["ada_w"][L])
    m["ada_bc"] = f(inputs["ada_b"][L].reshape(48, 128).T)
    col = lambda v: v.reshape(8, 128).T
    m["gcols"] = f(np.stack([col(inputs["norm_mix_pre"][L]), col(inputs["norm_ffn_pre"][L])], 1))
    m["grows"] = f(np.concatenate([inputs["norm_mix_post"][L], inputs["norm_ffn_post"][L]])[None, :])
    w_in = inputs["w_in"][L]
    ha = [4 * g + j for g in range(3)]
    qc = lambda h: np.arange(64 * h, 64 * h + 64)
    kc = lambda h: 768 + qc(h)
    vc = lambda h: 1536 + qc(h)
    R0 = 2304
    cols = [qc(ha[0]), qc(ha[1]), qc(ha[2]), vc(ha[2]), kc(ha[0]), kc(ha[1]), kc(ha[2]), kc(ha[2]), vc(ha[0]), vc(ha[1])]
    rw = []
    for sec in range(3):
        rw.append(R0 + sec * 1024 + 256 * j + np.arange(256))
    rw.append(R0 + 3072 + np.arange(256))
    cols = np.concatenate(cols + rw)
    assert cols.shape[0] == NTA * 128
    m["w_in_A"] = f(w_in[:, cols])
    m["w_in_G"] = f(w_in[:, R0 + 3328:R0 + 3328 + 2048])
    mu = inputs["shift_mu"][L][cols[640:] - R0]
    m["mu_A"] = f(mu.reshape(8, 128).T)
    my = 256 * j + np.arange(256)
    m["w2a"] = f(np.concatenate([inputs["decay_w2"][L][:, my], inputs["iclr_a2"][L][:, my]], 0))
    m["g2"] = f(inputs["gate_g2"][L][:, my])
    vecs = [inputs["decay_w0"][L], inputs["iclr_a0"][L], inputs["k_k"][L], inputs["k_a"][L], inputs["r_k"][L].reshape(-1),
            inputs["gn_w"][L], inputs["gn_b"][L], inputs["gn_b"][L]]
    chv = np.stack([v[my].reshape(2, 128) for v in vecs], -1)
    m["chv"] = f(chv.transpose(1, 0, 2))
    qs = np.zeros((128, 4), np.float32); qs[:, j] = 1.0
    m["qsel"] = qs
    m["w_branch"] = f(inputs["w_branch"][L])
    m["w_out"] = f(inputs["w_out"][L])
    m["w_ff1"] = f(inputs["w_ff1"][L])
    m["w_ff2"] = f(inputs["w_ff2"][L])
    for k, v in host_consts().items():
        m["k_" + k] = f(v)
    return m


def kernel(**inputs):
    inputs = {k: np.asarray(v) for k, v in inputs.items()}
    in_maps = [core_inputs(inputs, c // 4, c % 4) for c in range(8)]
    nc = build(99, "all")
    res = run_bass_kernel_spmd(nc, in_maps, core_ids=list(range(8)))
    out = np.zeros((2, SEQ, D), np.float32)
    for c in range(8):
        out[c // 4, 2048 * (c % 4):2048 * (c % 4 + 1)] = res.results[c]["out"]
    return out
```
